# Optimizing a Trainium2 kernel written in Bass

```python
import math
import jax, jax.numpy as jnp
from jax import lax
import numpy as np

D_MODEL = 1024
BATCH = 8
SEQ = 2048
DEPTH = 1

HEAD_DIM = 64
ATTN_PAIRS = ((128, 1), (512, 4), (2048, 16))
HEADS_PER_GROUP = 4
N_ATTN_HEADS = HEADS_PER_GROUP * len(ATTN_PAIRS)
ATTN_WIDTH = N_ATTN_HEADS * HEAD_DIM
ATTN_OUT_WIDTH = HEADS_PER_GROUP * HEAD_DIM
GLA_HEADS = 4
GLA_DK = 64
GLA_DV = 128
GLA_K_WIDTH = GLA_HEADS * GLA_DK
GLA_V_WIDTH = GLA_HEADS * GLA_DV
GLA_RANK = 16
GLA_TAU = 16.0
GLA_CHUNK = 64
REL_BUCKETS = 32
REL_MAX_DISTANCE = 1024
D_FF = 4 * D_MODEL
ALPHA = (2 * DEPTH) ** 0.25
BETA = (8 * DEPTH) ** -0.25
LN_EPS = 1e-5
NORM_EPS = 1e-6
NEG_INF = -1e30

IN_SIZES = (ATTN_WIDTH, ATTN_WIDTH, ATTN_WIDTH,
            GLA_K_WIDTH, GLA_K_WIDTH, GLA_V_WIDTH, GLA_V_WIDTH,
            GLA_RANK, GLA_RANK, D_MODEL, D_MODEL)
IN_COLS = int(sum(IN_SIZES))
IN_SPLITS = tuple(int(c) for c in np.cumsum(IN_SIZES)[:-1])

kernel_name = "hybrid_dilated_attn_gla_sqrelu_deepnorm"


def _layernorm(x, g, b):
    xf = x.astype(jnp.float32)
    mu = jnp.mean(xf, axis=-1, keepdims=True)
    var = jnp.mean(jnp.square(xf - mu), axis=-1, keepdims=True)
    y = (xf - mu) * lax.rsqrt(var + LN_EPS) * g.astype(jnp.float32) + b.astype(jnp.float32)
    return y.astype(x.dtype)


def _t5_bucket(rel):
    nb = REL_BUCKETS // 2
    max_exact = nb // 2
    ret = (rel > 0).astype(np.int32) * nb
    n = np.abs(rel)
    large = max_exact + (np.log(np.maximum(n, 1) / max_exact)
                         / np.log(REL_MAX_DISTANCE / max_exact) * (nb - max_exact)).astype(np.int32)
    large = np.minimum(large, nb - 1)
    return (ret + np.where(n < max_exact, n, large)).astype(np.int32)


def _group_bias(rel_bias, group, dilation, blk):
    rel_sub = np.arange(3 * blk)[None, :] - blk - np.arange(blk)[:, None]
    bucket = _t5_bucket(rel_sub * dilation)
    heads = slice(group * HEADS_PER_GROUP, (group + 1) * HEADS_PER_GROUP)
    return rel_bias.astype(jnp.float32)[bucket][:, :, heads].transpose(2, 0, 1)


def _dilated_group(q, k, v, bias, r, k_side):
    B, S, Hg, dh = q.shape
    blk = k_side
    span = r * blk
    P = -(-S // span) * span
    L = P // r
    nb = L // blk
    pad = ((0, 0), (0, P - S), (0, 0), (0, 0))

    def to_blocks(t):
        t = jnp.pad(t, pad).reshape(B, L, r, Hg, dh).transpose(0, 2, 1, 3, 4)
        return t.reshape(B, r, nb, blk, Hg, dh)

    def window(t):
        tp = jnp.pad(t, ((0, 0), (0, 0), (1, 1), (0, 0), (0, 0), (0, 0)))
        return jnp.concatenate([tp[:, :, :-2], tp[:, :, 1:-1], tp[:, :, 2:]], axis=3)

    qb = to_blocks(q)
    kw = window(to_blocks(k))
    vw = window(to_blocks(v)).astype(jnp.float32)

    pos = np.arange(P).reshape(L, r).T.reshape(r, nb, blk)
    posp = np.pad(pos, ((0, 0), (1, 1), (0, 0)), constant_values=-1)
    kpos = np.concatenate([posp[:, :-2], posp[:, 1:-1], posp[:, 2:]], axis=2)
    kvalid = (kpos >= 0) & (kpos < S)
    rel = np.arange(3 * blk)[None, :] - blk - np.arange(blk)[:, None]
    band = np.abs(rel) <= k_side
    mask = band[None, None] & kvalid[:, :, None, :]

    s = jnp.einsum('brnqhd,brnkhd->brnhqk', qb, kw).astype(jnp.float32) * (dh ** -0.5) + bias
    s = jnp.where(mask[:, :, None], s, NEG_INF)
    m = jnp.max(s, axis=-1, keepdims=True)
    e = jnp.exp(s - m)
    den = jnp.sum(e, axis=-1, keepdims=True)
    o = jnp.einsum('brnhqk,brnkhd->brnhqd', e, vw) / den
    lse = (m + jnp.log(den))[..., 0]

    o = o.transpose(0, 1, 2, 4, 3, 5).reshape(B, r, L, Hg, dh).transpose(0, 2, 1, 3, 4)
    o = o.reshape(B, P, Hg, dh)[:, :S]
    lse = lse.transpose(0, 1, 2, 4, 3).reshape(B, r, L, Hg).transpose(0, 2, 1, 3)
    lse = lse.reshape(B, P, Hg)[:, :S]
    return o, lse


def _gla_direction(q, k, v, log_a, strict):
    B, S, H, dk = q.shape
    dv = v.shape[-1]
    C = GLA_CHUNK
    NC = S // C

    def chunks(t):
        return t.reshape(B, NC, C, H, t.shape[-1]).transpose(0, 3, 1, 2, 4)

    q, k, v, log_a = chunks(q), chunks(k), chunks(v), chunks(log_a)
    b = jnp.cumsum(log_a, axis=3)
    q_in = q * jnp.exp(b)
    k_in = k * jnp.exp(-b)
    a = jnp.einsum('bhncd,bhnsd->bhncs', q_in, k_in)
    tri = np.tril(np.ones((C, C), dtype=bool), k=-1 if strict else 0)
    a = jnp.where(tri, a, 0.0)
    o = jnp.einsum('bhncs,bhnsv->bhncv', a, v)

    b_last = b[:, :, :, -1:]
    chunk_kv = jnp.einsum('bhncd,bhncv->nbhdv', k * jnp.exp(b_last - b), v)
    decay = jnp.exp(b_last[:, :, :, 0]).transpose(2, 0, 1, 3)

    def step(state, inp):
        dec, kv = inp
        return dec[..., None] * state + kv, state

    _, s_prev = lax.scan(step, jnp.zeros((B, H, dk, dv), jnp.float32), (decay, chunk_kv))
    o = o + jnp.einsum('bhncd,nbhdv->bhncv', q_in, s_prev)
    return o.transpose(0, 2, 3, 1, 4).reshape(B, S, H, dv)


def setup_inputs(seed: int = 0) -> dict:
    key = jax.random.key(seed)
    ks = jax.random.split(key, 20)
    f32 = jnp.float32
    col_scale = np.ones((IN_COLS,), np.float32)
    off = np.concatenate([[0], np.cumsum(IN_SIZES)])
    col_scale[off[2]:off[3]] = BETA
    col_scale[off[5]:off[6]] = BETA
    nrm = lambda k, shape, scale: jax.random.normal(k, shape, f32) * scale
    return {
        "x": jax.random.normal(ks[0], (BATCH, SEQ, D_MODEL), f32),
        "w_in": nrm(ks[1], (DEPTH, D_MODEL, IN_COLS), D_MODEL ** -0.5) * jnp.asarray(col_scale),
        "rel_bias": nrm(ks[2], (REL_BUCKETS, N_ATTN_HEADS), 0.5),
        "w_lr_fwd": nrm(ks[3], (DEPTH, GLA_RANK, GLA_K_WIDTH), GLA_RANK ** -0.5),
        "b_lr_fwd": nrm(ks[4], (DEPTH, GLA_K_WIDTH), 0.1),
        "w_lr_bwd": nrm(ks[5], (DEPTH, GLA_RANK, GLA_K_WIDTH), GLA_RANK ** -0.5),
        "b_lr_bwd": nrm(ks[6], (DEPTH, GLA_K_WIDTH), 0.1),
        "gla_norm_g": 1.0 + nrm(ks[7], (DEPTH, GLA_V_WIDTH), 0.02),
        "w_attn_branch": nrm(ks[8], (DEPTH, ATTN_OUT_WIDTH, D_MODEL), ATTN_OUT_WIDTH ** -0.5),
        "w_gla_branch": nrm(ks[9], (DEPTH, GLA_V_WIDTH, D_MODEL), GLA_V_WIDTH ** -0.5),
        "w_out": nrm(ks[10], (DEPTH, D_MODEL, D_MODEL), BETA * D_MODEL ** -0.5),
        "ln1_g": 1.0 + nrm(ks[11], (DEPTH, D_MODEL), 0.02),
        "ln1_b": nrm(ks[12], (DEPTH, D_MODEL), 0.02),
        "w_ff1": nrm(ks[13], (DEPTH, D_MODEL, D_FF), BETA * D_MODEL ** -0.5),
        "b_ff1": nrm(ks[14], (DEPTH, D_FF), 0.02),
        "w_ff2": nrm(ks[15], (DEPTH, D_FF, D_MODEL), BETA * D_FF ** -0.5),
        "b_ff2": nrm(ks[16], (DEPTH, D_MODEL), 0.02),
        "ln2_g": 1.0 + nrm(ks[17], (DEPTH, D_MODEL), 0.02),
        "ln2_b": nrm(ks[18], (DEPTH, D_MODEL), 0.02),
    }


def reference(x, w_in, rel_bias, w_lr_fwd, b_lr_fwd, w_lr_bwd, b_lr_bwd, gla_norm_g,
              w_attn_branch, w_gla_branch, w_out, ln1_g, ln1_b, w_ff1, b_ff1, w_ff2, b_ff2,
              ln2_g, ln2_b):
    B, S, _ = x.shape
    f32 = jnp.float32
    for l in range(DEPTH):
        proj = x @ w_in[l]
        (aq, ak, av, gq, gk, gv, gout, lr_f, lr_b, gate_a, gate_g) = jnp.split(proj, IN_SPLITS, axis=-1)

        aq = aq.reshape(B, S, N_ATTN_HEADS, HEAD_DIM)
        ak = ak.reshape(B, S, N_ATTN_HEADS, HEAD_DIM)
        av = av.reshape(B, S, N_ATTN_HEADS, HEAD_DIM)
        outs, lses = [], []
        for gi, (win, dil) in enumerate(ATTN_PAIRS):
            k_side = (win // 2) // dil
            hs = slice(gi * HEADS_PER_GROUP, (gi + 1) * HEADS_PER_GROUP)
            bias = _group_bias(rel_bias, gi, dil, k_side)
            o_g, lse_g = _dilated_group(aq[:, :, hs], ak[:, :, hs], av[:, :, hs], bias, dil, k_side)
            outs.append(o_g)
            lses.append(lse_g)
        mix_w = jax.nn.softmax(jnp.stack(lses, axis=0), axis=0)
        y_attn = jnp.sum(mix_w[..., None] * jnp.stack(outs, axis=0), axis=0)
        y_attn = y_attn.reshape(B, S, ATTN_OUT_WIDTH).astype(x.dtype)

        gq = gq.astype(f32).reshape(B, S, GLA_HEADS, GLA_DK) * (GLA_DK ** -0.5)
        gk = gk.astype(f32).reshape(B, S, GLA_HEADS, GLA_DK)
        gv = gv.astype(f32).reshape(B, S, GLA_HEADS, GLA_DV)
        la_f = (jax.nn.log_sigmoid((lr_f @ w_lr_fwd[l] + b_lr_fwd[l]).astype(f32)) / GLA_TAU
                ).reshape(B, S, GLA_HEADS, GLA_DK)
        la_b = (jax.nn.log_sigmoid((lr_b @ w_lr_bwd[l] + b_lr_bwd[l]).astype(f32)) / GLA_TAU
                ).reshape(B, S, GLA_HEADS, GLA_DK)
        o_fwd = _gla_direction(gq, gk, gv, la_f, False)
        o_bwd = jnp.flip(_gla_direction(jnp.flip(gq, 1), jnp.flip(gk, 1), jnp.flip(gv, 1),
                                        jnp.flip(la_b, 1), True), 1)
        o_gla = o_fwd + o_bwd
        o_gla = o_gla * lax.rsqrt(jnp.mean(jnp.square(o_gla), axis=-1, keepdims=True) + NORM_EPS)
        y_gla = (o_gla.reshape(B, S, GLA_V_WIDTH) * gla_norm_g[l].astype(f32)
                 * jax.nn.silu(gout.astype(f32))).astype(x.dtype)

        merged = (jax.nn.sigmoid(gate_a) * (y_attn @ w_attn_branch[l])
                  + jax.nn.sigmoid(gate_g) * (y_gla @ w_gla_branch[l]))
        x = _layernorm(ALPHA * x + merged @ w_out[l], ln1_g[l], ln1_b[l])

        ff = jnp.square(jax.nn.relu(x @ w_ff1[l] + b_ff1[l])) @ w_ff2[l] + b_ff2[l]
        x = _layernorm(ALPHA * x + ff, ln2_g[l], ln2_b[l])
    return x
```

```python
import contextlib
import os
import numpy as np
import concourse.bass as bass
import concourse.mybir as mybir
from concourse.bass_utils import run_bass_kernel_spmd

F32 = mybir.dt.float32
BF16 = mybir.dt.bfloat16
AF = mybir.ActivationFunctionType
ALU = mybir.AluOpType

S = 2048
D = 1024
DFF = 4096
NCOL = 5920
ALPHA = 2.0 ** 0.25
LN_EPS = 1e-5
NORM_EPS = 1e-6
NSLOT = 3
SAME_ENGINE_WAR_SYNC = True
KB = 1024

ENGS = ("pe", "act", "dve", "pool", "sp")


class Prog:
    def __init__(self, nc):
        self.nc = nc
        self.ops = []
        self.lastw = {}
        self.readers = {}

    def op(self, eng, fn, reads=(), writes=(), signal=True, chan=None):
        oid = len(self.ops)
        deps = set()
        for r in reads:
            w = self.lastw.get(r)
            if w is not None:
                deps.add((w, "raw"))
        for r in writes:
            w = self.lastw.get(r)
            if w is not None:
                deps.add((w, "waw"))
            for rd in self.readers.get(r, ()):
                deps.add((rd, "war"))
        for r in reads:
            self.readers.setdefault(r, []).append(oid)
        for r in writes:
            self.lastw[r] = oid
            self.readers[r] = []
        self.ops.append(dict(id=oid, eng=eng, fn=fn, deps=deps, signal=signal, chan=chan))
        return oid

    def alias(self, new, olds):
        rs = list(self.readers.get(new, []))
        w = self.lastw.get(new)
        if w is not None:
            rs.append(w)
        for o in olds:
            rs.extend(self.readers.get(o, []))
            w = self.lastw.get(o)
            if w is not None:
                rs.append(w)
        self.readers[new] = rs
        self.lastw.pop(new, None)

    def dma(self, eng, out, in_, reads=(), writes=(), chan=None):
        assert chan is not None
        return self.op(eng, lambda e: e.dma_start(out=out, in_=in_), reads, writes, True, chan)

    def finalize(self):
        ops = self.ops
        cnt = {}
        last_of = {}
        for o in ops:
            if o["chan"] is None:
                last_of[o["eng"]] = o["id"]
        for e, i in last_of.items():
            ops[i]["signal"] = True
        import bisect
        sig_ids = {}
        for o in ops:
            if o["chan"] is None and o["signal"]:
                sig_ids.setdefault(o["eng"], []).append(o["id"])
        for o in ops:
            for (d, kind) in o["deps"]:
                do = ops[d]
                if do["chan"] is not None or do["signal"]:
                    continue
                lst = sig_ids.setdefault(do["eng"], [])
                p = bisect.bisect_left(lst, d)
                if p >= len(lst) or lst[p] >= o["id"]:
                    do["signal"] = True
                    bisect.insort(lst, d)
        pending = {}
        for o in ops:
            if o["chan"] is not None:
                k = "ch_" + o["chan"]
                cnt[k] = cnt.get(k, 0) + 16
                o["tok"] = (k, cnt[k])
            else:
                k = "e_" + o["eng"]
                pending.setdefault(k, []).append(o)
                if o["signal"]:
                    cnt[k] = cnt.get(k, 0) + 1
                    for p in pending[k]:
                        p["tok"] = (k, cnt[k])
                    pending[k] = []
        self.semkeys = sorted(cnt.keys())
        self.cnt = cnt
        chan_hist = {}
        for o in ops:
            if o["chan"] is not None:
                chan_hist.setdefault(o["tok"][0], []).append((o["id"], o["tok"][1]))
        known = {e: {} for e in ENGS}
        snap = {}
        per_eng = {e: [] for e in ENGS}
        for o in ops:
            e = o["eng"]
            kn = known[e]
            need = {}
            for (d, kind) in o["deps"]:
                do = ops[d]
                same = (do["eng"] == e) and do["chan"] is None and o["chan"] is None
                if same and kind != "raw":
                    if e != "pe" and kn.get(do["tok"][0], 0) < do["tok"][1]:
                        self.n_same_war = getattr(self, "n_same_war", 0) + 1
                        if SAME_ENGINE_WAR_SYNC:
                            k, v = do["tok"]
                            if v > need.get(k, (0, None))[0]:
                                need[k] = (v, d)
                    continue
                k, v = do["tok"]
                if v > need.get(k, (0, None))[0]:
                    need[k] = (v, d)
            waits = {}
            for k, (v, d) in need.items():
                if k.startswith("ch_"):
                    lst = chan_hist[k]
                    p = bisect.bisect_left(lst, (o["id"], 0)) - 1
                    v = max(v, lst[p][1])
                if kn.get(k, 0) >= v:
                    continue
                waits[k] = v
                kn[k] = v
                for kk, vv in snap[d].items():
                    if kn.get(kk, 0) < vv:
                        kn[kk] = vv
            o["waits"] = waits
            snap[o["id"]] = dict(kn)
            per_eng[e].append(o)
        return per_eng

    def emit(self):
        nc = self.nc
        per_eng = self.finalize()
        with contextlib.ExitStack() as st:
            sems = {k: st.enter_context(nc.semaphore(k)) for k in self.semkeys}
            block = st.enter_context(nc.Block())

            def run(ename):
                def body(eng):
                    for o in per_eng[ename]:
                        for k, v in o["waits"].items():
                            eng.wait_ge(sems[k], v)
                        ins = o["fn"](eng)
                        if o["chan"] is not None:
                            ins.then_inc(sems[o["tok"][0]], 16)
                        elif o["signal"]:
                            ins.then_inc(sems[o["tok"][0]], 1)
                    if ename == "sp":
                        for k in self.semkeys:
                            if k.startswith("ch_"):
                                eng.wait_ge(sems[k], self.cnt[k])
                return body

            block.sync(run("sp"))
            block.gpsimd(run("pool"))
            block.scalar(run("act"))
            block.vector(run("dve"))
            block.tensor(run("pe"))


def _t5_bucket(rel):
    nb = 16
    max_exact = 8
    ret = (rel > 0).astype(np.int32) * nb
    n = np.abs(rel)
    large = max_exact + (np.log(np.maximum(n, 1) / max_exact) / np.log(1024 / max_exact) * (nb - max_exact)).astype(np.int32)
    large = np.minimum(large, nb - 1)
    return (ret + np.where(n < max_exact, n, large)).astype(np.int32)


def _host_tables():
    j = np.arange(256)[:, None]
    i = np.arange(128)[None, :]
    buckets, masks = [], []
    for g, dil in enumerate((1, 4, 16)):
        delta = (j - 64 - i) if g < 2 else (j - i)
        m = (np.abs(delta) <= 64)
        if g == 2:
            m = m & (j < 128)
        buckets.append(_t5_bucket(delta * dil))
        masks.append(m.astype(np.float32))
    return buckets, masks


def _consts():
    s = np.arange(128)[:, None]
    c = np.arange(128)[None, :]
    v = -1.0 / 16.0
    c32 = np.stack([
        np.where(s <= c, v, 0.0), np.where(s >= c, v, 0.0),
        np.where(s > c, v, 0.0), np.where(s < c, v, 0.0)], 0).astype(np.float32)
    c32 = np.ascontiguousarray(c32.transpose(1, 0, 2))
    maskF = (s <= c).astype(np.float32)
    maskB = (s > c).astype(np.float32)
    ident = np.eye(128, dtype=np.float32)
    ones = np.ones((128, 128), np.float32)
    onesF = np.ones((128, 64), np.float32); onesF[0:64] = 0
    onesL = np.ones((128, 64), np.float32); onesL[64:128] = 0
    c16 = np.concatenate([maskF, maskB, ident, ones, onesF, onesL], 1)
    return c32, np.ascontiguousarray(c16)


def build_nc(debug=False):
    STOP = os.environ.get("MK_STOP", "")
    nc = bass.Bass("TRN2", target_bir_lowering=False)

    def din(name, shape):
        return nc.dram_tensor(name, list(shape), F32, kind="ExternalInput").ap()

    xT_d = din("xT", [D, S])
    x_d = din("x", [S, D])
    win_d = din("w_in", [D, NCOL])
    wlr_d = din("wlr", [33, 512])
    c32_d = din("c32", [128, 4, 128])
    c16_d = din("c16", [128, 640])
    bias_d = din("biasT", [128, 12, 2, 128])
    mask_d = din("maskT", [128, 3, 2, 128])
    wab_d = din("w_ab", [256, D])
    wgb_d = din("w_gb", [512, D])
    wout_d = din("w_out", [D, D])
    w1_d = din("w_ff1", [D, DFF])
    b1_d = din("b1T", [128, 32])
    w2_d = din("w_ff2", [DFF, D])
    gT_d = din("gT", [128, 4])
    vec_d = din("vecs", [128, 5, D])
    out_d = nc.dram_tensor("out", [S, D], F32, kind="ExternalOutput").ap()
    dbg_d = {}
    if debug:
        for nm, shp in (("d_yattn", [128, 2, S]), ("d_ygla", [128, 4, S]), ("d_merged", [128, 8, S])):
            dbg_d[nm] = nc.dram_tensor(nm, shp, F32, kind="ExternalOutput").ap()

    with contextlib.ExitStack() as st:
        def sb(name, shape, dt):
            return st.enter_context(nc.sbuf_tensor("sb_" + name, list(shape), dt))

        ARENA_KB = 172
        arena = sb("arena", [128, ARENA_KB * KB // 2], BF16)

        def av(off_kb, shape, dt):
            n = int(np.prod(shape))
            nb = n * (4 if dt == F32 else 2)
            o = int(off_kb * KB)
            assert o + nb <= ARENA_KB * KB, (off_kb, shape)
            ap = arena[:, o // 2:(o + nb) // 2]
            if dt == F32:
                ap = ap.bitcast(F32)
            if len(shape) == 2:
                ap = ap.rearrange("p (a b) -> p a b", a=shape[0])
            elif len(shape) == 3:
                ap = ap.rearrange("p (a b c) -> p a b c", a=shape[0], b=shape[1])
            elif len(shape) == 4:
                ap = ap.rearrange("p (a b c d) -> p a b c d", a=shape[0], b=shape[1], c=shape[2])
            return ap

        wst = [sb("wst%d" % i, [128, 8, 512], BF16) for i in range(NSLOT)]
        c32 = sb("c32", [128, 4, 128], F32)
        c16 = sb("c16", [128, 640], BF16)
        expb = sb("expb", [128, 12, 2, 128], BF16)
        wlr = sb("wlr", [33, 512], BF16)
        b1T = sb("b1T", [128, 32], F32)
        gT = sb("gT", [128, 4], F32)
        dec = sb("dec", [128, 2, 2, 16], F32)
        stats = sb("stats", [128, 4, 2, 6], F32)
        mv = sb("mv", [128, 4, 4], F32)
        epst = sb("epst", [128, 2], F32)
        psb = [st.enter_context(nc.psum_tensor("ps%d" % i, [128, 512], F32)) for i in range(8)]

        maskF = c16[:, 0:128]
        maskB = c16[:, 128:256]
        ident = c16[:, 256:384]
        ones = c16[:, 384:512]
        onesF = c16[:, 512:576]
        onesL = c16[:, 576:640]

        P = Prog(nc)
        bank_ctr = [0]

        def nb():
            b = bank_ctr[0] % 8
            bank_ctr[0] += 1
            return b

        def PS(b):
            return "ps%d" % b

        def mm(out, lhsT, rhs, start, stop, reads, writes, signal=None):
            P.op("pe", lambda e: e.matmul(out, lhsT, rhs, start=start, stop=stop), reads, writes,
                 signal=(stop if signal is None else signal))

        def act(out, in_, func, reads, writes, bias=None, scale=None):
            kw = {}
            if bias is not None:
                kw["bias"] = bias
            if scale is not None:
                kw["scale"] = scale
            P.op("act", lambda e: e.activation(out, in_, func, **kw), reads, writes)

        def tt(eng, out, in0, in1, op, reads, writes):
            P.op(eng, lambda e: e.tensor_tensor(out, in0, in1, op), reads, writes)

        def ts(eng, out, in0, s1, s2, op0, op1, reads, writes):
            if op1 is None:
                P.op(eng, lambda e: e.tensor_scalar(out, in0, s1, None, op0), reads, writes)
            else:
                P.op(eng, lambda e: e.tensor_scalar(out, in0, s1, s2, op0, op1), reads, writes)

        def stt(eng, out, in0, scalar, in1, op0, op1, reads, writes):
            P.op(eng, lambda e: e.scalar_tensor_tensor(out, in0, scalar, in1, op0, op1), reads, writes)

        def cp(eng, out, in_, reads, writes):
            if eng == "act":
                P.op("act", lambda e: e.activation(out, in_, AF.Copy), reads, writes)
            else:
                P.op(eng, lambda e: e.tensor_copy(out, in_), reads, writes)

        def memset(eng, ap, val, writes):
            P.op(eng, lambda e: e.memset(ap, val), (), writes)

        panel_seq = []
        ptr = {"use": 0, "issue": 0}

        def win_panel(c0, n):
            return win_d[:, c0:c0 + n].rearrange("(k p) c -> p k c", p=128)

        def issue_panels(hold=0):
            while ptr["issue"] < min(len(panel_seq), ptr["use"] + NSLOT - hold):
                s = ptr["issue"] % NSLOT
                for (off, n, src) in panel_seq[ptr["issue"]]:
                    P.dma("pool", wst[s][:, :, off:off + n], src, writes=["W%d" % s], chan="W%d" % s)
                ptr["issue"] += 1

        def next_panel(hold=0):
            issue_panels(hold)
            s = ptr["use"] % NSLOT
            ptr["use"] += 1
            return s

        panel_seq.append([(0, 32, win_panel(3840, 32))])
        panel_seq.append([(0, 512, win_panel(2304, 512))])
        panel_seq.append([(0, 512, win_panel(3328, 512))])
        panel_seq.append([(0, 512, win_panel(2816, 512))])
        for g in range(3):
            panel_seq.append([(0, 256, win_panel(g * 256, 256)), (256, 256, win_panel(768 + g * 256, 256))])
            panel_seq.append([(0, 256, win_panel(1536 + g * 256, 256))])
        for u in range(2):
            panel_seq.append([(0, 512, win_panel(3872 + u * 512, 512))])
            panel_seq.append([(0, 512, win_panel(4896 + u * 512, 512))])
        for tg in range(4):
            for u in range(8):
                panel_seq.append([(0, 512, w1_d[:, u * 512:(u + 1) * 512].rearrange("(k p) c -> p k c", p=128))])
            for half in range(2):
                for rep in range(2 if (tg == 3 and half == 1) else 1):
                    for u in range(4):
                        panel_seq.append([(0, 512, w2_d[u * 1024:(u + 1) * 1024, half * 512:(half + 1) * 512]
                                           .rearrange("(f p) c -> p f c", p=128))])

        P.dma("sp", c32[:, :, :], c32_d, writes=["c32"], chan="c32")
        P.dma("sp", b1T[:, :], b1_d, writes=["b1T"], chan="small")
        P.dma("sp", gT[:, :], gT_d, writes=["gT"], chan="gT")
        memset("dve", epst[:, 0:1], LN_EPS, ["epst"])
        memset("dve", epst[:, 1:2], NORM_EPS, ["epst"])

        xT = av(0, [8, 2560], BF16)
        memset("dve", xT[:, :, 0:256], 0.0, ["xTpadL"])
        memset("dve", xT[:, :, 2304:2560], 0.0, ["xTpadR"])
        issue_panels(hold=1)
        xT_src = xT_d.rearrange("(k p) t -> p k t", p=128)
        for k in range(8):
            P.dma("pool", xT[:, k, 256:2304], xT_src[:, k, :], writes=["xT%d" % k], chan="xT%d" % k)
        XT = ["xT%d" % k for k in range(8)] + ["xTpadL", "xTpadR"]

        def XTk(k):
            return ["xT%d" % k, "xTpadL", "xTpadR"]
        P.dma("pool", wlr[:, :], wlr_d, writes=["wlr"], chan="wlr")
        P.dma("pool", c16[:, :], c16_d, writes=["c16"], chan="c16")

        if STOP == "INIT":
            P.emit()
            return nc
        y_glaT = av(40, [4, S], BF16)
        gqT = av(64, [2, S], BF16)
        gkT = av(72, [2, S], BF16)
        gk_tm = av(80, [16, 256], BF16)
        qfT = av(88, [2, S], BF16)
        kfT = av(96, [2, S], BF16)
        qbT = av(104, [2, S], BF16)
        kbT = av(112, [2, S], BF16)
        kdec = av(120, [16, 2, 256], BF16)
        gv = av(136, [16, 512], BF16)
        goutT = av(152, [4, S], BF16)
        lrT = av(56, [S], BF16)

        def proj_fm(slot, col, ncols, evac):
            banks = [nb() for _ in range(4)]
            for k in range(8):
                for tg in range(4):
                    mm(psb[banks[tg]][0:ncols, :], wst[slot][:, k, col:col + ncols],
                       xT[:, k, 256 + tg * 512:256 + (tg + 1) * 512], k == 0, k == 7,
                       XTk(k) + ["W%d" % slot], [PS(banks[tg])])
            for tg in range(4):
                evac(tg, banks[tg])

        def proj_tm(slot, col, ncols, tok_ap_fn, evac):
            b = nb()
            for k in range(8):
                mm(psb[b][:, 0:ncols], tok_ap_fn(k), wst[slot][:, k, col:col + ncols], k == 0, k == 7,
                   XTk(k) + ["W%d" % slot], [PS(b)])
            evac(b)

        s_lr = next_panel()
        memset("dve", lrT[32:33, :], 1.0, ["lrT1"])

        def ev_lr(tg, b):
            cp("act", lrT[0:32, tg * 512:(tg + 1) * 512], psb[b][0:32, :], [PS(b)], ["lrT"])
        proj_fm(s_lr, 0, 32, ev_lr)

        s_qk = next_panel()
        for c in range(4):
            dstT = gqT if c < 2 else gkT
            nm = "gqT" if c < 2 else "gkT"
            sc = 0.125 if c < 2 else 1.0

            def ev(tg, b, dstT=dstT, nm=nm, sc=sc, c=c):
                if (tg + c) % 2 == 0:
                    act(dstT[:, c % 2, tg * 512:(tg + 1) * 512], psb[b][:, :], AF.Copy, [PS(b)], [nm], scale=sc)
                else:
                    ts("dve", dstT[:, c % 2, tg * 512:(tg + 1) * 512], psb[b][:, :], sc, None, ALU.mult, None, [PS(b)], [nm])
            proj_fm(s_qk, c * 128, 128, ev)
        for t in range(16):
            def ev(b, t=t):
                cp("act" if t % 2 else "dve", gk_tm[:, t, :], psb[b][:, 0:256], [PS(b)], ["gk_tm"])
            proj_tm(s_qk, 256, 256, lambda k, t=t: xT[:, k, 256 + t * 128:256 + (t + 1) * 128], ev)

        s_go = next_panel()
        gtmp = av(56 + 4, [512], F32)
        for c in range(4):
            def ev(tg, b, c=c):
                act(gtmp, psb[b][:, :], AF.Silu, [PS(b)], ["gtmp"])
                ts("dve", goutT[:, c, tg * 512:(tg + 1) * 512], gtmp, gT[:, c:c + 1], None, ALU.mult, None,
                   ["gtmp", "gT"], ["goutT"])
            proj_fm(s_go, c * 128, 128, ev)

        s_gv = next_panel()
        for t in range(16):
            def ev(b, t=t):
                cp("dve" if t % 2 else "act", gv[:, t, :], psb[b][:, :], [PS(b)], ["gv"])
            proj_tm(s_gv, 0, 512, lambda k, t=t: xT[:, k, 256 + t * 128:256 + (t + 1) * 128], ev)

        if STOP == "G1":
            P.emit()
            return nc
        sp_t2 = av(40, [2, 512], F32)
        e_pos2 = av(44, [2, 4, 128], F32)
        e_neg2 = av(48, [2, 4, 128], F32)
        e_k2 = av(52, [2, 2, 256], F32)
        def g2_bufs(t):
            g2 = t % 2
            return g2, sp_t2[:, g2, :], e_pos2[:, g2, :, :], e_neg2[:, g2, :, :], e_k2[:, g2, :, :]

        def g2_front(t):
            tok = slice(t * 128, (t + 1) * 128)
            g2, sp_t, e_pos, e_neg, e_k = g2_bufs(t)
            bz = nb()
            mm(psb[bz][:, :], lrT[0:33, tok], wlr[0:33, :], True, True, ["lrT", "lrT1", "wlr"], [PS(bz)])
            act(sp_t, psb[bz][:, :], AF.Exp, [PS(bz)], ["sp_t%d" % g2], scale=-1.0)
            act(sp_t, sp_t, AF.Ln, ["sp_t%d" % g2], ["sp_t%d" % g2], bias=1.0)

        def g2_back(t):
            tok = slice(t * 128, (t + 1) * 128)
            g2, sp_t, e_pos, e_neg, e_k = g2_bufs(t)
            bc_ = nb()
            for q in range(4):
                mm(psb[bc_][:, q * 128:(q + 1) * 128], sp_t[:, q * 128:(q + 1) * 128], c32[:, 0 if q < 2 else 1, :],
                   True, True, ["sp_t%d" % g2, "c32"], [PS(bc_)], signal=(q == 3))
            bk = nb()
            mm(psb[bk][:, 0:256], c32[:, 2, :], sp_t[:, 0:256], True, True, ["sp_t%d" % g2, "c32"], [PS(bk)], signal=False)
            mm(psb[bk][:, 256:512], c32[:, 3, :], sp_t[:, 256:512], True, True, ["sp_t%d" % g2, "c32"], [PS(bk)])
            pc4 = psb[bc_][:, :].rearrange("p (a b) -> p a b", a=4)
            act(e_pos, pc4, AF.Exp, [PS(bc_)], ["e_pos%d" % g2])
            act(e_neg, pc4, AF.Exp, [PS(bc_)], ["e_neg%d" % g2], scale=-1.0)
            act(e_k, psb[bk][:, :].rearrange("p (a b) -> p a b", a=2), AF.Exp, [PS(bk)], ["e_k%d" % g2])
            tt("dve", qfT[:, :, tok], gqT[:, :, tok], e_pos[:, 0:2, :], ALU.mult, ["gqT", "e_pos%d" % g2], ["qfT"])
            tt("dve", kfT[:, :, tok], gkT[:, :, tok], e_neg[:, 0:2, :], ALU.mult, ["gkT", "e_neg%d" % g2], ["kfT"])
            tt("dve", qbT[:, :, tok], gqT[:, :, tok], e_pos[:, 2:4, :], ALU.mult, ["gqT", "e_pos%d" % g2], ["qbT"])
            tt("dve", kbT[:, :, tok], gkT[:, :, tok], e_neg[:, 2:4, :], ALU.mult, ["gkT", "e_neg%d" % g2], ["kbT"])
            cp("dve", dec[:, 0, :, t], e_pos[:, 0:2, 127], ["e_pos%d" % g2], ["dec"])
            cp("dve", dec[:, 1, :, t], e_pos[:, 2:4, 0], ["e_pos%d" % g2], ["dec"])
            tt("dve", kdec[:, t, :, :], e_k, gk_tm[:, t:t + 1, :].to_broadcast([128, 2, 256]), ALU.mult,
               ["e_k%d" % g2, "gk_tm"], ["kdec"])

        g2_front(0)
        for t in range(16):
            if t + 1 < 16:
                g2_front(t + 1)
            g2_back(t)

        if STOP == "G2":
            P.emit()
            return nc
        S16 = av(64, [16, 2, 2, 128], BF16)
        S32 = av(80, [2, 2, 2, 128], F32)
        S16N = ["S16_%d_%d" % (t, d) for t in range(16) for d in range(2)]
        for nm in S16N:
            P.alias(nm, ["gqT", "gkT", "gk_tm"])
        memset("dve", S32[:, 0, 0, :, :], 0.0, ["S32_0_0"])
        memset("dve", S32[:, 0, 1, :, :], 0.0, ["S32_0_1"])
        am = av(84, [2, 8, 128], BF16)
        P.alias("am0", ["gk_tm"])
        P.alias("am1", ["gk_tm"])
        G2T = ["%s%d" % (n, i) for n in ("sp_t", "e_pos", "e_neg", "e_k") for i in range(2)]
        P.alias("y_glaT", G2T)

        def scan_step(i):
            tf = i
            tb = 15 - i
            b = nb()
            kvp = psb[b][:, :].rearrange("p (d j v) -> p d j v", d=2, j=2)
            for d, tsel in ((0, tf), (1, tb)):
                for j in range(2):
                    for hh in range(2):
                        h = 2 * j + hh
                        mm(kvp[hh * 64:(hh + 1) * 64, d, j, :], kdec[:, tsel, d, j * 128 + hh * 64:j * 128 + hh * 64 + 64],
                           gv[:, tsel, h * 128:(h + 1) * 128], True, True, ["kdec", "gv"], [PS(b)],
                           signal=(d == 1 and j == 1 and hh == 1))
            src, dst = i % 2, (i + 1) % 2
            for d, tsel, tout in ((0, tf, tf + 1), (1, tb, tb - 1)):
                for j in range(2):
                    stt("dve", S32[:, dst, d, j, :], S32[:, src, d, j, :], dec[:, d, j, tsel:tsel + 1], kvp[:, d, j, :],
                        ALU.mult, ALU.add, ["S32_%d_%d" % (src, d), "dec", PS(b)], ["S32_%d_%d" % (dst, d)])
                cp("act", S16[:, tout, d, :, :], S32[:, dst, d, :, :], ["S32_%d_%d" % (dst, d)], ["S16_%d_%d" % (tout, d)])

        sq3 = [av(56, [4, 128], BF16), av(57, [4, 128], BF16), av(58, [4, 128], BF16)]
        osb2 = [av(59, [4, 128], F32), av(61, [4, 128], F32)]
        for nm in ("sq0", "sq1", "sq2", "o_sb0", "o_sb1"):
            P.alias(nm, ["lrT", "lrT1", "gtmp"])
        g4bo = {}

        def g4_A(n, t):
            tok = slice(t * 128, (t + 1) * 128)
            ab = n % 2
            bA = [nb(), nb()]
            for hh in range(2):
                rows = slice(hh * 64, (hh + 1) * 64)
                for d, (kk, qq, knm, qnm) in enumerate(((kfT, qfT, "kfT", "qfT"), (kbT, qbT, "kbT", "qbT"))):
                    for j in range(2):
                        mm(psb[bA[hh]][:, (d * 2 + j) * 128:(d * 2 + j + 1) * 128], kk[rows, j, tok], qq[rows, j, tok],
                           True, True, [knm, qnm], [PS(bA[hh])], signal=(d == 1 and j == 1))
            for hh in range(2):
                tt("dve", am[:, ab, hh * 4:(hh + 1) * 4, :].rearrange("p (d j) c -> p d j c", d=2),
                   psb[bA[hh]][:, :].rearrange("p (d j c) -> p d j c", d=2, j=2),
                   c16[:, 0:256].rearrange("p (d c) -> p d c", d=2).unsqueeze(2).to_broadcast([128, 2, 2, 128]),
                   ALU.mult, [PS(bA[hh]), "c16"], ["am%d" % ab])

        def g4_o(n, t):
            tok = slice(t * 128, (t + 1) * 128)
            ab = n % 2
            bo = nb()
            g4bo[n] = bo
            for h in range(4):
                j, hh = h // 2, h % 2
                rows = slice(hh * 64, (hh + 1) * 64)
                outp = psb[bo][:, h * 128:(h + 1) * 128]
                parts = [(gv[:, t, h * 128:(h + 1) * 128], am[:, ab, hh * 4 + j, :], ["gv", "am%d" % ab]),
                         (gv[:, t, h * 128:(h + 1) * 128], am[:, ab, hh * 4 + 2 + j, :], ["gv", "am%d" % ab])]
                if t > 0:
                    parts.append((S16[rows, t, 0, j, :], qfT[rows, j, tok], ["S16_%d_0" % t, "qfT"]))
                if t < 15:
                    parts.append((S16[rows, t, 1, j, :], qbT[rows, j, tok], ["S16_%d_1" % t, "qbT"]))
                for pi, (l, r, rd) in enumerate(parts):
                    mm(outp, l, r, pi == 0, pi == len(parts) - 1, rd, [PS(bo)],
                       signal=(h == 3 and pi == len(parts) - 1))
            po4 = psb[bo][:, :].rearrange("p (h c) -> p h c", h=4)
            act(sq3[n % 3], po4, AF.Square, [PS(bo)], ["sq%d" % (n % 3)])
            cp("act", osb2[n % 2], po4, [PS(bo)], ["o_sb%d" % (n % 2)])

        def g4_fin(n, t):
            tok = slice(t * 128, (t + 1) * 128)
            sq, o_sb = sq3[n % 3], osb2[n % 2]
            bs = nb()
            mm(psb[bs][:, :], ones, sq.rearrange("p h c -> p (h c)"), True, True, ["sq%d" % (n % 3), "c16"], [PS(bs)])
            ps4 = psb[bs][:, :].rearrange("p (h c) -> p h c", h=4)
            act(ps4, ps4, AF.Ln, [PS(bs), "epst"], [PS(bs)], bias=epst[:, 1:2], scale=1.0 / 128.0)
            act(ps4, ps4, AF.Exp, [PS(bs)], [PS(bs)], scale=-0.5)
            tt("dve", o_sb, ps4, o_sb, ALU.mult, ["o_sb%d" % (n % 2), PS(bs)], ["o_sb%d" % (n % 2)])
            tt("dve", y_glaT[:, :, tok], o_sb, goutT[:, :, tok], ALU.mult, ["o_sb%d" % (n % 2), "goutT"], ["y_glaT"])

        for i in range(7):
            scan_step(i)
        T = []
        for i in range(7, 15):
            T += [14 - i, i + 1]
        for n in range(16 + 2):
            if n < 16 and n % 2 == 0:
                scan_step(7 + n // 2)
            if n < 16:
                g4_A(n, T[n])
            if 0 <= n - 1 < 16:
                g4_o(n - 1, T[n - 1])
            if 0 <= n - 2 < 16:
                g4_fin(n - 2, T[n - 2])

        if STOP == "G4":
            P.emit()
            return nc
        S16N_ = ["S16_%d_%d" % (t, d) for t in range(16) for d in range(2)]
        bias_st = av(64, [12, 2, 128], F32)
        mask_st = av(76, [3, 2, 128], F32)
        P.alias("bias_st", S16N_)
        P.alias("mask_st", S16N_)
        P.dma("sp", bias_st, bias_d, writes=["bias_st"], chan="bias")
        P.dma("sp", mask_st, mask_d, writes=["mask_st"], chan="mask")
        act(bias_st, bias_st, AF.Exp, ["bias_st"], ["bias_st"])
        for g in range(3):
            tt("dve", expb[:, 4 * g:4 * g + 4, :, :], bias_st[:, 4 * g:4 * g + 4, :, :],
               mask_st[:, g:g + 1, :, :].to_broadcast([128, 4, 2, 128]), ALU.mult,
               ["bias_st", "mask_st"], ["expb"])
        acc = av(64, [2, 2, S], F32)
        qT2 = [av(96, [2, S], BF16), av(140, [2, S], BF16)]
        kT2 = [av(104, [2, 2560], BF16), av(148, [2, 2560], BF16)]
        Vt2 = [av(114, [20, 256], BF16), av(158, [20, 256], BF16)]
        et_raw = av(124, [4, 8, 128], BF16)
        et = av(132, [4, 8, 128], BF16)
        ETN = ["et%d" % i for i in range(4)] + ["et_raw%d" % i for i in range(4)]
        QKV = ["qT0", "qT1", "kT0", "kT1", "kTpad0", "kTpad1", "Vt0", "Vt1"]
        P.alias("acc", S16N + ["S32_0_0", "S32_0_1", "S32_1_0", "S32_1_1", "am0", "am1", "qfT", "kfT", "bias_st", "mask_st"])
        P.alias("qT0", ["kfT", "qbT"])
        P.alias("kT0", ["qbT", "kbT", "kdec"])
        P.alias("kTpad0", ["qbT", "kbT", "kdec"])
        P.alias("Vt0", ["kbT", "kdec"])
        for nm in ("qT1", "kT1", "kTpad1", "Vt1"):
            P.alias(nm, ["gv", "goutT"])
        for nm in ETN:
            P.alias(nm, ["kdec", "gv"])
        y_attnT = av(56, [2, S], BF16)
        for q4 in range(4):
            P.alias("y_attnT%d" % q4, ["sq0", "sq1", "sq2", "o_sb0", "o_sb1", "gtmp", "lrT", "lrT1"])
        it = [0]

        def proj_fm_part(slot, col, ncols, evac, tgs):
            banks = {tg: nb() for tg in tgs}
            for k in range(8):
                for tg in tgs:
                    mm(psb[banks[tg]][0:ncols, :], wst[slot][:, k, col:col + ncols],
                       xT[:, k, 256 + tg * 512:256 + (tg + 1) * 512], k == 0, k == 7,
                       XTk(k) + ["W%d" % slot], [PS(banks[tg])])
            for tg in tgs:
                evac(tg, banks[tg])

        def group_work(g, r):
            st_ = g % 2
            qT, kT, Vt = qT2[st_], kT2[st_], Vt2[st_]
            qn, kn, kpn, vn = "qT%d" % st_, "kT%d" % st_, "kTpad%d" % st_, "Vt%d" % st_
            L = S // r
            pad = 64 if g < 2 else 0
            Lp = L + 2 * pad
            nq = L // 128
            nk = 2 if g < 2 else 1
            ntile = {0: 17, 1: 5, 2: 1}[g]
            kTv = kT[:, :, 0:r * Lp].rearrange("p c (r m) -> p c r m", r=r)
            qTv = qT[:, :, :].rearrange("p c (r m) -> p c r m", r=r)
            slots = {}
            slices = []
            late = []

            def sl_first():
                slots["qk"] = next_panel()
                if pad:
                    memset("dve", kTv[:, :, :, 0:pad], 0.0, [kpn])
                    memset("dve", kTv[:, :, :, pad + L:Lp], 0.0, [kpn])
            slices.append(sl_first)
            for c in range(2):
                def evq(tg, b, c=c):
                    src = psb[b][:, :].rearrange("p (m r) -> p r m", r=r)
                    dst = qTv[:, c, :, tg * 512 // r:(tg + 1) * 512 // r]
                    cp("act" if tg % 2 else "dve", dst, src, [PS(b)], [qn])

                def evk(tg, b, c=c):
                    src = psb[b][:, :].rearrange("p (m r) -> p r m", r=r)
                    dst = kTv[:, c, :, pad + tg * 512 // r:pad + (tg + 1) * 512 // r]
                    cp("dve" if tg % 2 else "act", dst, src, [PS(b)], [kn])
                dest = late if (g == 2 and c == 1) else slices
                for tgs in ((0, 1), (2, 3)):
                    dest.append(lambda c=c, evq=evq, tgs=tgs: proj_fm_part(slots["qk"], c * 128, 128, evq, tgs))
                    dest.append(lambda c=c, evk=evk, tgs=tgs: proj_fm_part(slots["qk"], 256 + c * 128, 128, evk, tgs))

            def sl_v0():
                slots["v"] = next_panel(hold=(1 if g == 2 else 0))
            slices.append(sl_v0)
            vts = [(cls, i) for cls in range(r) for i in range(ntile)]

            def v_tiles(lst):
                for (cls, i) in lst:
                    start = 256 + r * (128 * i - pad) + cls
                    vi = cls * ntile + i

                    def evv(b, vi=vi):
                        cp("act" if vi % 2 else "dve", Vt[:, vi, :], psb[b][:, 0:256], [PS(b)], [vn])
                    proj_tm(slots["v"], 0, 256,
                            lambda k, start=start: xT[:, k, start:start + 127 * r + 1:r], evv)
            for p0 in range(0, len(vts), 2):
                slices.append(lambda lst=vts[p0:p0 + 2]: v_tiles(lst))

            pairs = []

            U = 2 if g < 2 else 4

            def do_S(c, pair, eb):
                bS = [nb(), nb()]
                for hh in range(2):
                    rows = slice(hh * 64, (hh + 1) * 64)
                    spv = psb[bS[hh]][:, 0:U * nk * 128].rearrange("p (u k q) -> p u k q", u=U, k=nk)
                    for ui, (cls, qi) in enumerate(pair):
                        for kt in range(nk):
                            mm(spv[:, ui, kt, :], kTv[rows, c, cls, 128 * (qi + kt):128 * (qi + kt) + 128],
                               qTv[rows, c, cls, 128 * qi:128 * qi + 128], True, True, [kn, kpn, qn], [PS(bS[hh])],
                               signal=(ui == U - 1 and kt == nk - 1))
                for hh in range(2):
                    spv = psb[bS[hh]][:, 0:U * nk * 128].rearrange("p (u k q) -> p u k q", u=U, k=nk)
                    er = et_raw[:, eb, hh * 4:(hh + 1) * 4, :].rearrange("p (u k) q -> p u k q", u=U)
                    ee = et[:, eb, hh * 4:(hh + 1) * 4, :].rearrange("p (u k) q -> p u k q", u=U)
                    act(er, spv, AF.Exp, [PS(bS[hh])], ["et_raw%d" % eb], scale=0.125)
                    tt("dve", ee, er,
                       expb[:, 4 * g + 2 * c + hh, 0:nk, :].unsqueeze(1).to_broadcast([128, U, nk, 128]), ALU.mult,
                       ["et_raw%d" % eb, "expb"], ["et%d" % eb])

            def do_PV(c, pair, eb):
                accv = acc[:, c, :, :].rearrange("p n (m r) -> p n m r", r=r)
                bos = {}
                for ui, (cls, qi) in enumerate(pair):
                    if ui % 2 == 0:
                        bos[ui // 2] = nb()
                    bo = bos[ui // 2]
                    pv = psb[bo][:, (ui % 2) * 256:(ui % 2) * 256 + 256].rearrange("p (n q) -> p n q", n=2)
                    for hh in range(2):
                        rows = slice(hh * 64, (hh + 1) * 64)
                        hcol = (2 * c + hh) * 64
                        for n in range(2):
                            for kt in range(nk):
                                vi = cls * ntile + qi + kt
                                if n == 0:
                                    l = Vt[:, vi, hcol:hcol + 64]
                                elif g < 2 and qi + kt == 0:
                                    l = onesF
                                elif g < 2 and qi + kt == ntile - 1:
                                    l = onesL
                                else:
                                    l = ones[:, 0:64]
                                mm(pv[rows, n, :], l, et[:, eb, hh * 4 + ui * nk + kt, :], kt == 0, kt == nk - 1,
                                   [vn, "c16", "et%d" % eb], [PS(bo)],
                                   signal=(hh == 1 and n == 1 and kt == nk - 1))
                for ui, (cls, qi) in enumerate(pair):
                    bo = bos[ui // 2]
                    pv = psb[bo][:, (ui % 2) * 256:(ui % 2) * 256 + 256].rearrange("p (n q) -> p n q", n=2)
                    dst = accv[:, :, 128 * qi:128 * qi + 128, cls]
                    if g == 0:
                        cp("act", dst, pv, [PS(bo)], ["acc"])
                    else:
                        tt("dve", dst, pv, dst, ALU.add, [PS(bo), "acc"], ["acc"])

            units = [(cls, qi) for cls in range(r) for qi in range(nq)]
            for c in range(2):
                for up in range(0, len(units), U):
                    pr = units[up:up + U]
                    pairs.append((lambda eb, c=c, pr=pr: do_S(c, pr, eb), lambda eb, c=c, pr=pr: do_PV(c, pr, eb)))
            return slices, pairs, late

        work = [group_work(g, r) for g, r in enumerate((1, 4, 16))]
        for f in work[0][0]:
            f()
        wbr = av(144, [6, D], BF16)
        vec_ln1g = av(160, [D], F32)
        vec_ln1b = av(164, [D], F32)
        vec_ln2g = av(168, [D], F32)
        for g in range(3):
            pairs = work[g][1]
            nxt = work[g + 1][0] if g + 1 < 3 else work[g][2]
            if g == 2:
                for nm in ("wbr", "vecs"):
                    P.alias(nm, ["qT1", "kT1", "kTpad1", "Vt1", "gv", "goutT"])
                P.dma("pool", wbr[:, 0:2, :], wab_d.rearrange("(k p) c -> p k c", p=128), writes=["wbr"], chan="wbr")
                P.dma("pool", wbr[:, 2:6, :], wgb_d.rearrange("(k p) c -> p k c", p=128), writes=["wbr"], chan="wbr")
                P.dma("sp", vec_ln1g, vec_d[:, 0, :], writes=["vecs"], chan="vecs")
                P.dma("sp", vec_ln1b, vec_d[:, 1, :], writes=["vecs"], chan="vecs")
                P.dma("sp", vec_ln2g, vec_d[:, 2, :], writes=["vecs"], chan="vecs")
            done = 0
            ebs = [(it[0] + i) % 4 for i in range(len(pairs))]
            it[0] += len(pairs)
            skew = 1 if g < 2 else 2
            for pi in range(min(skew, len(pairs))):
                pairs[pi][0](ebs[pi])
            for pi in range(len(pairs)):
                if pi + skew < len(pairs):
                    pairs[pi + skew][0](ebs[pi + skew])
                want = (pi + 1) * len(nxt) // len(pairs) if g < 2 else min(len(nxt), 2 * (pi + 1))
                while done < want:
                    nxt[done]()
                    done += 1
                pairs[pi][1](ebs[pi])

        if STOP == "A0":
            P.emit()
            return nc
        rden = av(96, [2, S], F32)
        for q4 in range(4):
            P.alias("rden%d" % q4, ["kfT", "qbT", "kbT", "qT0", "kT0", "kTpad0"])
        for q4 in range(4):
            qs = slice(q4 * 512, (q4 + 1) * 512)
            act(rden[:, :, qs], acc[:, :, 1, qs], AF.Ln, ["acc"], ["rden%d" % q4])
            act(rden[:, :, qs], rden[:, :, qs], AF.Exp, ["rden%d" % q4], ["rden%d" % q4], scale=-1.0)
            tt("dve", y_attnT[:, :, qs], acc[:, :, 0, qs], rden[:, :, qs], ALU.mult, ["acc", "rden%d" % q4], ["y_attnT%d" % q4])

        if debug:
            dtmp = av(112, [4, S], F32)
            P.alias("dtmp", ["gv", "kdec", "rden0", "rden1", "rden2", "rden3"] + ETN + QKV)
            cp("dve", dtmp[:, 0:2, :], y_attnT, ["y_attnT0", "y_attnT1", "y_attnT2", "y_attnT3"], ["dtmp"])
            P.dma("sp", dbg_d["d_yattn"], dtmp[:, 0:2, :], reads=["dtmp"], chan="dbg0")
            cp("dve", dtmp, y_glaT, ["y_glaT"], ["dtmp"])
            P.dma("sp", dbg_d["d_ygla"], dtmp, reads=["dtmp"], chan="dbg1")

        if STOP == "A":
            P.emit()
            return nc
        mergedT = av(64, [8, S], BF16)
        mtmp = av(128, [2, 2, 512], F32)
        sgt = av(136, [2, 2, 512], F32)
        for q4 in range(4):
            P.alias("mergedT%d" % q4, ["acc"])
        for nm in ("mtmp0", "mtmp1", "sgt0", "sgt1"):
            P.alias(nm, ["dtmp", "gv", "kdec"] + QKV + ETN)
        wout = av(96, [8, D], BF16)
        X1Bv = [av(112, [4, D], F32), av(144, [4, D], F32)]
        x1T = av(128, [2, 8, 512], BF16)
        vec_ln2b = av(58, [D], F32)
        hT = av(0, [32, 512], BF16)
        b2v = av(32, [D], F32)
        x1h = av(36, [2, D], BF16)
        rtmp4 = av(40, [2, 512], BF16)
        rtmp4b = av(62, [2, 512], BF16)
        rtmp = [rtmp4[:, 0, :], rtmp4[:, 1, :], rtmp4b[:, 0, :], rtmp4b[:, 1, :]]
        u2 = av(42, [4, D], F32)
        vecs = {0: (vec_ln1g, vec_ln1b), 2: (vec_ln2g, vec_ln2b)}
        P.alias("wout", QKV + ["rden0", "rden1", "rden2", "rden3"])
        X1B = ["x1b%d_%d" % (bf, i) for bf in range(2) for i in range(4)]
        for nm in X1B[0:4]:
            P.alias(nm, ["kdec", "dtmp"] + ETN + QKV)
        P.dma("pool", wout, wout_d.rearrange("(k p) c -> p k c", p=128), writes=["wout"], chan="wout")
        x_tiles = x_d.rearrange("(t p) d -> t p d", p=128)
        out_tiles = out_d.rearrange("(t p) d -> t p d", p=128)

        def layernorm(u_ap, unm, slot, gi, out_ap, onm):
            vg, vb = vecs[gi]
            for hf in range(2):
                P.op("dve", lambda e, hf=hf: e.bn_stats(stats[:, slot, hf, :], u_ap[:, hf * 512:(hf + 1) * 512]),
                     [unm], ["stats%d" % slot])
            P.op("dve", lambda e: e.bn_aggr(mv[:, slot, 0:2], stats[:, slot, :, :]), ["stats%d" % slot], ["mv%d" % slot])
            act(mv[:, slot, 2:3], mv[:, slot, 1:2], AF.Sqrt, ["mv%d" % slot, "epst"], ["mv%d" % slot], bias=epst[:, 0:1])
            P.op("dve", lambda e: e.reciprocal(mv[:, slot, 2:3], mv[:, slot, 2:3]), ["mv%d" % slot], ["mv%d" % slot])
            stt("dve", mv[:, slot, 3:4], mv[:, slot, 0:1], -1.0, mv[:, slot, 2:3], ALU.mult, ALU.mult,
                ["mv%d" % slot], ["mv%d" % slot])
            act(out_ap, u_ap, AF.Identity, [unm, "mv%d" % slot], [onm], bias=mv[:, slot, 3:4], scale=mv[:, slot, 2:3])
            tt("dve", out_ap, out_ap, vg, ALU.mult, [onm, "vecs", "vecs2"], [onm])
            tt("dve", out_ap, out_ap, vb, ALU.add, [onm, "vecs", "vecs2"], [onm])

        def F1a(tg, tiles, cast=True):
            bf = tg % 2
            for tt_ in tiles:
                t = tg * 4 + tt_
                xs = t % 2
                xnm = "x1b%d_%d" % (bf, tt_)
                xt = X1Bv[bf][:, tt_, :]
                P.dma("sp", xt, x_tiles[t], writes=[xnm], chan=xnm)
                bks = [nb(), nb()]
                for hf in range(2):
                    for j in range(8):
                        mm(psb[bks[hf]][:, :], mergedT[:, j, t * 128:(t + 1) * 128], wout[:, j, hf * 512:(hf + 1) * 512],
                           j == 0, j == 7, ["mergedT%d" % tg, "wout"], [PS(bks[hf])])
                for hf in range(2):
                    stt("dve", xt[:, hf * 512:(hf + 1) * 512], xt[:, hf * 512:(hf + 1) * 512], ALPHA,
                        psb[bks[hf]][:, :], ALU.mult, ALU.add, [xnm, PS(bks[hf])], [xnm])
                layernorm(xt, xnm, xs, 0, xt, xnm)
            if cast:
                F1a_cast(tg, tiles)

        def F1a_cast(tg, tiles):
            bf = tg % 2
            for tt_ in tiles:
                xs = (tg * 4 + tt_) % 2
                xnm = "x1b%d_%d" % (bf, tt_)
                xt = X1Bv[bf][:, tt_, :]
                cp("act", x1h[:, xs, :], xt, [xnm], ["x1h%d" % xs])
                stt("dve", xt, xt, ALPHA, b2v, ALU.mult, ALU.add, [xnm, "b2v"], [xnm])

        def F1b(tg, tiles):
            bf = tg % 2
            for tt_ in tiles:
                xs = (tg * 4 + tt_) % 2
                bt = nb()
                ptb = psb[bt][:, :].bitcast(BF16)
                for k in range(8):
                    P.op("pe", lambda e, k=k, xs=xs, ptb=ptb: e.transpose(ptb[:, k * 128:(k + 1) * 128],
                                                                         x1h[:, xs, k * 128:(k + 1) * 128], ident),
                         ["x1h%d" % xs, "c16"], [PS(bt)], signal=(k == 7))
                cp("act", x1T[:, bf, :, tt_ * 128:(tt_ + 1) * 128], ptb.rearrange("p (k c) -> p k c", k=8), [PS(bt)],
                   ["x1T%d" % bf])

        def F2(tg):
            bf = tg % 2
            for u in range(8):
                s1 = next_panel()
                for f4 in range(4):
                    fc = u * 4 + f4
                    b = nb()
                    for k in range(8):
                        mm(psb[b][:, :], wst[s1][:, k, f4 * 128:(f4 + 1) * 128], x1T[:, bf, k, :], k == 0, k == 7,
                           ["W%d" % s1, "x1T%d" % bf], [PS(b)])
                    rb = fc % 4
                    act(rtmp[rb], psb[b][:, :], AF.Relu, [PS(b), "b1T"], ["rtmp%d" % rb], bias=b1T[:, fc:fc + 1])
                    tt("dve", hT[:, fc, :], rtmp[rb], rtmp[rb], ALU.mult, ["rtmp%d" % rb], ["hT"])
                if tg > 0 and 2 <= u <= 5:
                    LN2(tg - 1, [u - 2])

        def F3h(tg, hf, before_last=None, tile_sets=((0, 1, 2, 3),), after_set=None):
            bf = tg % 2
            for si, tset in enumerate(tile_sets):
                banks = {tt_: nb() for tt_ in tset}
                for u in range(4):
                    if u == 3 and before_last is not None and si == len(tile_sets) - 1:
                        before_last()
                    s2 = next_panel()
                    for tt_ in tset:
                        for f8 in range(8):
                            fc = u * 8 + f8
                            mm(psb[banks[tt_]][:, :], hT[:, fc, tt_ * 128:(tt_ + 1) * 128], wst[s2][:, f8, :],
                               fc == 0, fc == 31, ["hT", "W%d" % s2], [PS(banks[tt_])])
                for tt_ in tset:
                    tt("dve", u2[:, tt_, hf * 512:(hf + 1) * 512], psb[banks[tt_]][:, :],
                       X1Bv[bf][:, tt_, hf * 512:(hf + 1) * 512], ALU.add,
                       [PS(banks[tt_]), "x1b%d_%d" % (bf, tt_)], ["u2_%d" % tt_])
                if after_set is not None:
                    after_set(tset)

        def LN2(tg, tiles):
            for tt_ in tiles:
                t = tg * 4 + tt_
                layernorm(u2[:, tt_, :], "u2_%d" % tt_, 2 + tt_ % 2, 2, u2[:, tt_, :], "u2_%d" % tt_)
                P.dma("sp", out_tiles[t], u2[:, tt_, :], reads=["u2_%d" % tt_], chan="ot%d" % tt_)


        mit = [0]
        for u in range(2):
            s_ga = next_panel()
            s_gg = next_panel(hold=1)
            order = [(jj, tg) for jj in range(4) for tg in range(4)] if u == 0 else \
                    [(jj, tg) for tg in range(4) for jj in range(4)]
            for (jj, tg) in order:
                if True:
                    j = u * 4 + jj
                    mb = mit[0] % 2
                    mit[0] += 1
                    tks = slice(tg * 512, (tg + 1) * 512)
                    b_ga, b_gg, b_za, b_zg = nb(), nb(), nb(), nb()
                    for (bq, sl) in ((b_ga, s_ga), (b_gg, s_gg)):
                        for k in range(8):
                            mm(psb[bq][:, :], wst[sl][:, k, jj * 128:(jj + 1) * 128],
                               xT[:, k, 256 + tg * 512:256 + (tg + 1) * 512], k == 0, k == 7,
                               XTk(k) + ["W%d" % sl], [PS(bq)])
                    for k in range(2):
                        mm(psb[b_za][:, :], wbr[:, k, j * 128:(j + 1) * 128], y_attnT[:, k, tks], k == 0, k == 1,
                           ["wbr", "y_attnT%d" % tg], [PS(b_za)])
                    for k in range(4):
                        mm(psb[b_zg][:, :], wbr[:, 2 + k, j * 128:(j + 1) * 128], y_glaT[:, k, tks], k == 0, k == 3,
                           ["wbr", "y_glaT"], [PS(b_zg)])
                    act(sgt[:, mb, 0, :], psb[b_ga][:, :], AF.Sigmoid, [PS(b_ga)], ["sgt%d" % mb])
                    act(sgt[:, mb, 1, :], psb[b_gg][:, :], AF.Sigmoid, [PS(b_gg)], ["sgt%d" % mb])
                    tt("dve", mtmp[:, mb, 0, :], psb[b_za][:, :], sgt[:, mb, 0, :], ALU.mult, [PS(b_za), "sgt%d" % mb], ["mtmp%d" % mb])
                    tt("dve", mtmp[:, mb, 1, :], psb[b_zg][:, :], sgt[:, mb, 1, :], ALU.mult, [PS(b_zg), "sgt%d" % mb], ["mtmp%d" % mb])
                    tt("dve", mergedT[:, j, tks], mtmp[:, mb, 0, :], mtmp[:, mb, 1, :], ALU.add, ["mtmp%d" % mb], ["mergedT%d" % tg])
                    if u == 1 and not debug and (tg, jj) in ((1, 0), (1, 3), (2, 2), (3, 1)):
                        F1a(0, ({(1, 0): 0, (1, 3): 1, (2, 2): 2, (3, 1): 3}[(tg, jj)],), cast=False)

        if debug:
            dtm = av(0, [8, S], F32)
            P.alias("dtm", XT + ["y_glaT"] + ["y_attnT0", "y_attnT1", "y_attnT2", "y_attnT3"])
            cp("dve", dtm, mergedT, ["mergedT0", "mergedT1", "mergedT2", "mergedT3"], ["dtm"])
            P.dma("sp", dbg_d["d_merged"], dtm, reads=["dtm"], chan="dbg2")

        if STOP == "M":
            P.emit()
            return nc
        for nm in X1B[4:8]:
            P.alias(nm, ["wbr", "gv", "goutT", "dtmp"] + QKV)
        for nm in ("x1T0", "x1T1"):
            P.alias(nm, ["mtmp0", "mtmp1", "sgt0", "sgt1", "gv", "kdec", "dtmp"] + ETN + QKV)
        for nm in ("hT", "b2v", "x1h0", "x1h1", "rtmp0", "rtmp1", "rtmp2", "rtmp3", "u2_0", "u2_1", "u2_2", "u2_3", "vecs2"):
            P.alias(nm, XT + ["y_glaT", "dtm"] + ["y_attnT0", "y_attnT1", "y_attnT2", "y_attnT3"])
        P.dma("sp", vec_ln2b, vec_d[:, 3, :], writes=["vecs2"], chan="vecs2")
        P.dma("sp", b2v, vec_d[:, 4, :], writes=["b2v"], chan="b2v")
        if debug:
            F1a(0, (0, 1, 2, 3), cast=False)
        for tl in ((0, 1), (2, 3)):
            F1a_cast(0, tl)
            F1b(0, tl)
        for tg in range(4):
            F2(tg)
            nxt = tg + 1 < 4
            if nxt:
                F1a(tg + 1, (0, 1))
            F3h(tg, 0)
            if nxt:
                F1b(tg + 1, (0, 1))
                F1a(tg + 1, (2, 3))
            if nxt:
                F3h(tg, 1, before_last=lambda tg=tg: F1b(tg + 1, (2, 3)))
            else:
                F3h(tg, 1, tile_sets=((0, 1), (2, 3)), after_set=lambda tset: LN2(3, list(tset)))

        P.emit()
    return nc


_NC_CACHE = {}


def _prep_shared(w_in, rel_bias, w_lr_fwd, b_lr_fwd, w_lr_bwd, b_lr_bwd, gla_norm_g, w_attn_branch,
                 w_gla_branch, w_out, ln1_g, ln1_b, w_ff1, b_ff1, w_ff2, b_ff2, ln2_g, ln2_b):
    f = lambda a: np.ascontiguousarray(np.asarray(a, dtype=np.float32))
    rel_bias = f(rel_bias)
    buckets, masks = _host_tables()
    bias = np.zeros((256, 12, 128), np.float32)
    for g in range(3):
        for hs in range(4):
            bias[:, 4 * g + hs, :] = rel_bias[buckets[g], 4 * g + hs]
    biasT = np.ascontiguousarray(bias.reshape(2, 128, 12, 128).transpose(1, 2, 0, 3))
    maskT = np.ascontiguousarray(np.stack(masks, 1).reshape(2, 128, 3, 128).transpose(1, 2, 0, 3))
    wlr = np.zeros((33, 512), np.float32)
    wlr[0:16, 0:256] = f(w_lr_fwd)[0]
    wlr[16:32, 256:512] = f(w_lr_bwd)[0]
    wlr[32, 0:256] = f(b_lr_fwd)[0]
    wlr[32, 256:512] = f(b_lr_bwd)[0]
    c32, c16 = _consts()
    vecs = np.stack([f(ln1_g)[0], f(ln1_b)[0], f(ln2_g)[0], f(ln2_b)[0], f(b_ff2)[0]], 0)
    vecs = np.ascontiguousarray(np.broadcast_to(vecs[None], (128, 5, D)))
    return {
        "w_in": f(w_in)[0], "wlr": wlr, "c32": c32, "c16": c16, "biasT": biasT, "maskT": maskT,
        "w_ab": f(w_attn_branch)[0], "w_gb": f(w_gla_branch)[0], "w_out": f(w_out)[0],
        "w_ff1": f(w_ff1)[0], "b1T": np.ascontiguousarray(f(b_ff1)[0].reshape(32, 128).T),
        "w_ff2": f(w_ff2)[0], "gT": np.ascontiguousarray(f(gla_norm_g)[0].reshape(4, 128).T),
        "vecs": vecs,
    }


def kernel(x, **params):
    debug = bool(os.environ.get("MK_DEBUG"))
    x = np.asarray(x, dtype=np.float32)
    B = x.shape[0]
    shared = _prep_shared(**params)
    key = debug
    if key not in _NC_CACHE:
        _NC_CACHE[key] = build_nc(debug)
    nc = _NC_CACHE[key]
    in_maps = []
    for b in range(B):
        m = dict(shared)
        m["x"] = np.ascontiguousarray(x[b])
        m["xT"] = np.ascontiguousarray(x[b].T)
        in_maps.append(m)
    res = run_bass_kernel_spmd(nc, in_maps, core_ids=list(range(B)))
    out = np.stack([np.asarray(r["out"], dtype=np.float32) for r in res.results], 0)
    if debug:
        kernel.dbg = [{k: np.asarray(v) for k, v in r.items() if k.startswith("d_")} for r in res.results]
    return out
```

```python
import contextlib
import os
import numpy as np
import concourse.bass as bass
import concourse.mybir as mybir
from concourse.bass_utils import run_bass_kernel_spmd

F32 = mybir.dt.float32
BF16 = mybir.dt.bfloat16
AF = mybir.ActivationFunctionType
ALU = mybir.AluOpType

S = 2048
D = 1024
DFF = 4096
NCOL = 5920
ALPHA = 2.0 ** 0.25
LN_EPS = 1e-5
NORM_EPS = 1e-6
NSLOT = 3
SAME_ENGINE_WAR_SYNC = True
KB = 1024

ENGS = ("pe", "act", "dve", "pool", "sp")


class Prog:
    def __init__(self, nc):
        self.nc = nc
        self.ops = []
        self.lastw = {}
        self.readers = {}

    def op(self, eng, fn, reads=(), writes=(), signal=True, chan=None):
        oid = len(self.ops)
        deps = set()
        for r in reads:
            w = self.lastw.get(r)
            if w is not None:
                deps.add((w, "raw"))
        for r in writes:
            w = self.lastw.get(r)
            if w is not None:
                deps.add((w, "waw"))
            for rd in self.readers.get(r, ()):
                deps.add((rd, "war"))
        for r in reads:
            self.readers.setdefault(r, []).append(oid)
        for r in writes:
            self.lastw[r] = oid
            self.readers[r] = []
        self.ops.append(dict(id=oid, eng=eng, fn=fn, deps=deps, signal=signal, chan=chan))
        return oid

    def alias(self, new, olds):
        rs = list(self.readers.get(new, []))
        w = self.lastw.get(new)
        if w is not None:
            rs.append(w)
        for o in olds:
            rs.extend(self.readers.get(o, []))
            w = self.lastw.get(o)
            if w is not None:
                rs.append(w)
        self.readers[new] = rs
        self.lastw.pop(new, None)

    def dma(self, eng, out, in_, reads=(), writes=(), chan=None):
        assert chan is not None
        return self.op(eng, lambda e: e.dma_start(out=out, in_=in_), reads, writes, True, chan)

    def finalize(self):
        ops = self.ops
        cnt = {}
        last_of = {}
        for o in ops:
            if o["chan"] is None:
                last_of[o["eng"]] = o["id"]
        for e, i in last_of.items():
            ops[i]["signal"] = True
        import bisect
        sig_ids = {}
        for o in ops:
            if o["chan"] is None and o["signal"]:
                sig_ids.setdefault(o["eng"], []).append(o["id"])
        for o in ops:
            for (d, kind) in o["deps"]:
                do = ops[d]
                if do["chan"] is not None or do["signal"]:
                    continue
                lst = sig_ids.setdefault(do["eng"], [])
                p = bisect.bisect_left(lst, d)
                if p >= len(lst) or lst[p] >= o["id"]:
                    do["signal"] = True
                    bisect.insort(lst, d)
        pending = {}
        for o in ops:
            if o["chan"] is not None:
                k = "ch_" + o["chan"]
                cnt[k] = cnt.get(k, 0) + 16
                o["tok"] = (k, cnt[k])
            else:
                k = "e_" + o["eng"]
                pending.setdefault(k, []).append(o)
                if o["signal"]:
                    cnt[k] = cnt.get(k, 0) + 1
                    for p in pending[k]:
                        p["tok"] = (k, cnt[k])
                    pending[k] = []
        self.semkeys = sorted(cnt.keys())
        self.cnt = cnt
        chan_hist = {}
        for o in ops:
            if o["chan"] is not None:
                chan_hist.setdefault(o["tok"][0], []).append((o["id"], o["tok"][1]))
        known = {e: {} for e in ENGS}
        snap = {}
        per_eng = {e: [] for e in ENGS}
        for o in ops:
            e = o["eng"]
            kn = known[e]
            need = {}
            for (d, kind) in o["deps"]:
                do = ops[d]
                same = (do["eng"] == e) and do["chan"] is None and o["chan"] is None
                if same and kind != "raw":
                    if e != "pe" and kn.get(do["tok"][0], 0) < do["tok"][1]:
                        self.n_same_war = getattr(self, "n_same_war", 0) + 1
                        if SAME_ENGINE_WAR_SYNC:
                            k, v = do["tok"]
                            if v > need.get(k, (0, None))[0]:
                                need[k] = (v, d)
                    continue
                k, v = do["tok"]
                if v > need.get(k, (0, None))[0]:
                    need[k] = (v, d)
            waits = {}
            for k, (v, d) in need.items():
                if k.startswith("ch_"):
                    lst = chan_hist[k]
                    p = bisect.bisect_left(lst, (o["id"], 0)) - 1
                    v = max(v, lst[p][1])
                if kn.get(k, 0) >= v:
                    continue
                waits[k] = v
                kn[k] = v
                for kk, vv in snap[d].items():
                    if kn.get(kk, 0) < vv:
                        kn[kk] = vv
            o["waits"] = waits
            snap[o["id"]] = dict(kn)
            per_eng[e].append(o)
        return per_eng

    def emit(self):
        nc = self.nc
        per_eng = self.finalize()
        with contextlib.ExitStack() as st:
            sems = {k: st.enter_context(nc.semaphore(k)) for k in self.semkeys}
            block = st.enter_context(nc.Block())

            def run(ename):
                def body(eng):
                    for o in per_eng[ename]:
                        for k, v in o["waits"].items():
                            eng.wait_ge(sems[k], v)
                        ins = o["fn"](eng)
                        if o["chan"] is not None:
                            ins.then_inc(sems[o["tok"][0]], 16)
                        elif o["signal"]:
                            ins.then_inc(sems[o["tok"][0]], 1)
                    if ename == "sp":
                        for k in self.semkeys:
                            if k.startswith("ch_"):
                                eng.wait_ge(sems[k], self.cnt[k])
                return body

            block.sync(run("sp"))
            block.gpsimd(run("pool"))
            block.scalar(run("act"))
            block.vector(run("dve"))
            block.tensor(run("pe"))


def _t5_bucket(rel):
    nb = 16
    max_exact = 8
    ret = (rel > 0).astype(np.int32) * nb
    n = np.abs(rel)
    large = max_exact + (np.log(np.maximum(n, 1) / max_exact) / np.log(1024 / max_exact) * (nb - max_exact)).astype(np.int32)
    large = np.minimum(large, nb - 1)
    return (ret + np.where(n < max_exact, n, large)).astype(np.int32)


def _host_tables():
    j = np.arange(256)[:, None]
    i = np.arange(128)[None, :]
    buckets, masks = [], []
    for g, dil in enumerate((1, 4, 16)):
        delta = (j - 64 - i) if g < 2 else (j - i)
        m = (np.abs(delta) <= 64)
        if g == 2:
            m = m & (j < 128)
        buckets.append(_t5_bucket(delta * dil))
        masks.append(m.astype(np.float32))
    return buckets, masks


def _consts():
    s = np.arange(128)[:, None]
    c = np.arange(128)[None, :]
    v = -1.0 / 16.0
    c32 = np.stack([
        np.where(s <= c, v, 0.0), np.where(s >= c, v, 0.0),
        np.where(s > c, v, 0.0), np.where(s < c, v, 0.0)], 0).astype(np.float32)
    c32 = np.ascontiguousarray(c32.transpose(1, 0, 2))
    maskF = (s <= c).astype(np.float32)
    maskB = (s > c).astype(np.float32)
    ident = np.eye(128, dtype=np.float32)
    ones = np.ones((128, 128), np.float32)
    onesF = np.ones((128, 64), np.float32); onesF[0:64] = 0
    onesL = np.ones((128, 64), np.float32); onesL[64:128] = 0
    c16 = np.concatenate([maskF, maskB, ident, ones, onesF, onesL], 1)
    return c32, np.ascontiguousarray(c16)


def build_nc(debug=False):
    STOP = os.environ.get("MK_STOP", "")
    nc = bass.Bass("TRN2", target_bir_lowering=False)

    def din(name, shape):
        return nc.dram_tensor(name, list(shape), F32, kind="ExternalInput").ap()

    xT_d = din("xT", [D, S])
    x_d = din("x", [S, D])
    win_d = din("w_in", [D, NCOL])
    wlr_d = din("wlr", [33, 512])
    c32_d = din("c32", [128, 4, 128])
    c16_d = din("c16", [128, 640])
    bias_d = din("biasT", [128, 12, 2, 128])
    mask_d = din("maskT", [128, 3, 2, 128])
    wab_d = din("w_ab", [256, D])
    wgb_d = din("w_gb", [512, D])
    wout_d = din("w_out", [D, D])
    w1_d = din("w_ff1", [D, DFF])
    b1_d = din("b1T", [128, 32])
    w2_d = din("w_ff2", [DFF, D])
    gT_d = din("gT", [128, 4])
    vec_d = din("vecs", [128, 5, D])
    out_d = nc.dram_tensor("out", [S, D], F32, kind="ExternalOutput").ap()
    dbg_d = {}
    if debug:
        for nm, shp in (("d_yattn", [128, 2, S]), ("d_ygla", [128, 4, S]), ("d_merged", [128, 8, S])):
            dbg_d[nm] = nc.dram_tensor(nm, shp, F32, kind="ExternalOutput").ap()

    with contextlib.ExitStack() as st:
        def sb(name, shape, dt):
            return st.enter_context(nc.sbuf_tensor("sb_" + name, list(shape), dt))

        ARENA_KB = 172
        arena = sb("arena", [128, ARENA_KB * KB // 2], BF16)

        def av(off_kb, shape, dt):
            n = int(np.prod(shape))
            nb = n * (4 if dt == F32 else 2)
            o = int(off_kb * KB)
            assert o + nb <= ARENA_KB * KB, (off_kb, shape)
            ap = arena[:, o // 2:(o + nb) // 2]
            if dt == F32:
                ap = ap.bitcast(F32)
            if len(shape) == 2:
                ap = ap.rearrange("p (a b) -> p a b", a=shape[0])
            elif len(shape) == 3:
                ap = ap.rearrange("p (a b c) -> p a b c", a=shape[0], b=shape[1])
            elif len(shape) == 4:
                ap = ap.rearrange("p (a b c d) -> p a b c d", a=shape[0], b=shape[1], c=shape[2])
            return ap

        wst = [sb("wst%d" % i, [128, 8, 512], BF16) for i in range(NSLOT)]
        c32 = sb("c32", [128, 4, 128], F32)
        c16 = sb("c16", [128, 640], BF16)
        expb = sb("expb", [128, 12, 2, 128], BF16)
        wlr = sb("wlr", [33, 512], BF16)
        b1T = sb("b1T", [128, 32], F32)
        gT = sb("gT", [128, 4], F32)
        dec = sb("dec", [128, 2, 2, 16], F32)
        stats = sb("stats", [128, 4, 2, 6], F32)
        mv = sb("mv", [128, 4, 4], F32)
        epst = sb("epst", [128, 2], F32)
        psb = [st.enter_context(nc.psum_tensor("ps%d" % i, [128, 512], F32)) for i in range(8)]

        maskF = c16[:, 0:128]
        maskB = c16[:, 128:256]
        ident = c16[:, 256:384]
        ones = c16[:, 384:512]
        onesF = c16[:, 512:576]
        onesL = c16[:, 576:640]

        P = Prog(nc)
        bank_ctr = [0]

        def nb():
            b = bank_ctr[0] % 8
            bank_ctr[0] += 1
            return b

        def PS(b):
            return "ps%d" % b

        def mm(out, lhsT, rhs, start, stop, reads, writes, signal=None):
            P.op("pe", lambda e: e.matmul(out, lhsT, rhs, start=start, stop=stop), reads, writes,
                 signal=(stop if signal is None else signal))

        def act(out, in_, func, reads, writes, bias=None, scale=None):
            kw = {}
            if bias is not None:
                kw["bias"] = bias
            if scale is not None:
                kw["scale"] = scale
            P.op("act", lambda e: e.activation(out, in_, func, **kw), reads, writes)

        def tt(eng, out, in0, in1, op, reads, writes):
            P.op(eng, lambda e: e.tensor_tensor(out, in0, in1, op), reads, writes)

        def ts(eng, out, in0, s1, s2, op0, op1, reads, writes):
            if op1 is None:
                P.op(eng, lambda e: e.tensor_scalar(out, in0, s1, None, op0), reads, writes)
            else:
                P.op(eng, lambda e: e.tensor_scalar(out, in0, s1, s2, op0, op1), reads, writes)

        def stt(eng, out, in0, scalar, in1, op0, op1, reads, writes):
            P.op(eng, lambda e: e.scalar_tensor_tensor(out, in0, scalar, in1, op0, op1), reads, writes)

        def cp(eng, out, in_, reads, writes):
            if eng == "act":
                P.op("act", lambda e: e.activation(out, in_, AF.Copy), reads, writes)
            else:
                P.op(eng, lambda e: e.tensor_copy(out, in_), reads, writes)

        def memset(eng, ap, val, writes):
            P.op(eng, lambda e: e.memset(ap, val), (), writes)

        panel_seq = []
        ptr = {"use": 0, "issue": 0}

        def win_panel(c0, n):
            return win_d[:, c0:c0 + n].rearrange("(k p) c -> p k c", p=128)

        def issue_panels(hold=0):
            while ptr["issue"] < min(len(panel_seq), ptr["use"] + NSLOT - hold):
                s = ptr["issue"] % NSLOT
                for (off, n, src) in panel_seq[ptr["issue"]]:
                    P.dma("pool", wst[s][:, :, off:off + n], src, writes=["W%d" % s], chan="W%d" % s)
                ptr["issue"] += 1

        def next_panel(hold=0):
            issue_panels(hold)
            s = ptr["use"] % NSLOT
            ptr["use"] += 1
            return s

        panel_seq.append([(0, 32, win_panel(3840, 32))])
        panel_seq.append([(0, 512, win_panel(2304, 512))])
        panel_seq.append([(0, 512, win_panel(3328, 512))])
        panel_seq.append([(0, 512, win_panel(2816, 512))])
        for g in range(3):
            panel_seq.append([(0, 256, win_panel(g * 256, 256)), (256, 256, win_panel(768 + g * 256, 256))])
            panel_seq.append([(0, 256, win_panel(1536 + g * 256, 256))])
        for u in range(2):
            panel_seq.append([(0, 512, win_panel(3872 + u * 512, 512))])
            panel_seq.append([(0, 512, win_panel(4896 + u * 512, 512))])
        for tg in range(4):
            for u in range(8):
                panel_seq.append([(0, 512, w1_d[:, u * 512:(u + 1) * 512].rearrange("(k p) c -> p k c", p=128))])
            for half in range(2):
                for rep in range(2 if (tg == 3 and half == 1) else 1):
                    for u in range(4):
                        panel_seq.append([(0, 512, w2_d[u * 1024:(u + 1) * 1024, half * 512:(half + 1) * 512]
                                           .rearrange("(f p) c -> p f c", p=128))])

        P.dma("sp", c32[:, :, :], c32_d, writes=["c32"], chan="c32")
        P.dma("sp", b1T[:, :], b1_d, writes=["b1T"], chan="small")
        P.dma("sp", gT[:, :], gT_d, writes=["gT"], chan="gT")
        memset("dve", epst[:, 0:1], LN_EPS, ["epst"])
        memset("dve", epst[:, 1:2], NORM_EPS, ["epst"])

        xT = av(0, [8, 2560], BF16)
        memset("dve", xT[:, :, 0:256], 0.0, ["xTpadL"])
        memset("dve", xT[:, :, 2304:2560], 0.0, ["xTpadR"])
        issue_panels(hold=1)
        xT_src = xT_d.rearrange("(k p) t -> p k t", p=128)
        for k in range(8):
            P.dma("pool", xT[:, k, 256:2304], xT_src[:, k, :], writes=["xT%d" % k], chan="xT%d" % k)
        XT = ["xT%d" % k for k in range(8)] + ["xTpadL", "xTpadR"]

        def XTk(k):
            return ["xT%d" % k, "xTpadL", "xTpadR"]
        P.dma("pool", wlr[:, :], wlr_d, writes=["wlr"], chan="wlr")
        P.dma("pool", c16[:, :], c16_d, writes=["c16"], chan="c16")

        if STOP == "INIT":
            P.emit()
            return nc
        y_glaT = av(40, [4, S], BF16)
        gqT = av(64, [2, S], BF16)
        gkT = av(72, [2, S], BF16)
        gk_tm = av(80, [16, 256], BF16)
        qfT = av(88, [2, S], BF16)
        kfT = av(96, [2, S], BF16)
        qbT = av(104, [2, S], BF16)
        kbT = av(112, [2, S], BF16)
        kdec = av(120, [16, 2, 256], BF16)
        gv = av(136, [16, 512], BF16)
        goutT = av(152, [4, S], BF16)
        lrT = av(56, [S], BF16)

        def proj_fm(slot, col, ncols, evac):
            banks = [nb() for _ in range(4)]
            for k in range(8):
                for tg in range(4):
                    mm(psb[banks[tg]][0:ncols, :], wst[slot][:, k, col:col + ncols],
                       xT[:, k, 256 + tg * 512:256 + (tg + 1) * 512], k == 0, k == 7,
                       XTk(k) + ["W%d" % slot], [PS(banks[tg])])
            for tg in range(4):
                evac(tg, banks[tg])

        def proj_tm(slot, col, ncols, tok_ap_fn, evac):
            b = nb()
            for k in range(8):
                mm(psb[b][:, 0:ncols], tok_ap_fn(k), wst[slot][:, k, col:col + ncols], k == 0, k == 7,
                   XTk(k) + ["W%d" % slot], [PS(b)])
            evac(b)

        s_lr = next_panel()
        memset("dve", lrT[32:33, :], 1.0, ["lrT1"])

        def ev_lr(tg, b):
            cp("act", lrT[0:32, tg * 512:(tg + 1) * 512], psb[b][0:32, :], [PS(b)], ["lrT"])
        proj_fm(s_lr, 0, 32, ev_lr)

        s_qk = next_panel()
        for c in range(4):
            dstT = gqT if c < 2 else gkT
            nm = "gqT" if c < 2 else "gkT"
            sc = 0.125 if c < 2 else 1.0

            def ev(tg, b, dstT=dstT, nm=nm, sc=sc, c=c):
                if (tg + c) % 2 == 0:
                    act(dstT[:, c % 2, tg * 512:(tg + 1) * 512], psb[b][:, :], AF.Copy, [PS(b)], [nm], scale=sc)
                else:
                    ts("dve", dstT[:, c % 2, tg * 512:(tg + 1) * 512], psb[b][:, :], sc, None, ALU.mult, None, [PS(b)], [nm])
            proj_fm(s_qk, c * 128, 128, ev)
        for t in range(16):
            def ev(b, t=t):
                cp("act" if t % 2 else "dve", gk_tm[:, t, :], psb[b][:, 0:256], [PS(b)], ["gk_tm"])
            proj_tm(s_qk, 256, 256, lambda k, t=t: xT[:, k, 256 + t * 128:256 + (t + 1) * 128], ev)

        s_go = next_panel()
        gtmp = av(56 + 4, [512], F32)
        for c in range(4):
            def ev(tg, b, c=c):
                act(gtmp, psb[b][:, :], AF.Silu, [PS(b)], ["gtmp"])
                ts("dve", goutT[:, c, tg * 512:(tg + 1) * 512], gtmp, gT[:, c:c + 1], None, ALU.mult, None,
                   ["gtmp", "gT"], ["goutT"])
            proj_fm(s_go, c * 128, 128, ev)

        s_gv = next_panel()
        for t in range(16):
            def ev(b, t=t):
                cp("dve" if t % 2 else "act", gv[:, t, :], psb[b][:, :], [PS(b)], ["gv"])
            proj_tm(s_gv, 0, 512, lambda k, t=t: xT[:, k, 256 + t * 128:256 + (t + 1) * 128], ev)

        if STOP == "G1":
            P.emit()
            return nc
        sp_t2 = av(40, [2, 512], F32)
        e_pos2 = av(44, [2, 4, 128], F32)
        e_neg2 = av(48, [2, 4, 128], F32)
        e_k2 = av(52, [2, 2, 256], F32)
        def g2_bufs(t):
            g2 = t % 2
            return g2, sp_t2[:, g2, :], e_pos2[:, g2, :, :], e_neg2[:, g2, :, :], e_k2[:, g2, :, :]

        def g2_front(t):
            tok = slice(t * 128, (t + 1) * 128)
            g2, sp_t, e_pos, e_neg, e_k = g2_bufs(t)
            bz = nb()
            mm(psb[bz][:, :], lrT[0:33, tok], wlr[0:33, :], True, True, ["lrT", "lrT1", "wlr"], [PS(bz)])
            act(sp_t, psb[bz][:, :], AF.Exp, [PS(bz)], ["sp_t%d" % g2], scale=-1.0)
            act(sp_t, sp_t, AF.Ln, ["sp_t%d" % g2], ["sp_t%d" % g2], bias=1.0)

        def g2_back(t):
            tok = slice(t * 128, (t + 1) * 128)
            g2, sp_t, e_pos, e_neg, e_k = g2_bufs(t)
            bc_ = nb()
            for q in range(4):
                mm(psb[bc_][:, q * 128:(q + 1) * 128], sp_t[:, q * 128:(q + 1) * 128], c32[:, 0 if q < 2 else 1, :],
                   True, True, ["sp_t%d" % g2, "c32"], [PS(bc_)], signal=(q == 3))
            bk = nb()
            mm(psb[bk][:, 0:256], c32[:, 2, :], sp_t[:, 0:256], True, True, ["sp_t%d" % g2, "c32"], [PS(bk)], signal=False)
            mm(psb[bk][:, 256:512], c32[:, 3, :], sp_t[:, 256:512], True, True, ["sp_t%d" % g2, "c32"], [PS(bk)])
            pc4 = psb[bc_][:, :].rearrange("p (a b) -> p a b", a=4)
            act(e_pos, pc4, AF.Exp, [PS(bc_)], ["e_pos%d" % g2])
            act(e_neg, pc4, AF.Exp, [PS(bc_)], ["e_neg%d" % g2], scale=-1.0)
            act(e_k, psb[bk][:, :].rearrange("p (a b) -> p a b", a=2), AF.Exp, [PS(bk)], ["e_k%d" % g2])
            tt("dve", qfT[:, :, tok], gqT[:, :, tok], e_pos[:, 0:2, :], ALU.mult, ["gqT", "e_pos%d" % g2], ["qfT"])
            tt("dve", kfT[:, :, tok], gkT[:, :, tok], e_neg[:, 0:2, :], ALU.mult, ["gkT", "e_neg%d" % g2], ["kfT"])
            tt("dve", qbT[:, :, tok], gqT[:, :, tok], e_pos[:, 2:4, :], ALU.mult, ["gqT", "e_pos%d" % g2], ["qbT"])
            tt("dve", kbT[:, :, tok], gkT[:, :, tok], e_neg[:, 2:4, :], ALU.mult, ["gkT", "e_neg%d" % g2], ["kbT"])
            cp("dve", dec[:, 0, :, t], e_pos[:, 0:2, 127], ["e_pos%d" % g2], ["dec"])
            cp("dve", dec[:, 1, :, t], e_pos[:, 2:4, 0], ["e_pos%d" % g2], ["dec"])
            tt("dve", kdec[:, t, :, :], e_k, gk_tm[:, t:t + 1, :].to_broadcast([128, 2, 256]), ALU.mult,
               ["e_k%d" % g2, "gk_tm"], ["kdec"])

        g2_front(0)
        for t in range(16):
            if t + 1 < 16:
                g2_front(t + 1)
            g2_back(t)

        if STOP == "G2":
            P.emit()
            return nc
        S16 = av(64, [16, 2, 2, 128], BF16)
        S32 = av(80, [2, 2, 2, 128], F32)
        S16N = ["S16_%d_%d" % (t, d) for t in range(16) for d in range(2)]
        for nm in S16N:
            P.alias(nm, ["gqT", "gkT", "gk_tm"])
        memset("dve", S32[:, 0, 0, :, :], 0.0, ["S32_0_0"])
        memset("dve", S32[:, 0, 1, :, :], 0.0, ["S32_0_1"])
        am = av(84, [2, 8, 128], BF16)
        P.alias("am0", ["gk_tm"])
        P.alias("am1", ["gk_tm"])
        G2T = ["%s%d" % (n, i) for n in ("sp_t", "e_pos", "e_neg", "e_k") for i in range(2)]
        P.alias("y_glaT", G2T)

        def scan_step(i):
            tf = i
            tb = 15 - i
            b = nb()
            kvp = psb[b][:, :].rearrange("p (d j v) -> p d j v", d=2, j=2)
            for d, tsel in ((0, tf), (1, tb)):
                for j in range(2):
                    for hh in range(2):
                        h = 2 * j + hh
                        mm(kvp[hh * 64:(hh + 1) * 64, d, j, :], kdec[:, tsel, d, j * 128 + hh * 64:j * 128 + hh * 64 + 64],
                           gv[:, tsel, h * 128:(h + 1) * 128], True, True, ["kdec", "gv"], [PS(b)],
                           signal=(d == 1 and j == 1 and hh == 1))
            src, dst = i % 2, (i + 1) % 2
            for d, tsel, tout in ((0, tf, tf + 1), (1, tb, tb - 1)):
                for j in range(2):
                    stt("dve", S32[:, dst, d, j, :], S32[:, src, d, j, :], dec[:, d, j, tsel:tsel + 1], kvp[:, d, j, :],
                        ALU.mult, ALU.add, ["S32_%d_%d" % (src, d), "dec", PS(b)], ["S32_%d_%d" % (dst, d)])
                cp("act", S16[:, tout, d, :, :], S32[:, dst, d, :, :], ["S32_%d_%d" % (dst, d)], ["S16_%d_%d" % (tout, d)])

        sq3 = [av(56, [4, 128], BF16), av(57, [4, 128], BF16), av(58, [4, 128], BF16)]
        osb2 = [av(59, [4, 128], F32), av(61, [4, 128], F32)]
        for nm in ("sq0", "sq1", "sq2", "o_sb0", "o_sb1"):
            P.alias(nm, ["lrT", "lrT1", "gtmp"])
        g4bo = {}

        def g4_A(n, t):
            tok = slice(t * 128, (t + 1) * 128)
            ab = n % 2
            bA = [nb(), nb()]
            for hh in range(2):
                rows = slice(hh * 64, (hh + 1) * 64)
                for d, (kk, qq, knm, qnm) in enumerate(((kfT, qfT, "kfT", "qfT"), (kbT, qbT, "kbT", "qbT"))):
                    for j in range(2):
                        mm(psb[bA[hh]][:, (d * 2 + j) * 128:(d * 2 + j + 1) * 128], kk[rows, j, tok], qq[rows, j, tok],
                           True, True, [knm, qnm], [PS(bA[hh])], signal=(d == 1 and j == 1))
            for hh in range(2):
                tt("dve", am[:, ab, hh * 4:(hh + 1) * 4, :].rearrange("p (d j) c -> p d j c", d=2),
                   psb[bA[hh]][:, :].rearrange("p (d j c) -> p d j c", d=2, j=2),
                   c16[:, 0:256].rearrange("p (d c) -> p d c", d=2).unsqueeze(2).to_broadcast([128, 2, 2, 128]),
                   ALU.mult, [PS(bA[hh]), "c16"], ["am%d" % ab])

        def g4_o(n, t):
            tok = slice(t * 128, (t + 1) * 128)
            ab = n % 2
            bo = nb()
            g4bo[n] = bo
            for h in range(4):
                j, hh = h // 2, h % 2
                rows = slice(hh * 64, (hh + 1) * 64)
                outp = psb[bo][:, h * 128:(h + 1) * 128]
                parts = [(gv[:, t, h * 128:(h + 1) * 128], am[:, ab, hh * 4 + j, :], ["gv", "am%d" % ab]),
                         (gv[:, t, h * 128:(h + 1) * 128], am[:, ab, hh * 4 + 2 + j, :], ["gv", "am%d" % ab])]
                if t > 0:
                    parts.append((S16[rows, t, 0, j, :], qfT[rows, j, tok], ["S16_%d_0" % t, "qfT"]))
                if t < 15:
                    parts.append((S16[rows, t, 1, j, :], qbT[rows, j, tok], ["S16_%d_1" % t, "qbT"]))
                for pi, (l, r, rd) in enumerate(parts):
                    mm(outp, l, r, pi == 0, pi == len(parts) - 1, rd, [PS(bo)],
                       signal=(h == 3 and pi == len(parts) - 1))
            po4 = psb[bo][:, :].rearrange("p (h c) -> p h c", h=4)
            act(sq3[n % 3], po4, AF.Square, [PS(bo)], ["sq%d" % (n % 3)])
            cp("act", osb2[n % 2], po4, [PS(bo)], ["o_sb%d" % (n % 2)])

        def g4_fin(n, t):
            tok = slice(t * 128, (t + 1) * 128)
            sq, o_sb = sq3[n % 3], osb2[n % 2]
            bs = nb()
            mm(psb[bs][:, :], ones, sq.rearrange("p h c -> p (h c)"), True, True, ["sq%d" % (n % 3), "c16"], [PS(bs)])
            ps4 = psb[bs][:, :].rearrange("p (h c) -> p h c", h=4)
            act(ps4, ps4, AF.Ln, [PS(bs), "epst"], [PS(bs)], bias=epst[:, 1:2], scale=1.0 / 128.0)
            act(ps4, ps4, AF.Exp, [PS(bs)], [PS(bs)], scale=-0.5)
            tt("dve", o_sb, ps4, o_sb, ALU.mult, ["o_sb%d" % (n % 2), PS(bs)], ["o_sb%d" % (n % 2)])
            tt("dve", y_glaT[:, :, tok], o_sb, goutT[:, :, tok], ALU.mult, ["o_sb%d" % (n % 2), "goutT"], ["y_glaT"])

        for i in range(7):
            scan_step(i)
        T = []
        for i in range(7, 15):
            T += [14 - i, i + 1]
        for n in range(16 + 2):
            if n < 16 and n % 2 == 0:
                scan_step(7 + n // 2)
            if n < 16:
                g4_A(n, T[n])
            if 0 <= n - 1 < 16:
                g4_o(n - 1, T[n - 1])
            if 0 <= n - 2 < 16:
                g4_fin(n - 2, T[n - 2])

        if STOP == "G4":
            P.emit()
            return nc
        S16N_ = ["S16_%d_%d" % (t, d) for t in range(16) for d in range(2)]
        bias_st = av(64, [12, 2, 128], F32)
        mask_st = av(76, [3, 2, 128], F32)
        P.alias("bias_st", S16N_)
        P.alias("mask_st", S16N_)
        P.dma("sp", bias_st, bias_d, writes=["bias_st"], chan="bias")
        P.dma("sp", mask_st, mask_d, writes=["mask_st"], chan="mask")
        act(bias_st, bias_st, AF.Exp, ["bias_st"], ["bias_st"])
        for g in range(3):
            tt("dve", expb[:, 4 * g:4 * g + 4, :, :], bias_st[:, 4 * g:4 * g + 4, :, :],
               mask_st[:, g:g + 1, :, :].to_broadcast([128, 4, 2, 128]), ALU.mult,
               ["bias_st", "mask_st"], ["expb"])
        acc = av(64, [2, 2, S], F32)
        qT2 = [av(96, [2, S], BF16), av(140, [2, S], BF16)]
        kT2 = [av(104, [2, 2560], BF16), av(148, [2, 2560], BF16)]
        Vt2 = [av(114, [20, 256], BF16), av(158, [20, 256], BF16)]
        et_raw = av(124, [4, 8, 128], BF16)
        et = av(132, [4, 8, 128], BF16)
        ETN = ["et%d" % i for i in range(4)] + ["et_raw%d" % i for i in range(4)]
        QKV = ["qT0", "qT1", "kT0", "kT1", "kTpad0", "kTpad1", "Vt0", "Vt1"]
        P.alias("acc", S16N + ["S32_0_0", "S32_0_1", "S32_1_0", "S32_1_1", "am0", "am1", "qfT", "kfT", "bias_st", "mask_st"])
        P.alias("qT0", ["kfT", "qbT"])
        P.alias("kT0", ["qbT", "kbT", "kdec"])
        P.alias("kTpad0", ["qbT", "kbT", "kdec"])
        P.alias("Vt0", ["kbT", "kdec"])
        for nm in ("qT1", "kT1", "kTpad1", "Vt1"):
            P.alias(nm, ["gv", "goutT"])
        for nm in ETN:
            P.alias(nm, ["kdec", "gv"])
        y_attnT = av(56, [2, S], BF16)
        for q4 in range(4):
            P.alias("y_attnT%d" % q4, ["sq0", "sq1", "sq2", "o_sb0", "o_sb1", "gtmp", "lrT", "lrT1"])
        it = [0]

        def proj_fm_part(slot, col, ncols, evac, tgs):
            banks = {tg: nb() for tg in tgs}
            for k in range(8):
                for tg in tgs:
                    mm(psb[banks[tg]][0:ncols, :], wst[slot][:, k, col:col + ncols],
                       xT[:, k, 256 + tg * 512:256 + (tg + 1) * 512], k == 0, k == 7,
                       XTk(k) + ["W%d" % slot], [PS(banks[tg])])
            for tg in tgs:
                evac(tg, banks[tg])

        def group_work(g, r):
            st_ = g % 2
            qT, kT, Vt = qT2[st_], kT2[st_], Vt2[st_]
            qn, kn, kpn, vn = "qT%d" % st_, "kT%d" % st_, "kTpad%d" % st_, "Vt%d" % st_
            L = S // r
            pad = 64 if g < 2 else 0
            Lp = L + 2 * pad
            nq = L // 128
            nk = 2 if g < 2 else 1
            ntile = {0: 17, 1: 5, 2: 1}[g]
            kTv = kT[:, :, 0:r * Lp].rearrange("p c (r m) -> p c r m", r=r)
            qTv = qT[:, :, :].rearrange("p c (r m) -> p c r m", r=r)
            slots = {}
            slices = []
            late = []

            def sl_first():
                slots["qk"] = next_panel()
                if pad:
                    memset("dve", kTv[:, :, :, 0:pad], 0.0, [kpn])
                    memset("dve", kTv[:, :, :, pad + L:Lp], 0.0, [kpn])
            slices.append(sl_first)
            for c in range(2):
                def evq(tg, b, c=c):
                    src = psb[b][:, :].rearrange("p (m r) -> p r m", r=r)
                    dst = qTv[:, c, :, tg * 512 // r:(tg + 1) * 512 // r]
                    cp("act" if tg % 2 else "dve", dst, src, [PS(b)], [qn])

                def evk(tg, b, c=c):
                    src = psb[b][:, :].rearrange("p (m r) -> p r m", r=r)
                    dst = kTv[:, c, :, pad + tg * 512 // r:pad + (tg + 1) * 512 // r]
                    cp("dve" if tg % 2 else "act", dst, src, [PS(b)], [kn])
                dest = late if (g == 2 and c == 1) else slices
                for tgs in ((0, 1), (2, 3)):
                    dest.append(lambda c=c, evq=evq, tgs=tgs: proj_fm_part(slots["qk"], c * 128, 128, evq, tgs))
                    dest.append(lambda c=c, evk=evk, tgs=tgs: proj_fm_part(slots["qk"], 256 + c * 128, 128, evk, tgs))

            def sl_v0():
                slots["v"] = next_panel(hold=(1 if g == 2 else 0))
            slices.append(sl_v0)
            vts = [(cls, i) for cls in range(r) for i in range(ntile)]

            def v_tiles(lst):
                for (cls, i) in lst:
                    start = 256 + r * (128 * i - pad) + cls
                    vi = cls * ntile + i

                    def evv(b, vi=vi):
                        cp("act" if vi % 2 else "dve", Vt[:, vi, :], psb[b][:, 0:256], [PS(b)], [vn])
                    proj_tm(slots["v"], 0, 256,
                            lambda k, start=start: xT[:, k, start:start + 127 * r + 1:r], evv)
            for p0 in range(0, len(vts), 2):
                slices.append(lambda lst=vts[p0:p0 + 2]: v_tiles(lst))

            pairs = []

            U = 2 if g < 2 else 4

            def do_S(c, pair, eb):
                bS = [nb(), nb()]
                for hh in range(2):
                    rows = slice(hh * 64, (hh + 1) * 64)
                    spv = psb[bS[hh]][:, 0:U * nk * 128].rearrange("p (u k q) -> p u k q", u=U, k=nk)
                    for ui, (cls, qi) in enumerate(pair):
                        for kt in range(nk):
                            mm(spv[:, ui, kt, :], kTv[rows, c, cls, 128 * (qi + kt):128 * (qi + kt) + 128],
                               qTv[rows, c, cls, 128 * qi:128 * qi + 128], True, True, [kn, kpn, qn], [PS(bS[hh])],
                               signal=(ui == U - 1 and kt == nk - 1))
                for hh in range(2):
                    spv = psb[bS[hh]][:, 0:U * nk * 128].rearrange("p (u k q) -> p u k q", u=U, k=nk)
                    er = et_raw[:, eb, hh * 4:(hh + 1) * 4, :].rearrange("p (u k) q -> p u k q", u=U)
                    ee = et[:, eb, hh * 4:(hh + 1) * 4, :].rearrange("p (u k) q -> p u k q", u=U)
                    act(er, spv, AF.Exp, [PS(bS[hh])], ["et_raw%d" % eb], scale=0.125)
                    tt("dve", ee, er,
                       expb[:, 4 * g + 2 * c + hh, 0:nk, :].unsqueeze(1).to_broadcast([128, U, nk, 128]), ALU.mult,
                       ["et_raw%d" % eb, "expb"], ["et%d" % eb])

            def do_PV(c, pair, eb):
                accv = acc[:, c, :, :].rearrange("p n (m r) -> p n m r", r=r)
                bos = {}
                for ui, (cls, qi) in enumerate(pair):
                    if ui % 2 == 0:
                        bos[ui // 2] = nb()
                    bo = bos[ui // 2]
                    pv = psb[bo][:, (ui % 2) * 256:(ui % 2) * 256 + 256].rearrange("p (n q) -> p n q", n=2)
                    for hh in range(2):
                        rows = slice(hh * 64, (hh + 1) * 64)
                        hcol = (2 * c + hh) * 64
                        for n in range(2):
                            for kt in range(nk):
                                vi = cls * ntile + qi + kt
                                if n == 0:
                                    l = Vt[:, vi, hcol:hcol + 64]
                                elif g < 2 and qi + kt == 0:
                                    l = onesF
                                elif g < 2 and qi + kt == ntile - 1:
                                    l = onesL
                                else:
                                    l = ones[:, 0:64]
                                mm(pv[rows, n, :], l, et[:, eb, hh * 4 + ui * nk + kt, :], kt == 0, kt == nk - 1,
                                   [vn, "c16", "et%d" % eb], [PS(bo)],
                                   signal=(hh == 1 and n == 1 and kt == nk - 1))
                for ui, (cls, qi) in enumerate(pair):
                    bo = bos[ui // 2]
                    pv = psb[bo][:, (ui % 2) * 256:(ui % 2) * 256 + 256].rearrange("p (n q) -> p n q", n=2)
                    dst = accv[:, :, 128 * qi:128 * qi + 128, cls]
                    if g == 0:
                        cp("act", dst, pv, [PS(bo)], ["acc"])
                    else:
                        tt("dve", dst, pv, dst, ALU.add, [PS(bo), "acc"], ["acc"])

            units = [(cls, qi) for cls in range(r) for qi in range(nq)]
            for c in range(2):
                for up in range(0, len(units), U):
                    pr = units[up:up + U]
                    pairs.append((lambda eb, c=c, pr=pr: do_S(c, pr, eb), lambda eb, c=c, pr=pr: do_PV(c, pr, eb)))
            return slices, pairs, late

        work = [group_work(g, r) for g, r in enumerate((1, 4, 16))]
        for f in work[0][0]:
            f()
        wbr = av(144, [6, D], BF16)
        vec_ln1g = av(160, [D], F32)
        vec_ln1b = av(164, [D], F32)
        vec_ln2g = av(168, [D], F32)
        for g in range(3):
            pairs = work[g][1]
            nxt = work[g + 1][0] if g + 1 < 3 else work[g][2]
            if g == 2:
                for nm in ("wbr", "vecs"):
                    P.alias(nm, ["qT1", "kT1", "kTpad1", "Vt1", "gv", "goutT"])
                P.dma("pool", wbr[:, 0:2, :], wab_d.rearrange("(k p) c -> p k c", p=128), writes=["wbr"], chan="wbr")
                P.dma("pool", wbr[:, 2:6, :], wgb_d.rearrange("(k p) c -> p k c", p=128), writes=["wbr"], chan="wbr")
                P.dma("sp", vec_ln1g, vec_d[:, 0, :], writes=["vecs"], chan="vecs")
                P.dma("sp", vec_ln1b, vec_d[:, 1, :], writes=["vecs"], chan="vecs")
                P.dma("sp", vec_ln2g, vec_d[:, 2, :], writes=["vecs"], chan="vecs")
            done = 0
            ebs = [(it[0] + i) % 4 for i in range(len(pairs))]
            it[0] += len(pairs)
            issued = 0
            for pi in range(len(pairs)):
                want = (pi + 1) * len(nxt) // len(pairs) if g < 2 else min(len(nxt), 2 * (pi + 1))
                has_fill = done < want
                ahead = 1 if (g < 2 or has_fill) else 2
                while issued <= min(pi + ahead, len(pairs) - 1):
                    pairs[issued][0](ebs[issued])
                    issued += 1
                while done < want:
                    nxt[done]()
                    done += 1
                pairs[pi][1](ebs[pi])

        if STOP == "A0":
            P.emit()
            return nc
        rden = av(96, [2, S], F32)
        for q4 in range(4):
            P.alias("rden%d" % q4, ["kfT", "qbT", "kbT", "qT0", "kT0", "kTpad0"])
        for q4 in range(4):
            qs = slice(q4 * 512, (q4 + 1) * 512)
            act(rden[:, :, qs], acc[:, :, 1, qs], AF.Ln, ["acc"], ["rden%d" % q4])
            act(rden[:, :, qs], rden[:, :, qs], AF.Exp, ["rden%d" % q4], ["rden%d" % q4], scale=-1.0)
            tt("dve", y_attnT[:, :, qs], acc[:, :, 0, qs], rden[:, :, qs], ALU.mult, ["acc", "rden%d" % q4], ["y_attnT%d" % q4])

        if debug:
            dtmp = av(112, [4, S], F32)
            P.alias("dtmp", ["gv", "kdec", "rden0", "rden1", "rden2", "rden3"] + ETN + QKV)
            cp("dve", dtmp[:, 0:2, :], y_attnT, ["y_attnT0", "y_attnT1", "y_attnT2", "y_attnT3"], ["dtmp"])
            P.dma("sp", dbg_d["d_yattn"], dtmp[:, 0:2, :], reads=["dtmp"], chan="dbg0")
            cp("dve", dtmp, y_glaT, ["y_glaT"], ["dtmp"])
            P.dma("sp", dbg_d["d_ygla"], dtmp, reads=["dtmp"], chan="dbg1")

        if STOP == "A":
            P.emit()
            return nc
        mergedT = av(64, [8, S], BF16)
        mtmp = av(128, [2, 2, 512], F32)
        sgt = av(136, [2, 2, 512], F32)
        for q4 in range(4):
            P.alias("mergedT%d" % q4, ["acc"])
        for nm in ("mtmp0", "mtmp1", "sgt0", "sgt1"):
            P.alias(nm, ["dtmp", "gv", "kdec"] + QKV + ETN)
        wout = av(96, [8, D], BF16)
        X1Bv = [av(112, [4, D], F32), av(144, [4, D], F32)]
        x1T = av(128, [2, 8, 512], BF16)
        vec_ln2b = av(58, [D], F32)
        hT = av(0, [32, 512], BF16)
        b2v = av(32, [D], F32)
        x1h = av(36, [2, D], BF16)
        rtmp4 = av(40, [2, 512], BF16)
        rtmp4b = av(62, [2, 512], BF16)
        rtmp = [rtmp4[:, 0, :], rtmp4[:, 1, :], rtmp4b[:, 0, :], rtmp4b[:, 1, :]]
        u2 = av(42, [4, D], F32)
        vecs = {0: (vec_ln1g, vec_ln1b), 2: (vec_ln2g, vec_ln2b)}
        P.alias("wout", QKV + ["rden0", "rden1", "rden2", "rden3"])
        X1B = ["x1b%d_%d" % (bf, i) for bf in range(2) for i in range(4)]
        for nm in X1B[0:4]:
            P.alias(nm, ["kdec", "dtmp"] + ETN + QKV)
        P.dma("pool", wout, wout_d.rearrange("(k p) c -> p k c", p=128), writes=["wout"], chan="wout")
        x_tiles = x_d.rearrange("(t p) d -> t p d", p=128)
        out_tiles = out_d.rearrange("(t p) d -> t p d", p=128)

        def layernorm(u_ap, unm, slot, gi, out_ap, onm):
            vg, vb = vecs[gi]
            for hf in range(2):
                P.op("dve", lambda e, hf=hf: e.bn_stats(stats[:, slot, hf, :], u_ap[:, hf * 512:(hf + 1) * 512]),
                     [unm], ["stats%d" % slot])
            P.op("dve", lambda e: e.bn_aggr(mv[:, slot, 0:2], stats[:, slot, :, :]), ["stats%d" % slot], ["mv%d" % slot])
            act(mv[:, slot, 2:3], mv[:, slot, 1:2], AF.Sqrt, ["mv%d" % slot, "epst"], ["mv%d" % slot], bias=epst[:, 0:1])
            P.op("dve", lambda e: e.reciprocal(mv[:, slot, 2:3], mv[:, slot, 2:3]), ["mv%d" % slot], ["mv%d" % slot])
            stt("dve", mv[:, slot, 3:4], mv[:, slot, 0:1], -1.0, mv[:, slot, 2:3], ALU.mult, ALU.mult,
                ["mv%d" % slot], ["mv%d" % slot])
            act(out_ap, u_ap, AF.Identity, [unm, "mv%d" % slot], [onm], bias=mv[:, slot, 3:4], scale=mv[:, slot, 2:3])
            tt("dve", out_ap, out_ap, vg, ALU.mult, [onm, "vecs", "vecs2"], [onm])
            tt("dve", out_ap, out_ap, vb, ALU.add, [onm, "vecs", "vecs2"], [onm])

        def F1a(tg, tiles, cast=True):
            bf = tg % 2
            for tt_ in tiles:
                t = tg * 4 + tt_
                xs = t % 2
                xnm = "x1b%d_%d" % (bf, tt_)
                xt = X1Bv[bf][:, tt_, :]
                P.dma("sp", xt, x_tiles[t], writes=[xnm], chan=xnm)
                bks = [nb(), nb()]
                for hf in range(2):
                    for j in range(8):
                        mm(psb[bks[hf]][:, :], mergedT[:, j, t * 128:(t + 1) * 128], wout[:, j, hf * 512:(hf + 1) * 512],
                           j == 0, j == 7, ["mergedT%d" % tg, "wout"], [PS(bks[hf])])
                for hf in range(2):
                    stt("dve", xt[:, hf * 512:(hf + 1) * 512], xt[:, hf * 512:(hf + 1) * 512], ALPHA,
                        psb[bks[hf]][:, :], ALU.mult, ALU.add, [xnm, PS(bks[hf])], [xnm])
                layernorm(xt, xnm, xs, 0, xt, xnm)
            if cast:
                F1a_cast(tg, tiles)

        def F1a_cast(tg, tiles):
            bf = tg % 2
            for tt_ in tiles:
                xs = (tg * 4 + tt_) % 2
                xnm = "x1b%d_%d" % (bf, tt_)
                xt = X1Bv[bf][:, tt_, :]
                cp("act", x1h[:, xs, :], xt, [xnm], ["x1h%d" % xs])
                stt("dve", xt, xt, ALPHA, b2v, ALU.mult, ALU.add, [xnm, "b2v"], [xnm])

        def F1b(tg, tiles):
            bf = tg % 2
            for tt_ in tiles:
                xs = (tg * 4 + tt_) % 2
                bt = nb()
                ptb = psb[bt][:, :].bitcast(BF16)
                for k in range(8):
                    P.op("pe", lambda e, k=k, xs=xs, ptb=ptb: e.transpose(ptb[:, k * 128:(k + 1) * 128],
                                                                         x1h[:, xs, k * 128:(k + 1) * 128], ident),
                         ["x1h%d" % xs, "c16"], [PS(bt)], signal=(k == 7))
                cp("act", x1T[:, bf, :, tt_ * 128:(tt_ + 1) * 128], ptb.rearrange("p (k c) -> p k c", k=8), [PS(bt)],
                   ["x1T%d" % bf])

        def F2(tg):
            bf = tg % 2
            for u in range(8):
                s1 = next_panel()
                for f4 in range(4):
                    fc = u * 4 + f4
                    b = nb()
                    for k in range(8):
                        mm(psb[b][:, :], wst[s1][:, k, f4 * 128:(f4 + 1) * 128], x1T[:, bf, k, :], k == 0, k == 7,
                           ["W%d" % s1, "x1T%d" % bf], [PS(b)])
                    rb = fc % 4
                    act(rtmp[rb], psb[b][:, :], AF.Relu, [PS(b), "b1T"], ["rtmp%d" % rb], bias=b1T[:, fc:fc + 1])
                    tt("dve", hT[:, fc, :], rtmp[rb], rtmp[rb], ALU.mult, ["rtmp%d" % rb], ["hT"])
                if tg > 0 and 2 <= u <= 5:
                    LN2(tg - 1, [u - 2])

        def F3h(tg, hf, before_last=None, tile_sets=((0, 1, 2, 3),), after_set=None):
            bf = tg % 2
            for si, tset in enumerate(tile_sets):
                banks = {tt_: nb() for tt_ in tset}
                for u in range(4):
                    if u == 3 and before_last is not None and si == len(tile_sets) - 1:
                        before_last()
                    s2 = next_panel()
                    for tt_ in tset:
                        for f8 in range(8):
                            fc = u * 8 + f8
                            mm(psb[banks[tt_]][:, :], hT[:, fc, tt_ * 128:(tt_ + 1) * 128], wst[s2][:, f8, :],
                               fc == 0, fc == 31, ["hT", "W%d" % s2], [PS(banks[tt_])])
                for tt_ in tset:
                    tt("dve", u2[:, tt_, hf * 512:(hf + 1) * 512], psb[banks[tt_]][:, :],
                       X1Bv[bf][:, tt_, hf * 512:(hf + 1) * 512], ALU.add,
                       [PS(banks[tt_]), "x1b%d_%d" % (bf, tt_)], ["u2_%d" % tt_])
                if after_set is not None:
                    after_set(tset)

        def LN2(tg, tiles):
            for tt_ in tiles:
                t = tg * 4 + tt_
                layernorm(u2[:, tt_, :], "u2_%d" % tt_, 2 + tt_ % 2, 2, u2[:, tt_, :], "u2_%d" % tt_)
                P.dma("sp", out_tiles[t], u2[:, tt_, :], reads=["u2_%d" % tt_], chan="ot%d" % tt_)


        mit = [0]
        for u in range(2):
            s_ga = next_panel()
            s_gg = next_panel(hold=1)
            order = [(jj, tg) for jj in range(4) for tg in range(4)] if u == 0 else \
                    [(jj, tg) for tg in range(4) for jj in range(4)]
            for (jj, tg) in order:
                if True:
                    j = u * 4 + jj
                    mb = mit[0] % 2
                    mit[0] += 1
                    tks = slice(tg * 512, (tg + 1) * 512)
                    b_ga, b_gg, b_za, b_zg = nb(), nb(), nb(), nb()
                    for (bq, sl) in ((b_ga, s_ga), (b_gg, s_gg)):
                        for k in range(8):
                            mm(psb[bq][:, :], wst[sl][:, k, jj * 128:(jj + 1) * 128],
                               xT[:, k, 256 + tg * 512:256 + (tg + 1) * 512], k == 0, k == 7,
                               XTk(k) + ["W%d" % sl], [PS(bq)])
                    for k in range(2):
                        mm(psb[b_za][:, :], wbr[:, k, j * 128:(j + 1) * 128], y_attnT[:, k, tks], k == 0, k == 1,
                           ["wbr", "y_attnT%d" % tg], [PS(b_za)])
                    for k in range(4):
                        mm(psb[b_zg][:, :], wbr[:, 2 + k, j * 128:(j + 1) * 128], y_glaT[:, k, tks], k == 0, k == 3,
                           ["wbr", "y_glaT"], [PS(b_zg)])
                    act(sgt[:, mb, 0, :], psb[b_ga][:, :], AF.Sigmoid, [PS(b_ga)], ["sgt%d" % mb])
                    act(sgt[:, mb, 1, :], psb[b_gg][:, :], AF.Sigmoid, [PS(b_gg)], ["sgt%d" % mb])
                    tt("dve", mtmp[:, mb, 0, :], psb[b_za][:, :], sgt[:, mb, 0, :], ALU.mult, [PS(b_za), "sgt%d" % mb], ["mtmp%d" % mb])
                    tt("dve", mtmp[:, mb, 1, :], psb[b_zg][:, :], sgt[:, mb, 1, :], ALU.mult, [PS(b_zg), "sgt%d" % mb], ["mtmp%d" % mb])
                    tt("dve", mergedT[:, j, tks], mtmp[:, mb, 0, :], mtmp[:, mb, 1, :], ALU.add, ["mtmp%d" % mb], ["mergedT%d" % tg])
                    if u == 1 and not debug and (tg, jj) in ((1, 0), (1, 3), (2, 2), (3, 1)):
                        F1a(0, ({(1, 0): 0, (1, 3): 1, (2, 2): 2, (3, 1): 3}[(tg, jj)],), cast=False)

        if debug:
            dtm = av(0, [8, S], F32)
            P.alias("dtm", XT + ["y_glaT"] + ["y_attnT0", "y_attnT1", "y_attnT2", "y_attnT3"])
            cp("dve", dtm, mergedT, ["mergedT0", "mergedT1", "mergedT2", "mergedT3"], ["dtm"])
            P.dma("sp", dbg_d["d_merged"], dtm, reads=["dtm"], chan="dbg2")

        if STOP == "M":
            P.emit()
            return nc
        for nm in X1B[4:8]:
            P.alias(nm, ["wbr", "gv", "goutT", "dtmp"] + QKV)
        for nm in ("x1T0", "x1T1"):
            P.alias(nm, ["mtmp0", "mtmp1", "sgt0", "sgt1", "gv", "kdec", "dtmp"] + ETN + QKV)
        for nm in ("hT", "b2v", "x1h0", "x1h1", "rtmp0", "rtmp1", "rtmp2", "rtmp3", "u2_0", "u2_1", "u2_2", "u2_3", "vecs2"):
            P.alias(nm, XT + ["y_glaT", "dtm"] + ["y_attnT0", "y_attnT1", "y_attnT2", "y_attnT3"])
        P.dma("sp", vec_ln2b, vec_d[:, 3, :], writes=["vecs2"], chan="vecs2")
        P.dma("sp", b2v, vec_d[:, 4, :], writes=["b2v"], chan="b2v")
        if debug:
            F1a(0, (0, 1, 2, 3), cast=False)
        for tl in ((0, 1), (2, 3)):
            F1a_cast(0, tl)
            F1b(0, tl)
        for tg in range(4):
            F2(tg)
            nxt = tg + 1 < 4
            if nxt:
                F1a(tg + 1, (0, 1))
            F3h(tg, 0)
            if nxt:
                F1b(tg + 1, (0, 1))
                F1a(tg + 1, (2, 3))
            if nxt:
                F3h(tg, 1, before_last=lambda tg=tg: F1b(tg + 1, (2, 3)))
            else:
                F3h(tg, 1, tile_sets=((0, 1), (2, 3)), after_set=lambda tset: LN2(3, list(tset)))

        P.emit()
    return nc


_NC_CACHE = {}


def _prep_shared(w_in, rel_bias, w_lr_fwd, b_lr_fwd, w_lr_bwd, b_lr_bwd, gla_norm_g, w_attn_branch,
                 w_gla_branch, w_out, ln1_g, ln1_b, w_ff1, b_ff1, w_ff2, b_ff2, ln2_g, ln2_b):
    f = lambda a: np.ascontiguousarray(np.asarray(a, dtype=np.float32))
    rel_bias = f(rel_bias)
    buckets, masks = _host_tables()
    bias = np.zeros((256, 12, 128), np.float32)
    for g in range(3):
        for hs in range(4):
            bias[:, 4 * g + hs, :] = rel_bias[buckets[g], 4 * g + hs]
    biasT = np.ascontiguousarray(bias.reshape(2, 128, 12, 128).transpose(1, 2, 0, 3))
    maskT = np.ascontiguousarray(np.stack(masks, 1).reshape(2, 128, 3, 128).transpose(1, 2, 0, 3))
    wlr = np.zeros((33, 512), np.float32)
    wlr[0:16, 0:256] = f(w_lr_fwd)[0]
    wlr[16:32, 256:512] = f(w_lr_bwd)[0]
    wlr[32, 0:256] = f(b_lr_fwd)[0]
    wlr[32, 256:512] = f(b_lr_bwd)[0]
    c32, c16 = _consts()
    vecs = np.stack([f(ln1_g)[0], f(ln1_b)[0], f(ln2_g)[0], f(ln2_b)[0], f(b_ff2)[0]], 0)
    vecs = np.ascontiguousarray(np.broadcast_to(vecs[None], (128, 5, D)))
    return {
        "w_in": f(w_in)[0], "wlr": wlr, "c32": c32, "c16": c16, "biasT": biasT, "maskT": maskT,
        "w_ab": f(w_attn_branch)[0], "w_gb": f(w_gla_branch)[0], "w_out": f(w_out)[0],
        "w_ff1": f(w_ff1)[0], "b1T": np.ascontiguousarray(f(b_ff1)[0].reshape(32, 128).T),
        "w_ff2": f(w_ff2)[0], "gT": np.ascontiguousarray(f(gla_norm_g)[0].reshape(4, 128).T),
        "vecs": vecs,
    }


def kernel(x, **params):
    debug = bool(os.environ.get("MK_DEBUG"))
    x = np.asarray(x, dtype=np.float32)
    B = x.shape[0]
    shared = _prep_shared(**params)
    key = debug
    if key not in _NC_CACHE:
        _NC_CACHE[key] = build_nc(debug)
    nc = _NC_CACHE[key]
    in_maps = []
    for b in range(B):
        m = dict(shared)
        m["x"] = np.ascontiguousarray(x[b])
        m["xT"] = np.ascontiguousarray(x[b].T)
        in_maps.append(m)
    res = run_bass_kernel_spmd(nc, in_maps, core_ids=list(range(B)))
    out = np.stack([np.asarray(r["out"], dtype=np.float32) for r in res.results], 0)
    if debug:
        kernel.dbg = [{k: np.asarray(v) for k, v in r.items() if k.startswith("d_")} for r in res.results]
    return out
```

```python
import contextlib
import os
import numpy as np
import concourse.bass as bass
import concourse.mybir as mybir
from concourse.bass_utils import run_bass_kernel_spmd

F32 = mybir.dt.float32
BF16 = mybir.dt.bfloat16
AF = mybir.ActivationFunctionType
ALU = mybir.AluOpType

S = 2048
D = 1024
DFF = 4096
NCOL = 5920
ALPHA = 2.0 ** 0.25
LN_EPS = 1e-5
NORM_EPS = 1e-6
NSLOT = 3
SAME_ENGINE_WAR_SYNC = True
KB = 1024

ENGS = ("pe", "act", "dve", "pool", "sp")


class Prog:
    def __init__(self, nc):
        self.nc = nc
        self.ops = []
        self.lastw = {}
        self.readers = {}

    def op(self, eng, fn, reads=(), writes=(), signal=True, chan=None):
        oid = len(self.ops)
        deps = set()
        for r in reads:
            w = self.lastw.get(r)
            if w is not None:
                deps.add((w, "raw"))
        for r in writes:
            w = self.lastw.get(r)
            if w is not None:
                deps.add((w, "waw"))
            for rd in self.readers.get(r, ()):
                deps.add((rd, "war"))
        for r in reads:
            self.readers.setdefault(r, []).append(oid)
        for r in writes:
            self.lastw[r] = oid
            self.readers[r] = []
        self.ops.append(dict(id=oid, eng=eng, fn=fn, deps=deps, signal=signal, chan=chan))
        return oid

    def alias(self, new, olds):
        rs = list(self.readers.get(new, []))
        w = self.lastw.get(new)
        if w is not None:
            rs.append(w)
        for o in olds:
            rs.extend(self.readers.get(o, []))
            w = self.lastw.get(o)
            if w is not None:
                rs.append(w)
        self.readers[new] = rs
        self.lastw.pop(new, None)

    def dma(self, eng, out, in_, reads=(), writes=(), chan=None):
        assert chan is not None
        return self.op(eng, lambda e: e.dma_start(out=out, in_=in_), reads, writes, True, chan)

    def finalize(self):
        ops = self.ops
        cnt = {}
        last_of = {}
        for o in ops:
            if o["chan"] is None:
                last_of[o["eng"]] = o["id"]
        for e, i in last_of.items():
            ops[i]["signal"] = True
        import bisect
        sig_ids = {}
        for o in ops:
            if o["chan"] is None and o["signal"]:
                sig_ids.setdefault(o["eng"], []).append(o["id"])
        for o in ops:
            for (d, kind) in o["deps"]:
                do = ops[d]
                if do["chan"] is not None or do["signal"]:
                    continue
                lst = sig_ids.setdefault(do["eng"], [])
                p = bisect.bisect_left(lst, d)
                if p >= len(lst) or lst[p] >= o["id"]:
                    do["signal"] = True
                    bisect.insort(lst, d)
        pending = {}
        for o in ops:
            if o["chan"] is not None:
                k = "ch_" + o["chan"]
                cnt[k] = cnt.get(k, 0) + 16
                o["tok"] = (k, cnt[k])
            else:
                k = "e_" + o["eng"]
                pending.setdefault(k, []).append(o)
                if o["signal"]:
                    cnt[k] = cnt.get(k, 0) + 1
                    for p in pending[k]:
                        p["tok"] = (k, cnt[k])
                    pending[k] = []
        self.semkeys = sorted(cnt.keys())
        self.cnt = cnt
        chan_hist = {}
        for o in ops:
            if o["chan"] is not None:
                chan_hist.setdefault(o["tok"][0], []).append((o["id"], o["tok"][1]))
        known = {e: {} for e in ENGS}
        snap = {}
        per_eng = {e: [] for e in ENGS}
        for o in ops:
            e = o["eng"]
            kn = known[e]
            need = {}
            for (d, kind) in o["deps"]:
                do = ops[d]
                same = (do["eng"] == e) and do["chan"] is None and o["chan"] is None
                if same and kind != "raw":
                    if e != "pe" and kn.get(do["tok"][0], 0) < do["tok"][1]:
                        self.n_same_war = getattr(self, "n_same_war", 0) + 1
                        if SAME_ENGINE_WAR_SYNC:
                            k, v = do["tok"]
                            if v > need.get(k, (0, None))[0]:
                                need[k] = (v, d)
                    continue
                k, v = do["tok"]
                if v > need.get(k, (0, None))[0]:
                    need[k] = (v, d)
            waits = {}
            for k, (v, d) in need.items():
                if k.startswith("ch_"):
                    lst = chan_hist[k]
                    p = bisect.bisect_left(lst, (o["id"], 0)) - 1
                    v = max(v, lst[p][1])
                if kn.get(k, 0) >= v:
                    continue
                waits[k] = v
                kn[k] = v
                for kk, vv in snap[d].items():
                    if kn.get(kk, 0) < vv:
                        kn[kk] = vv
            o["waits"] = waits
            snap[o["id"]] = dict(kn)
            per_eng[e].append(o)
        return per_eng

    def emit(self):
        nc = self.nc
        per_eng = self.finalize()
        with contextlib.ExitStack() as st:
            sems = {k: st.enter_context(nc.semaphore(k)) for k in self.semkeys}
            block = st.enter_context(nc.Block())

            def run(ename):
                def body(eng):
                    for o in per_eng[ename]:
                        for k, v in o["waits"].items():
                            eng.wait_ge(sems[k], v)
                        ins = o["fn"](eng)
                        if o["chan"] is not None:
                            ins.then_inc(sems[o["tok"][0]], 16)
                        elif o["signal"]:
                            ins.then_inc(sems[o["tok"][0]], 1)
                    if ename == "sp":
                        for k in self.semkeys:
                            if k.startswith("ch_"):
                                eng.wait_ge(sems[k], self.cnt[k])
                return body

            block.sync(run("sp"))
            block.gpsimd(run("pool"))
            block.scalar(run("act"))
            block.vector(run("dve"))
            block.tensor(run("pe"))


def _t5_bucket(rel):
    nb = 16
    max_exact = 8
    ret = (rel > 0).astype(np.int32) * nb
    n = np.abs(rel)
    large = max_exact + (np.log(np.maximum(n, 1) / max_exact) / np.log(1024 / max_exact) * (nb - max_exact)).astype(np.int32)
    large = np.minimum(large, nb - 1)
    return (ret + np.where(n < max_exact, n, large)).astype(np.int32)


def _host_tables():
    j = np.arange(256)[:, None]
    i = np.arange(128)[None, :]
    buckets, masks = [], []
    for g, dil in enumerate((1, 4, 16)):
        delta = (j - 64 - i) if g < 2 else (j - i)
        m = (np.abs(delta) <= 64)
        if g == 2:
            m = m & (j < 128)
        buckets.append(_t5_bucket(delta * dil))
        masks.append(m.astype(np.float32))
    return buckets, masks


def _consts():
    s = np.arange(128)[:, None]
    c = np.arange(128)[None, :]
    v = -1.0 / 16.0
    c32 = np.stack([
        np.where(s <= c, v, 0.0), np.where(s >= c, v, 0.0),
        np.where(s > c, v, 0.0), np.where(s < c, v, 0.0)], 0).astype(np.float32)
    c32 = np.ascontiguousarray(c32.transpose(1, 0, 2))
    maskF = (s <= c).astype(np.float32)
    maskB = (s > c).astype(np.float32)
    ident = np.eye(128, dtype=np.float32)
    ones = np.ones((128, 128), np.float32)
    onesF = np.ones((128, 64), np.float32); onesF[0:64] = 0
    onesL = np.ones((128, 64), np.float32); onesL[64:128] = 0
    c16 = np.concatenate([maskF, maskB, ident, ones, onesF, onesL], 1)
    return c32, np.ascontiguousarray(c16)


def build_nc(debug=False):
    STOP = os.environ.get("MK_STOP", "")
    nc = bass.Bass("TRN2", target_bir_lowering=False)

    def din(name, shape):
        return nc.dram_tensor(name, list(shape), F32, kind="ExternalInput").ap()

    xT_d = din("xT", [D, S])
    x_d = din("x", [S, D])
    win_d = din("w_in", [D, NCOL])
    wlr_d = din("wlr", [33, 512])
    c32_d = din("c32", [128, 4, 128])
    c16_d = din("c16", [128, 640])
    bias_d = din("biasT", [128, 12, 2, 128])
    mask_d = din("maskT", [128, 3, 2, 128])
    wab_d = din("w_ab", [256, D])
    wgb_d = din("w_gb", [512, D])
    wout_d = din("w_out", [D, D])
    w1_d = din("w_ff1", [D, DFF])
    b1_d = din("b1T", [128, 32])
    w2_d = din("w_ff2", [DFF, D])
    gT_d = din("gT", [128, 4])
    vec_d = din("vecs", [128, 5, D])
    out_d = nc.dram_tensor("out", [S, D], F32, kind="ExternalOutput").ap()
    dbg_d = {}
    if debug:
        for nm, shp in (("d_yattn", [128, 2, S]), ("d_ygla", [128, 4, S]), ("d_merged", [128, 8, S])):
            dbg_d[nm] = nc.dram_tensor(nm, shp, F32, kind="ExternalOutput").ap()

    with contextlib.ExitStack() as st:
        def sb(name, shape, dt):
            return st.enter_context(nc.sbuf_tensor("sb_" + name, list(shape), dt))

        ARENA_KB = 172
        arena = sb("arena", [128, ARENA_KB * KB // 2], BF16)

        def av(off_kb, shape, dt):
            n = int(np.prod(shape))
            nb = n * (4 if dt == F32 else 2)
            o = int(off_kb * KB)
            assert o + nb <= ARENA_KB * KB, (off_kb, shape)
            ap = arena[:, o // 2:(o + nb) // 2]
            if dt == F32:
                ap = ap.bitcast(F32)
            if len(shape) == 2:
                ap = ap.rearrange("p (a b) -> p a b", a=shape[0])
            elif len(shape) == 3:
                ap = ap.rearrange("p (a b c) -> p a b c", a=shape[0], b=shape[1])
            elif len(shape) == 4:
                ap = ap.rearrange("p (a b c d) -> p a b c d", a=shape[0], b=shape[1], c=shape[2])
            return ap

        wst = [sb("wst%d" % i, [128, 8, 512], BF16) for i in range(NSLOT)]
        c32 = sb("c32", [128, 4, 128], F32)
        c16 = sb("c16", [128, 640], BF16)
        expb = sb("expb", [128, 12, 2, 128], BF16)
        wlr = sb("wlr", [33, 512], BF16)
        b1T = sb("b1T", [128, 32], F32)
        gT = sb("gT", [128, 4], F32)
        dec = sb("dec", [128, 2, 2, 16], F32)
        stats = sb("stats", [128, 4, 2, 6], F32)
        mv = sb("mv", [128, 4, 4], F32)
        epst = sb("epst", [128, 2], F32)
        psb = [st.enter_context(nc.psum_tensor("ps%d" % i, [128, 512], F32)) for i in range(8)]

        maskF = c16[:, 0:128]
        maskB = c16[:, 128:256]
        ident = c16[:, 256:384]
        ones = c16[:, 384:512]
        onesF = c16[:, 512:576]
        onesL = c16[:, 576:640]

        P = Prog(nc)
        bank_ctr = [0]

        def nb():
            b = bank_ctr[0] % 8
            bank_ctr[0] += 1
            return b

        def PS(b):
            return "ps%d" % b

        def mm(out, lhsT, rhs, start, stop, reads, writes, signal=None):
            P.op("pe", lambda e: e.matmul(out, lhsT, rhs, start=start, stop=stop), reads, writes,
                 signal=(stop if signal is None else signal))

        def act(out, in_, func, reads, writes, bias=None, scale=None):
            kw = {}
            if bias is not None:
                kw["bias"] = bias
            if scale is not None:
                kw["scale"] = scale
            P.op("act", lambda e: e.activation(out, in_, func, **kw), reads, writes)

        def tt(eng, out, in0, in1, op, reads, writes):
            P.op(eng, lambda e: e.tensor_tensor(out, in0, in1, op), reads, writes)

        def ts(eng, out, in0, s1, s2, op0, op1, reads, writes):
            if op1 is None:
                P.op(eng, lambda e: e.tensor_scalar(out, in0, s1, None, op0), reads, writes)
            else:
                P.op(eng, lambda e: e.tensor_scalar(out, in0, s1, s2, op0, op1), reads, writes)

        def stt(eng, out, in0, scalar, in1, op0, op1, reads, writes):
            P.op(eng, lambda e: e.scalar_tensor_tensor(out, in0, scalar, in1, op0, op1), reads, writes)

        def cp(eng, out, in_, reads, writes):
            if eng == "act":
                P.op("act", lambda e: e.activation(out, in_, AF.Copy), reads, writes)
            else:
                P.op(eng, lambda e: e.tensor_copy(out, in_), reads, writes)

        def memset(eng, ap, val, writes):
            P.op(eng, lambda e: e.memset(ap, val), (), writes)

        panel_seq = []
        ptr = {"use": 0, "issue": 0}

        def win_panel(c0, n):
            return win_d[:, c0:c0 + n].rearrange("(k p) c -> p k c", p=128)

        def issue_panels(hold=0):
            while ptr["issue"] < min(len(panel_seq), ptr["use"] + NSLOT - hold):
                s = ptr["issue"] % NSLOT
                for (off, n, src) in panel_seq[ptr["issue"]]:
                    P.dma("pool", wst[s][:, :, off:off + n], src, writes=["W%d" % s], chan="W%d" % s)
                ptr["issue"] += 1

        def next_panel(hold=0):
            issue_panels(hold)
            s = ptr["use"] % NSLOT
            ptr["use"] += 1
            return s

        panel_seq.append([(0, 32, win_panel(3840, 32))])
        panel_seq.append([(0, 512, win_panel(2304, 512))])
        panel_seq.append([(0, 512, win_panel(3328, 512))])
        panel_seq.append([(0, 512, win_panel(2816, 512))])
        for g in range(3):
            panel_seq.append([(0, 256, win_panel(g * 256, 256)), (256, 256, win_panel(768 + g * 256, 256))])
            panel_seq.append([(0, 256, win_panel(1536 + g * 256, 256))])
        for u in range(2):
            panel_seq.append([(0, 512, win_panel(3872 + u * 512, 512))])
            panel_seq.append([(0, 512, win_panel(4896 + u * 512, 512))])
        for tg in range(4):
            for u in range(8):
                panel_seq.append([(0, 512, w1_d[:, u * 512:(u + 1) * 512].rearrange("(k p) c -> p k c", p=128))])
            for half in range(2):
                for rep in range(2 if (tg == 3 and half == 1) else 1):
                    for u in range(4):
                        panel_seq.append([(0, 512, w2_d[u * 1024:(u + 1) * 1024, half * 512:(half + 1) * 512]
                                           .rearrange("(f p) c -> p f c", p=128))])

        P.dma("sp", c32[:, :, :], c32_d, writes=["c32"], chan="c32")
        P.dma("sp", b1T[:, :], b1_d, writes=["b1T"], chan="small")
        P.dma("sp", gT[:, :], gT_d, writes=["gT"], chan="gT")
        memset("dve", epst[:, 0:1], LN_EPS, ["epst"])
        memset("dve", epst[:, 1:2], NORM_EPS, ["epst"])

        xT = av(0, [8, 2560], BF16)
        memset("dve", xT[:, :, 0:256], 0.0, ["xTpadL"])
        memset("dve", xT[:, :, 2304:2560], 0.0, ["xTpadR"])
        issue_panels(hold=1)
        xT_src = xT_d.rearrange("(k p) t -> p k t", p=128)
        for k in range(8):
            P.dma("pool", xT[:, k, 256:2304], xT_src[:, k, :], writes=["xT%d" % k], chan="xT%d" % k)
        XT = ["xT%d" % k for k in range(8)] + ["xTpadL", "xTpadR"]

        def XTk(k):
            return ["xT%d" % k, "xTpadL", "xTpadR"]
        P.dma("pool", wlr[:, :], wlr_d, writes=["wlr"], chan="wlr")
        P.dma("pool", c16[:, :], c16_d, writes=["c16"], chan="c16")

        if STOP == "INIT":
            P.emit()
            return nc
        y_glaT = av(40, [4, S], BF16)
        gqT = av(64, [2, S], BF16)
        gkT = av(72, [2, S], BF16)
        gk_tm = av(80, [16, 256], BF16)
        qfT = av(88, [2, S], BF16)
        kfT = av(96, [2, S], BF16)
        qbT = av(104, [2, S], BF16)
        kbT = av(112, [2, S], BF16)
        kdec = av(120, [16, 2, 256], BF16)
        gv = av(136, [16, 512], BF16)
        goutT = av(152, [4, S], BF16)
        lrT = av(56, [S], BF16)

        def proj_fm(slot, col, ncols, evac):
            banks = [nb() for _ in range(4)]
            for k in range(8):
                for tg in range(4):
                    mm(psb[banks[tg]][0:ncols, :], wst[slot][:, k, col:col + ncols],
                       xT[:, k, 256 + tg * 512:256 + (tg + 1) * 512], k == 0, k == 7,
                       XTk(k) + ["W%d" % slot], [PS(banks[tg])])
            for tg in range(4):
                evac(tg, banks[tg])

        def proj_tm(slot, col, ncols, tok_ap_fn, evac):
            b = nb()
            for k in range(8):
                mm(psb[b][:, 0:ncols], tok_ap_fn(k), wst[slot][:, k, col:col + ncols], k == 0, k == 7,
                   XTk(k) + ["W%d" % slot], [PS(b)])
            evac(b)

        s_lr = next_panel()
        memset("dve", lrT[32:33, :], 1.0, ["lrT1"])

        def ev_lr(tg, b):
            cp("act", lrT[0:32, tg * 512:(tg + 1) * 512], psb[b][0:32, :], [PS(b)], ["lrT"])
        proj_fm(s_lr, 0, 32, ev_lr)

        s_qk = next_panel()
        for c in range(4):
            dstT = gqT if c < 2 else gkT
            nm = "gqT" if c < 2 else "gkT"
            sc = 0.125 if c < 2 else 1.0

            def ev(tg, b, dstT=dstT, nm=nm, sc=sc, c=c):
                if (tg + c) % 2 == 0:
                    act(dstT[:, c % 2, tg * 512:(tg + 1) * 512], psb[b][:, :], AF.Copy, [PS(b)], [nm], scale=sc)
                else:
                    ts("dve", dstT[:, c % 2, tg * 512:(tg + 1) * 512], psb[b][:, :], sc, None, ALU.mult, None, [PS(b)], [nm])
            proj_fm(s_qk, c * 128, 128, ev)
        for t in range(16):
            def ev(b, t=t):
                cp("act" if t % 2 else "dve", gk_tm[:, t, :], psb[b][:, 0:256], [PS(b)], ["gk_tm"])
            proj_tm(s_qk, 256, 256, lambda k, t=t: xT[:, k, 256 + t * 128:256 + (t + 1) * 128], ev)

        s_go = next_panel()
        gtmp = av(56 + 4, [512], F32)
        for c in range(4):
            def ev(tg, b, c=c):
                act(gtmp, psb[b][:, :], AF.Silu, [PS(b)], ["gtmp"])
                ts("dve", goutT[:, c, tg * 512:(tg + 1) * 512], gtmp, gT[:, c:c + 1], None, ALU.mult, None,
                   ["gtmp", "gT"], ["goutT"])
            proj_fm(s_go, c * 128, 128, ev)

        s_gv = next_panel()
        for t in range(16):
            def ev(b, t=t):
                cp("dve" if t % 2 else "act", gv[:, t, :], psb[b][:, :], [PS(b)], ["gv"])
            proj_tm(s_gv, 0, 512, lambda k, t=t: xT[:, k, 256 + t * 128:256 + (t + 1) * 128], ev)

        if STOP == "G1":
            P.emit()
            return nc
        sp_t2 = av(40, [2, 512], F32)
        e_pos2 = av(44, [2, 4, 128], F32)
        e_neg2 = av(48, [2, 4, 128], F32)
        e_k2 = av(52, [2, 2, 256], F32)
        def g2_bufs(t):
            g2 = t % 2
            return g2, sp_t2[:, g2, :], e_pos2[:, g2, :, :], e_neg2[:, g2, :, :], e_k2[:, g2, :, :]

        def g2_front(t):
            tok = slice(t * 128, (t + 1) * 128)
            g2, sp_t, e_pos, e_neg, e_k = g2_bufs(t)
            bz = nb()
            mm(psb[bz][:, :], lrT[0:33, tok], wlr[0:33, :], True, True, ["lrT", "lrT1", "wlr"], [PS(bz)])
            act(sp_t, psb[bz][:, :], AF.Exp, [PS(bz)], ["sp_t%d" % g2], scale=-1.0)
            act(sp_t, sp_t, AF.Ln, ["sp_t%d" % g2], ["sp_t%d" % g2], bias=1.0)

        def g2_back(t):
            tok = slice(t * 128, (t + 1) * 128)
            g2, sp_t, e_pos, e_neg, e_k = g2_bufs(t)
            bc_ = nb()
            for q in range(4):
                mm(psb[bc_][:, q * 128:(q + 1) * 128], sp_t[:, q * 128:(q + 1) * 128], c32[:, 0 if q < 2 else 1, :],
                   True, True, ["sp_t%d" % g2, "c32"], [PS(bc_)], signal=(q == 3))
            bk = nb()
            mm(psb[bk][:, 0:256], c32[:, 2, :], sp_t[:, 0:256], True, True, ["sp_t%d" % g2, "c32"], [PS(bk)], signal=False)
            mm(psb[bk][:, 256:512], c32[:, 3, :], sp_t[:, 256:512], True, True, ["sp_t%d" % g2, "c32"], [PS(bk)])
            pc4 = psb[bc_][:, :].rearrange("p (a b) -> p a b", a=4)
            act(e_pos, pc4, AF.Exp, [PS(bc_)], ["e_pos%d" % g2])
            act(e_neg, pc4, AF.Exp, [PS(bc_)], ["e_neg%d" % g2], scale=-1.0)
            act(e_k, psb[bk][:, :].rearrange("p (a b) -> p a b", a=2), AF.Exp, [PS(bk)], ["e_k%d" % g2])
            tt("dve", qfT[:, :, tok], gqT[:, :, tok], e_pos[:, 0:2, :], ALU.mult, ["gqT", "e_pos%d" % g2], ["qfT"])
            tt("dve", kfT[:, :, tok], gkT[:, :, tok], e_neg[:, 0:2, :], ALU.mult, ["gkT", "e_neg%d" % g2], ["kfT"])
            tt("dve", qbT[:, :, tok], gqT[:, :, tok], e_pos[:, 2:4, :], ALU.mult, ["gqT", "e_pos%d" % g2], ["qbT"])
            tt("dve", kbT[:, :, tok], gkT[:, :, tok], e_neg[:, 2:4, :], ALU.mult, ["gkT", "e_neg%d" % g2], ["kbT"])
            cp("dve", dec[:, 0, :, t], e_pos[:, 0:2, 127], ["e_pos%d" % g2], ["dec"])
            cp("dve", dec[:, 1, :, t], e_pos[:, 2:4, 0], ["e_pos%d" % g2], ["dec"])
            tt("dve", kdec[:, t, :, :], e_k, gk_tm[:, t:t + 1, :].to_broadcast([128, 2, 256]), ALU.mult,
               ["e_k%d" % g2, "gk_tm"], ["kdec"])

        g2_front(0)
        for t in range(16):
            if t + 1 < 16:
                g2_front(t + 1)
            g2_back(t)

        if STOP == "G2":
            P.emit()
            return nc
        S16 = av(64, [16, 2, 2, 128], BF16)
        S32 = av(80, [2, 2, 2, 128], F32)
        S16N = ["S16_%d_%d" % (t, d) for t in range(16) for d in range(2)]
        for nm in S16N:
            P.alias(nm, ["gqT", "gkT", "gk_tm"])
        memset("dve", S32[:, 0, 0, :, :], 0.0, ["S32_0_0"])
        memset("dve", S32[:, 0, 1, :, :], 0.0, ["S32_0_1"])
        am = av(84, [2, 8, 128], BF16)
        P.alias("am0", ["gk_tm"])
        P.alias("am1", ["gk_tm"])
        G2T = ["%s%d" % (n, i) for n in ("sp_t", "e_pos", "e_neg", "e_k") for i in range(2)]
        P.alias("y_glaT", G2T)

        def scan_step(i):
            tf = i
            tb = 15 - i
            b = nb()
            kvp = psb[b][:, :].rearrange("p (d j v) -> p d j v", d=2, j=2)
            for d, tsel in ((0, tf), (1, tb)):
                for j in range(2):
                    for hh in range(2):
                        h = 2 * j + hh
                        mm(kvp[hh * 64:(hh + 1) * 64, d, j, :], kdec[:, tsel, d, j * 128 + hh * 64:j * 128 + hh * 64 + 64],
                           gv[:, tsel, h * 128:(h + 1) * 128], True, True, ["kdec", "gv"], [PS(b)],
                           signal=(d == 1 and j == 1 and hh == 1))
            src, dst = i % 2, (i + 1) % 2
            for d, tsel, tout in ((0, tf, tf + 1), (1, tb, tb - 1)):
                for j in range(2):
                    stt("dve", S32[:, dst, d, j, :], S32[:, src, d, j, :], dec[:, d, j, tsel:tsel + 1], kvp[:, d, j, :],
                        ALU.mult, ALU.add, ["S32_%d_%d" % (src, d), "dec", PS(b)], ["S32_%d_%d" % (dst, d)])
                cp("act", S16[:, tout, d, :, :], S32[:, dst, d, :, :], ["S32_%d_%d" % (dst, d)], ["S16_%d_%d" % (tout, d)])

        sq3 = [av(56, [4, 128], BF16), av(57, [4, 128], BF16), av(58, [4, 128], BF16)]
        osb2 = [av(59, [4, 128], F32), av(61, [4, 128], F32)]
        for nm in ("sq0", "sq1", "sq2", "o_sb0", "o_sb1"):
            P.alias(nm, ["lrT", "lrT1", "gtmp"])
        g4bo = {}

        def g4_A(n, t):
            tok = slice(t * 128, (t + 1) * 128)
            ab = n % 2
            bA = [nb(), nb()]
            for hh in range(2):
                rows = slice(hh * 64, (hh + 1) * 64)
                for d, (kk, qq, knm, qnm) in enumerate(((kfT, qfT, "kfT", "qfT"), (kbT, qbT, "kbT", "qbT"))):
                    for j in range(2):
                        mm(psb[bA[hh]][:, (d * 2 + j) * 128:(d * 2 + j + 1) * 128], kk[rows, j, tok], qq[rows, j, tok],
                           True, True, [knm, qnm], [PS(bA[hh])], signal=(d == 1 and j == 1))
            for hh in range(2):
                tt("dve", am[:, ab, hh * 4:(hh + 1) * 4, :].rearrange("p (d j) c -> p d j c", d=2),
                   psb[bA[hh]][:, :].rearrange("p (d j c) -> p d j c", d=2, j=2),
                   c16[:, 0:256].rearrange("p (d c) -> p d c", d=2).unsqueeze(2).to_broadcast([128, 2, 2, 128]),
                   ALU.mult, [PS(bA[hh]), "c16"], ["am%d" % ab])

        def g4_o(n, t):
            tok = slice(t * 128, (t + 1) * 128)
            ab = n % 2
            bo = nb()
            g4bo[n] = bo
            for h in range(4):
                j, hh = h // 2, h % 2
                rows = slice(hh * 64, (hh + 1) * 64)
                outp = psb[bo][:, h * 128:(h + 1) * 128]
                parts = [(gv[:, t, h * 128:(h + 1) * 128], am[:, ab, hh * 4 + j, :], ["gv", "am%d" % ab]),
                         (gv[:, t, h * 128:(h + 1) * 128], am[:, ab, hh * 4 + 2 + j, :], ["gv", "am%d" % ab])]
                if t > 0:
                    parts.append((S16[rows, t, 0, j, :], qfT[rows, j, tok], ["S16_%d_0" % t, "qfT"]))
                if t < 15:
                    parts.append((S16[rows, t, 1, j, :], qbT[rows, j, tok], ["S16_%d_1" % t, "qbT"]))
                for pi, (l, r, rd) in enumerate(parts):
                    mm(outp, l, r, pi == 0, pi == len(parts) - 1, rd, [PS(bo)],
                       signal=(h == 3 and pi == len(parts) - 1))
            po4 = psb[bo][:, :].rearrange("p (h c) -> p h c", h=4)
            act(sq3[n % 3], po4, AF.Square, [PS(bo)], ["sq%d" % (n % 3)])
            cp("act", osb2[n % 2], po4, [PS(bo)], ["o_sb%d" % (n % 2)])

        def g4_fin(n, t):
            tok = slice(t * 128, (t + 1) * 128)
            sq, o_sb = sq3[n % 3], osb2[n % 2]
            bs = nb()
            mm(psb[bs][:, :], ones, sq.rearrange("p h c -> p (h c)"), True, True, ["sq%d" % (n % 3), "c16"], [PS(bs)])
            ps4 = psb[bs][:, :].rearrange("p (h c) -> p h c", h=4)
            act(ps4, ps4, AF.Ln, [PS(bs), "epst"], [PS(bs)], bias=epst[:, 1:2], scale=1.0 / 128.0)
            act(ps4, ps4, AF.Exp, [PS(bs)], [PS(bs)], scale=-0.5)
            tt("dve", o_sb, ps4, o_sb, ALU.mult, ["o_sb%d" % (n % 2), PS(bs)], ["o_sb%d" % (n % 2)])
            tt("dve", y_glaT[:, :, tok], o_sb, goutT[:, :, tok], ALU.mult, ["o_sb%d" % (n % 2), "goutT"], ["y_glaT"])

        for i in range(7):
            scan_step(i)
        T = []
        for i in range(7, 15):
            T += [14 - i, i + 1]
        for n in range(16 + 2):
            if n < 16 and n % 2 == 0:
                scan_step(7 + n // 2)
            if n < 16:
                g4_A(n, T[n])
            if 0 <= n - 1 < 16:
                g4_o(n - 1, T[n - 1])
            if 0 <= n - 2 < 16:
                g4_fin(n - 2, T[n - 2])

        if STOP == "G4":
            P.emit()
            return nc
        S16N_ = ["S16_%d_%d" % (t, d) for t in range(16) for d in range(2)]
        bias_st = av(64, [12, 2, 128], F32)
        mask_st = av(76, [3, 2, 128], F32)
        P.alias("bias_st", S16N_)
        P.alias("mask_st", S16N_)
        P.dma("sp", bias_st, bias_d, writes=["bias_st"], chan="bias")
        P.dma("sp", mask_st, mask_d, writes=["mask_st"], chan="mask")
        act(bias_st, bias_st, AF.Exp, ["bias_st"], ["bias_st"])
        for g in range(3):
            tt("dve", expb[:, 4 * g:4 * g + 4, :, :], bias_st[:, 4 * g:4 * g + 4, :, :],
               mask_st[:, g:g + 1, :, :].to_broadcast([128, 4, 2, 128]), ALU.mult,
               ["bias_st", "mask_st"], ["expb"])
        acc = av(64, [2, 2, S], F32)
        qT2 = [av(96, [2, S], BF16), av(140, [2, S], BF16)]
        kT2 = [av(104, [2, 2560], BF16), av(148, [2, 2560], BF16)]
        Vt2 = [av(114, [20, 256], BF16), av(158, [20, 256], BF16)]
        et_raw = av(124, [4, 8, 128], BF16)
        et = av(132, [4, 8, 128], BF16)
        ETN = ["et%d" % i for i in range(4)] + ["et_raw%d" % i for i in range(4)]
        QKV = ["qT0", "qT1", "kT0", "kT1", "kTpad0", "kTpad1", "Vt0", "Vt1"]
        P.alias("acc", S16N + ["S32_0_0", "S32_0_1", "S32_1_0", "S32_1_1", "am0", "am1", "qfT", "kfT", "bias_st", "mask_st"])
        P.alias("qT0", ["kfT", "qbT"])
        P.alias("kT0", ["qbT", "kbT", "kdec"])
        P.alias("kTpad0", ["qbT", "kbT", "kdec"])
        P.alias("Vt0", ["kbT", "kdec"])
        for nm in ("qT1", "kT1", "kTpad1", "Vt1"):
            P.alias(nm, ["gv", "goutT"])
        for nm in ETN:
            P.alias(nm, ["kdec", "gv"])
        y_attnT = av(56, [2, S], BF16)
        for q4 in range(4):
            P.alias("y_attnT%d" % q4, ["sq0", "sq1", "sq2", "o_sb0", "o_sb1", "gtmp", "lrT", "lrT1"])
        it = [0]

        def proj_fm_part(slot, col, ncols, evac, tgs):
            banks = {tg: nb() for tg in tgs}
            for k in range(8):
                for tg in tgs:
                    mm(psb[banks[tg]][0:ncols, :], wst[slot][:, k, col:col + ncols],
                       xT[:, k, 256 + tg * 512:256 + (tg + 1) * 512], k == 0, k == 7,
                       XTk(k) + ["W%d" % slot], [PS(banks[tg])])
            for tg in tgs:
                evac(tg, banks[tg])

        def group_work(g, r):
            st_ = g % 2
            qT, kT, Vt = qT2[st_], kT2[st_], Vt2[st_]
            qn, kn, kpn, vn = "qT%d" % st_, "kT%d" % st_, "kTpad%d" % st_, "Vt%d" % st_
            L = S // r
            pad = 64 if g < 2 else 0
            Lp = L + 2 * pad
            nq = L // 128
            nk = 2 if g < 2 else 1
            ntile = {0: 17, 1: 5, 2: 1}[g]
            kTv = kT[:, :, 0:r * Lp].rearrange("p c (r m) -> p c r m", r=r)
            qTv = qT[:, :, :].rearrange("p c (r m) -> p c r m", r=r)
            slots = {}
            slices = []
            late = []

            def sl_first():
                slots["qk"] = next_panel()
                if pad:
                    memset("dve", kTv[:, :, :, 0:pad], 0.0, [kpn])
                    memset("dve", kTv[:, :, :, pad + L:Lp], 0.0, [kpn])
            slices.append(sl_first)
            for c in range(2):
                def evq(tg, b, c=c):
                    src = psb[b][:, :].rearrange("p (m r) -> p r m", r=r)
                    dst = qTv[:, c, :, tg * 512 // r:(tg + 1) * 512 // r]
                    cp("act" if tg % 2 else "dve", dst, src, [PS(b)], [qn])

                def evk(tg, b, c=c):
                    src = psb[b][:, :].rearrange("p (m r) -> p r m", r=r)
                    dst = kTv[:, c, :, pad + tg * 512 // r:pad + (tg + 1) * 512 // r]
                    cp("dve" if tg % 2 else "act", dst, src, [PS(b)], [kn])
                dest = late if (g == 2 and c == 1) else slices
                for tgs in ((0, 1), (2, 3)):
                    dest.append(lambda c=c, evq=evq, tgs=tgs: proj_fm_part(slots["qk"], c * 128, 128, evq, tgs))
                    dest.append(lambda c=c, evk=evk, tgs=tgs: proj_fm_part(slots["qk"], 256 + c * 128, 128, evk, tgs))

            def sl_v0():
                slots["v"] = next_panel(hold=(1 if g == 2 else 0))
            slices.append(sl_v0)
            vts = [(cls, i) for cls in range(r) for i in range(ntile)]

            def v_tiles(lst):
                for (cls, i) in lst:
                    start = 256 + r * (128 * i - pad) + cls
                    vi = cls * ntile + i

                    def evv(b, vi=vi):
                        cp("act" if vi % 2 else "dve", Vt[:, vi, :], psb[b][:, 0:256], [PS(b)], [vn])
                    proj_tm(slots["v"], 0, 256,
                            lambda k, start=start: xT[:, k, start:start + 127 * r + 1:r], evv)
            for p0 in range(0, len(vts), 2):
                slices.append(lambda lst=vts[p0:p0 + 2]: v_tiles(lst))

            pairs = []

            U = 2 if g < 2 else 4

            def do_S(c, pair, eb):
                bS = [nb(), nb()]
                for hh in range(2):
                    rows = slice(hh * 64, (hh + 1) * 64)
                    spv = psb[bS[hh]][:, 0:U * nk * 128].rearrange("p (u k q) -> p u k q", u=U, k=nk)
                    for ui, (cls, qi) in enumerate(pair):
                        for kt in range(nk):
                            mm(spv[:, ui, kt, :], kTv[rows, c, cls, 128 * (qi + kt):128 * (qi + kt) + 128],
                               qTv[rows, c, cls, 128 * qi:128 * qi + 128], True, True, [kn, kpn, qn], [PS(bS[hh])],
                               signal=(ui == U - 1 and kt == nk - 1))
                for hh in range(2):
                    spv = psb[bS[hh]][:, 0:U * nk * 128].rearrange("p (u k q) -> p u k q", u=U, k=nk)
                    er = et_raw[:, eb, hh * 4:(hh + 1) * 4, :].rearrange("p (u k) q -> p u k q", u=U)
                    ee = et[:, eb, hh * 4:(hh + 1) * 4, :].rearrange("p (u k) q -> p u k q", u=U)
                    act(er, spv, AF.Exp, [PS(bS[hh])], ["et_raw%d" % eb], scale=0.125)
                    tt("dve", ee, er,
                       expb[:, 4 * g + 2 * c + hh, 0:nk, :].unsqueeze(1).to_broadcast([128, U, nk, 128]), ALU.mult,
                       ["et_raw%d" % eb, "expb"], ["et%d" % eb])

            def do_PV(c, pair, eb):
                accv = acc[:, c, :, :].rearrange("p n (m r) -> p n m r", r=r)
                bos = {}
                for ui, (cls, qi) in enumerate(pair):
                    if ui % 2 == 0:
                        bos[ui // 2] = nb()
                    bo = bos[ui // 2]
                    pv = psb[bo][:, (ui % 2) * 256:(ui % 2) * 256 + 256].rearrange("p (n q) -> p n q", n=2)
                    for hh in range(2):
                        rows = slice(hh * 64, (hh + 1) * 64)
                        hcol = (2 * c + hh) * 64
                        for n in range(2):
                            for kt in range(nk):
                                vi = cls * ntile + qi + kt
                                if n == 0:
                                    l = Vt[:, vi, hcol:hcol + 64]
                                elif g < 2 and qi + kt == 0:
                                    l = onesF
                                elif g < 2 and qi + kt == ntile - 1:
                                    l = onesL
                                else:
                                    l = ones[:, 0:64]
                                mm(pv[rows, n, :], l, et[:, eb, hh * 4 + ui * nk + kt, :], kt == 0, kt == nk - 1,
                                   [vn, "c16", "et%d" % eb], [PS(bo)],
                                   signal=(hh == 1 and n == 1 and kt == nk - 1))
                for ui, (cls, qi) in enumerate(pair):
                    bo = bos[ui // 2]
                    pv = psb[bo][:, (ui % 2) * 256:(ui % 2) * 256 + 256].rearrange("p (n q) -> p n q", n=2)
                    dst = accv[:, :, 128 * qi:128 * qi + 128, cls]
                    if g == 0:
                        cp("act", dst, pv, [PS(bo)], ["acc"])
                    else:
                        tt("dve", dst, pv, dst, ALU.add, [PS(bo), "acc"], ["acc"])

            units = [(cls, qi) for cls in range(r) for qi in range(nq)]
            for c in range(2):
                for up in range(0, len(units), U):
                    pr = units[up:up + U]
                    pairs.append((lambda eb, c=c, pr=pr: do_S(c, pr, eb), lambda eb, c=c, pr=pr: do_PV(c, pr, eb)))
            return slices, pairs, late

        work = [group_work(g, r) for g, r in enumerate((1, 4, 16))]
        for f in work[0][0]:
            f()
        wbr = av(144, [6, D], BF16)
        vec_ln1g = av(160, [D], F32)
        vec_ln1b = av(164, [D], F32)
        vec_ln2g = av(168, [D], F32)
        for g in range(3):
            pairs = work[g][1]
            nxt = work[g + 1][0] if g + 1 < 3 else work[g][2]
            if g == 2:
                for nm in ("wbr", "vecs"):
                    P.alias(nm, ["qT1", "kT1", "kTpad1", "Vt1", "gv", "goutT"])
                P.dma("pool", wbr[:, 0:2, :], wab_d.rearrange("(k p) c -> p k c", p=128), writes=["wbr"], chan="wbr")
                P.dma("pool", wbr[:, 2:6, :], wgb_d.rearrange("(k p) c -> p k c", p=128), writes=["wbr"], chan="wbr")
                P.dma("sp", vec_ln1g, vec_d[:, 0, :], writes=["vecs"], chan="vecs")
                P.dma("sp", vec_ln1b, vec_d[:, 1, :], writes=["vecs"], chan="vecs")
                P.dma("sp", vec_ln2g, vec_d[:, 2, :], writes=["vecs"], chan="vecs")
            done = 0
            ebs = [(it[0] + i) % 4 for i in range(len(pairs))]
            it[0] += len(pairs)
            issued = 0
            for pi in range(len(pairs)):
                want = (pi + 1) * len(nxt) // len(pairs) if g < 2 else min(len(nxt), 2 * (pi + 1))
                has_fill = done < want
                ahead = 1 if (g < 2 or has_fill) else 2
                while issued <= min(pi + ahead, len(pairs) - 1):
                    pairs[issued][0](ebs[issued])
                    issued += 1
                while done < want:
                    nxt[done]()
                    done += 1
                pairs[pi][1](ebs[pi])

        if STOP == "A0":
            P.emit()
            return nc
        rden = av(96, [2, S], F32)
        for q4 in range(4):
            P.alias("rden%d" % q4, ["kfT", "qbT", "kbT", "qT0", "kT0", "kTpad0"])
        for q4 in range(4):
            qs = slice(q4 * 512, (q4 + 1) * 512)
            act(rden[:, :, qs], acc[:, :, 1, qs], AF.Ln, ["acc"], ["rden%d" % q4])
            act(rden[:, :, qs], rden[:, :, qs], AF.Exp, ["rden%d" % q4], ["rden%d" % q4], scale=-1.0)
            tt("dve", y_attnT[:, :, qs], acc[:, :, 0, qs], rden[:, :, qs], ALU.mult, ["acc", "rden%d" % q4], ["y_attnT%d" % q4])

        if debug:
            dtmp = av(112, [4, S], F32)
            P.alias("dtmp", ["gv", "kdec", "rden0", "rden1", "rden2", "rden3"] + ETN + QKV)
            cp("dve", dtmp[:, 0:2, :], y_attnT, ["y_attnT0", "y_attnT1", "y_attnT2", "y_attnT3"], ["dtmp"])
            P.dma("sp", dbg_d["d_yattn"], dtmp[:, 0:2, :], reads=["dtmp"], chan="dbg0")
            cp("dve", dtmp, y_glaT, ["y_glaT"], ["dtmp"])
            P.dma("sp", dbg_d["d_ygla"], dtmp, reads=["dtmp"], chan="dbg1")

        if STOP == "A":
            P.emit()
            return nc
        mergedT = av(64, [8, S], BF16)
        mtmp = av(128, [2, 2, 512], F32)
        sgt = av(136, [2, 2, 512], F32)
        for q4 in range(4):
            P.alias("mergedT%d" % q4, ["acc"])
        for nm in ("mtmp0", "mtmp1", "sgt0", "sgt1"):
            P.alias(nm, ["dtmp", "gv", "kdec"] + QKV + ETN)
        wout = av(96, [8, D], BF16)
        X1Bv = [av(112, [4, D], F32), av(144, [4, D], F32)]
        x1T = av(128, [2, 8, 512], BF16)
        vec_ln2b = av(58, [D], F32)
        hT = av(0, [32, 512], BF16)
        b2v = av(32, [D], F32)
        x1h = av(36, [2, D], BF16)
        rtmp4 = av(40, [2, 512], BF16)
        rtmp4b = av(62, [2, 512], BF16)
        rtmp = [rtmp4[:, 0, :], rtmp4[:, 1, :], rtmp4b[:, 0, :], rtmp4b[:, 1, :]]
        u2 = av(42, [4, D], F32)
        vecs = {0: (vec_ln1g, vec_ln1b), 2: (vec_ln2g, vec_ln2b)}
        P.alias("wout", QKV + ["rden0", "rden1", "rden2", "rden3"])
        X1B = ["x1b%d_%d" % (bf, i) for bf in range(2) for i in range(4)]
        for nm in X1B[0:4]:
            P.alias(nm, ["kdec", "dtmp"] + ETN + QKV)
        issue_panels()
        P.dma("pool", wout, wout_d.rearrange("(k p) c -> p k c", p=128), writes=["wout"], chan="wout")
        x_tiles = x_d.rearrange("(t p) d -> t p d", p=128)
        out_tiles = out_d.rearrange("(t p) d -> t p d", p=128)

        def layernorm(u_ap, unm, slot, gi, out_ap, onm):
            vg, vb = vecs[gi]
            for hf in range(2):
                P.op("dve", lambda e, hf=hf: e.bn_stats(stats[:, slot, hf, :], u_ap[:, hf * 512:(hf + 1) * 512]),
                     [unm], ["stats%d" % slot])
            P.op("dve", lambda e: e.bn_aggr(mv[:, slot, 0:2], stats[:, slot, :, :]), ["stats%d" % slot], ["mv%d" % slot])
            act(mv[:, slot, 2:3], mv[:, slot, 1:2], AF.Sqrt, ["mv%d" % slot, "epst"], ["mv%d" % slot], bias=epst[:, 0:1])
            P.op("dve", lambda e: e.reciprocal(mv[:, slot, 2:3], mv[:, slot, 2:3]), ["mv%d" % slot], ["mv%d" % slot])
            stt("dve", mv[:, slot, 3:4], mv[:, slot, 0:1], -1.0, mv[:, slot, 2:3], ALU.mult, ALU.mult,
                ["mv%d" % slot], ["mv%d" % slot])
            act(out_ap, u_ap, AF.Identity, [unm, "mv%d" % slot], [onm], bias=mv[:, slot, 3:4], scale=mv[:, slot, 2:3])
            tt("dve", out_ap, out_ap, vg, ALU.mult, [onm, "vecs", "vecs2"], [onm])
            tt("dve", out_ap, out_ap, vb, ALU.add, [onm, "vecs", "vecs2"], [onm])

        def F1a(tg, tiles, cast=True):
            bf = tg % 2
            for tt_ in tiles:
                t = tg * 4 + tt_
                xs = t % 2
                xnm = "x1b%d_%d" % (bf, tt_)
                xt = X1Bv[bf][:, tt_, :]
                P.dma("sp", xt, x_tiles[t], writes=[xnm], chan=xnm)
                bks = [nb(), nb()]
                for hf in range(2):
                    for j in range(8):
                        mm(psb[bks[hf]][:, :], mergedT[:, j, t * 128:(t + 1) * 128], wout[:, j, hf * 512:(hf + 1) * 512],
                           j == 0, j == 7, ["mergedT%d" % tg, "wout"], [PS(bks[hf])])
                for hf in range(2):
                    stt("dve", xt[:, hf * 512:(hf + 1) * 512], xt[:, hf * 512:(hf + 1) * 512], ALPHA,
                        psb[bks[hf]][:, :], ALU.mult, ALU.add, [xnm, PS(bks[hf])], [xnm])
                layernorm(xt, xnm, xs, 0, xt, xnm)
            if cast:
                F1a_cast(tg, tiles)

        def F1a_cast(tg, tiles):
            bf = tg % 2
            for tt_ in tiles:
                xs = (tg * 4 + tt_) % 2
                xnm = "x1b%d_%d" % (bf, tt_)
                xt = X1Bv[bf][:, tt_, :]
                cp("act", x1h[:, xs, :], xt, [xnm], ["x1h%d" % xs])
                stt("dve", xt, xt, ALPHA, b2v, ALU.mult, ALU.add, [xnm, "b2v"], [xnm])

        def F1b(tg, tiles):
            bf = tg % 2
            for tt_ in tiles:
                xs = (tg * 4 + tt_) % 2
                bt = nb()
                ptb = psb[bt][:, :].bitcast(BF16)
                for k in range(8):
                    P.op("pe", lambda e, k=k, xs=xs, ptb=ptb: e.transpose(ptb[:, k * 128:(k + 1) * 128],
                                                                         x1h[:, xs, k * 128:(k + 1) * 128], ident),
                         ["x1h%d" % xs, "c16"], [PS(bt)], signal=(k == 7))
                cp("act", x1T[:, bf, :, tt_ * 128:(tt_ + 1) * 128], ptb.rearrange("p (k c) -> p k c", k=8), [PS(bt)],
                   ["x1T%d" % bf])

        def F2(tg):
            bf = tg % 2
            for u in range(8):
                s1 = next_panel()
                for f4 in range(4):
                    fc = u * 4 + f4
                    b = nb()
                    for k in range(8):
                        mm(psb[b][:, :], wst[s1][:, k, f4 * 128:(f4 + 1) * 128], x1T[:, bf, k, :], k == 0, k == 7,
                           ["W%d" % s1, "x1T%d" % bf], [PS(b)])
                    rb = fc % 4
                    act(rtmp[rb], psb[b][:, :], AF.Relu, [PS(b), "b1T"], ["rtmp%d" % rb], bias=b1T[:, fc:fc + 1])
                    tt("dve", hT[:, fc, :], rtmp[rb], rtmp[rb], ALU.mult, ["rtmp%d" % rb], ["hT"])
                if tg > 0 and 2 <= u <= 5:
                    LN2(tg - 1, [u - 2])

        def F3h(tg, hf, before_last=None, tile_sets=((0, 1, 2, 3),), after_set=None):
            bf = tg % 2
            for si, tset in enumerate(tile_sets):
                banks = {tt_: nb() for tt_ in tset}
                for u in range(4):
                    if u == 3 and before_last is not None and si == len(tile_sets) - 1:
                        before_last()
                    s2 = next_panel()
                    for tt_ in tset:
                        for f8 in range(8):
                            fc = u * 8 + f8
                            mm(psb[banks[tt_]][:, :], hT[:, fc, tt_ * 128:(tt_ + 1) * 128], wst[s2][:, f8, :],
                               fc == 0, fc == 31, ["hT", "W%d" % s2], [PS(banks[tt_])])
                for tt_ in tset:
                    tt("dve", u2[:, tt_, hf * 512:(hf + 1) * 512], psb[banks[tt_]][:, :],
                       X1Bv[bf][:, tt_, hf * 512:(hf + 1) * 512], ALU.add,
                       [PS(banks[tt_]), "x1b%d_%d" % (bf, tt_)], ["u2_%d" % tt_])
                if after_set is not None:
                    after_set(tset)

        def LN2(tg, tiles):
            for tt_ in tiles:
                t = tg * 4 + tt_
                layernorm(u2[:, tt_, :], "u2_%d" % tt_, 2 + tt_ % 2, 2, u2[:, tt_, :], "u2_%d" % tt_)
                P.dma("sp", out_tiles[t], u2[:, tt_, :], reads=["u2_%d" % tt_], chan="ot%d" % tt_)


        mit = [0]
        for u in range(2):
            s_ga = next_panel()
            s_gg = next_panel(hold=1)
            order = [(jj, tg) for jj in range(4) for tg in range(4)] if u == 0 else \
                    [(jj, tg) for tg in range(4) for jj in range(4)]
            for (jj, tg) in order:
                if True:
                    j = u * 4 + jj
                    mb = mit[0] % 2
                    mit[0] += 1
                    tks = slice(tg * 512, (tg + 1) * 512)
                    b_ga, b_gg, b_za, b_zg = nb(), nb(), nb(), nb()
                    for (bq, sl) in ((b_ga, s_ga), (b_gg, s_gg)):
                        for k in range(8):
                            mm(psb[bq][:, :], wst[sl][:, k, jj * 128:(jj + 1) * 128],
                               xT[:, k, 256 + tg * 512:256 + (tg + 1) * 512], k == 0, k == 7,
                               XTk(k) + ["W%d" % sl], [PS(bq)])
                    for k in range(2):
                        mm(psb[b_za][:, :], wbr[:, k, j * 128:(j + 1) * 128], y_attnT[:, k, tks], k == 0, k == 1,
                           ["wbr", "y_attnT%d" % tg], [PS(b_za)])
                    for k in range(4):
                        mm(psb[b_zg][:, :], wbr[:, 2 + k, j * 128:(j + 1) * 128], y_glaT[:, k, tks], k == 0, k == 3,
                           ["wbr", "y_glaT"], [PS(b_zg)])
                    act(sgt[:, mb, 0, :], psb[b_ga][:, :], AF.Sigmoid, [PS(b_ga)], ["sgt%d" % mb])
                    act(sgt[:, mb, 1, :], psb[b_gg][:, :], AF.Sigmoid, [PS(b_gg)], ["sgt%d" % mb])
                    tt("dve", mtmp[:, mb, 0, :], psb[b_za][:, :], sgt[:, mb, 0, :], ALU.mult, [PS(b_za), "sgt%d" % mb], ["mtmp%d" % mb])
                    tt("dve", mtmp[:, mb, 1, :], psb[b_zg][:, :], sgt[:, mb, 1, :], ALU.mult, [PS(b_zg), "sgt%d" % mb], ["mtmp%d" % mb])
                    tt("dve", mergedT[:, j, tks], mtmp[:, mb, 0, :], mtmp[:, mb, 1, :], ALU.add, ["mtmp%d" % mb], ["mergedT%d" % tg])
                    if u == 1 and not debug and (tg, jj) in ((1, 0), (1, 3), (2, 2), (3, 1)):
                        F1a(0, ({(1, 0): 0, (1, 3): 1, (2, 2): 2, (3, 1): 3}[(tg, jj)],), cast=False)

        if debug:
            dtm = av(0, [8, S], F32)
            P.alias("dtm", XT + ["y_glaT"] + ["y_attnT0", "y_attnT1", "y_attnT2", "y_attnT3"])
            cp("dve", dtm, mergedT, ["mergedT0", "mergedT1", "mergedT2", "mergedT3"], ["dtm"])
            P.dma("sp", dbg_d["d_merged"], dtm, reads=["dtm"], chan="dbg2")

        if STOP == "M":
            P.emit()
            return nc
        for nm in X1B[4:8]:
            P.alias(nm, ["wbr", "gv", "goutT", "dtmp"] + QKV)
        for nm in ("x1T0", "x1T1"):
            P.alias(nm, ["mtmp0", "mtmp1", "sgt0", "sgt1", "gv", "kdec", "dtmp"] + ETN + QKV)
        for nm in ("hT", "b2v", "x1h0", "x1h1", "rtmp0", "rtmp1", "rtmp2", "rtmp3", "u2_0", "u2_1", "u2_2", "u2_3", "vecs2"):
            P.alias(nm, XT + ["y_glaT", "dtm"] + ["y_attnT0", "y_attnT1", "y_attnT2", "y_attnT3"])
        P.dma("sp", vec_ln2b, vec_d[:, 3, :], writes=["vecs2"], chan="vecs2")
        P.dma("sp", b2v, vec_d[:, 4, :], writes=["b2v"], chan="b2v")
        if debug:
            F1a(0, (0, 1, 2, 3), cast=False)
        for tl in ((0, 1), (2, 3)):
            F1a_cast(0, tl)
            F1b(0, tl)
        for tg in range(4):
            F2(tg)
            nxt = tg + 1 < 4
            if nxt:
                F1a(tg + 1, (0, 1))
            F3h(tg, 0)
            if nxt:
                F1b(tg + 1, (0, 1))
                F1a(tg + 1, (2, 3))
            if nxt:
                F3h(tg, 1, before_last=lambda tg=tg: F1b(tg + 1, (2, 3)))
            else:
                F3h(tg, 1, tile_sets=((0, 1), (2, 3)), after_set=lambda tset: LN2(3, list(tset)))

        P.emit()
    return nc


_NC_CACHE = {}


def _prep_shared(w_in, rel_bias, w_lr_fwd, b_lr_fwd, w_lr_bwd, b_lr_bwd, gla_norm_g, w_attn_branch,
                 w_gla_branch, w_out, ln1_g, ln1_b, w_ff1, b_ff1, w_ff2, b_ff2, ln2_g, ln2_b):
    f = lambda a: np.ascontiguousarray(np.asarray(a, dtype=np.float32))
    rel_bias = f(rel_bias)
    buckets, masks = _host_tables()
    bias = np.zeros((256, 12, 128), np.float32)
    for g in range(3):
        for hs in range(4):
            bias[:, 4 * g + hs, :] = rel_bias[buckets[g], 4 * g + hs]
    biasT = np.ascontiguousarray(bias.reshape(2, 128, 12, 128).transpose(1, 2, 0, 3))
    maskT = np.ascontiguousarray(np.stack(masks, 1).reshape(2, 128, 3, 128).transpose(1, 2, 0, 3))
    wlr = np.zeros((33, 512), np.float32)
    wlr[0:16, 0:256] = f(w_lr_fwd)[0]
    wlr[16:32, 256:512] = f(w_lr_bwd)[0]
    wlr[32, 0:256] = f(b_lr_fwd)[0]
    wlr[32, 256:512] = f(b_lr_bwd)[0]
    c32, c16 = _consts()
    vecs = np.stack([f(ln1_g)[0], f(ln1_b)[0], f(ln2_g)[0], f(ln2_b)[0], f(b_ff2)[0]], 0)
    vecs = np.ascontiguousarray(np.broadcast_to(vecs[None], (128, 5, D)))
    return {
        "w_in": f(w_in)[0], "wlr": wlr, "c32": c32, "c16": c16, "biasT": biasT, "maskT": maskT,
        "w_ab": f(w_attn_branch)[0], "w_gb": f(w_gla_branch)[0], "w_out": f(w_out)[0],
        "w_ff1": f(w_ff1)[0], "b1T": np.ascontiguousarray(f(b_ff1)[0].reshape(32, 128).T),
        "w_ff2": f(w_ff2)[0], "gT": np.ascontiguousarray(f(gla_norm_g)[0].reshape(4, 128).T),
        "vecs": vecs,
    }


def kernel(x, **params):
    debug = bool(os.environ.get("MK_DEBUG"))
    x = np.asarray(x, dtype=np.float32)
    B = x.shape[0]
    shared = _prep_shared(**params)
    key = debug
    if key not in _NC_CACHE:
        _NC_CACHE[key] = build_nc(debug)
    nc = _NC_CACHE[key]
    in_maps = []
    for b in range(B):
        m = dict(shared)
        m["x"] = np.ascontiguousarray(x[b])
        m["xT"] = np.ascontiguousarray(x[b].T)
        in_maps.append(m)
    res = run_bass_kernel_spmd(nc, in_maps, core_ids=list(range(B)))
    out = np.stack([np.asarray(r["out"], dtype=np.float32) for r in res.results], 0)
    if debug:
        kernel.dbg = [{k: np.asarray(v) for k, v in r.items() if k.startswith("d_")} for r in res.results]
    return out
```

```python
import contextlib
import os
import numpy as np
import concourse.bass as bass
import concourse.mybir as mybir
from concourse.bass_utils import run_bass_kernel_spmd

F32 = mybir.dt.float32
BF16 = mybir.dt.bfloat16
AF = mybir.ActivationFunctionType
ALU = mybir.AluOpType

S = 2048
D = 1024
DFF = 4096
NCOL = 5920
ALPHA = 2.0 ** 0.25
LN_EPS = 1e-5
NORM_EPS = 1e-6
NSLOT = 3
SAME_ENGINE_WAR_SYNC = True
KB = 1024

ENGS = ("pe", "act", "dve", "pool", "sp")


class Prog:
    def __init__(self, nc):
        self.nc = nc
        self.ops = []
        self.lastw = {}
        self.readers = {}

    def op(self, eng, fn, reads=(), writes=(), signal=True, chan=None):
        oid = len(self.ops)
        deps = set()
        for r in reads:
            w = self.lastw.get(r)
            if w is not None:
                deps.add((w, "raw"))
        for r in writes:
            w = self.lastw.get(r)
            if w is not None:
                deps.add((w, "waw"))
            for rd in self.readers.get(r, ()):
                deps.add((rd, "war"))
        for r in reads:
            self.readers.setdefault(r, []).append(oid)
        for r in writes:
            self.lastw[r] = oid
            self.readers[r] = []
        self.ops.append(dict(id=oid, eng=eng, fn=fn, deps=deps, signal=signal, chan=chan))
        return oid

    def alias(self, new, olds):
        rs = list(self.readers.get(new, []))
        w = self.lastw.get(new)
        if w is not None:
            rs.append(w)
        for o in olds:
            rs.extend(self.readers.get(o, []))
            w = self.lastw.get(o)
            if w is not None:
                rs.append(w)
        self.readers[new] = rs
        self.lastw.pop(new, None)

    def dma(self, eng, out, in_, reads=(), writes=(), chan=None):
        assert chan is not None
        return self.op(eng, lambda e: e.dma_start(out=out, in_=in_), reads, writes, True, chan)

    def finalize(self):
        ops = self.ops
        cnt = {}
        last_of = {}
        for o in ops:
            if o["chan"] is None:
                last_of[o["eng"]] = o["id"]
        for e, i in last_of.items():
            ops[i]["signal"] = True
        import bisect
        sig_ids = {}
        for o in ops:
            if o["chan"] is None and o["signal"]:
                sig_ids.setdefault(o["eng"], []).append(o["id"])
        for o in ops:
            for (d, kind) in o["deps"]:
                do = ops[d]
                if do["chan"] is not None or do["signal"]:
                    continue
                lst = sig_ids.setdefault(do["eng"], [])
                p = bisect.bisect_left(lst, d)
                if p >= len(lst) or lst[p] >= o["id"]:
                    do["signal"] = True
                    bisect.insort(lst, d)
        pending = {}
        for o in ops:
            if o["chan"] is not None:
                k = "ch_" + o["chan"]
                cnt[k] = cnt.get(k, 0) + 16
                o["tok"] = (k, cnt[k])
            else:
                k = "e_" + o["eng"]
                pending.setdefault(k, []).append(o)
                if o["signal"]:
                    cnt[k] = cnt.get(k, 0) + 1
                    for p in pending[k]:
                        p["tok"] = (k, cnt[k])
                    pending[k] = []
        self.semkeys = sorted(cnt.keys())
        self.cnt = cnt
        chan_hist = {}
        for o in ops:
            if o["chan"] is not None:
                chan_hist.setdefault(o["tok"][0], []).append((o["id"], o["tok"][1]))
        known = {e: {} for e in ENGS}
        snap = {}
        per_eng = {e: [] for e in ENGS}
        for o in ops:
            e = o["eng"]
            kn = known[e]
            need = {}
            for (d, kind) in o["deps"]:
                do = ops[d]
                same = (do["eng"] == e) and do["chan"] is None and o["chan"] is None
                if same and kind != "raw":
                    if e != "pe" and kn.get(do["tok"][0], 0) < do["tok"][1]:
                        self.n_same_war = getattr(self, "n_same_war", 0) + 1
                        if SAME_ENGINE_WAR_SYNC:
                            k, v = do["tok"]
                            if v > need.get(k, (0, None))[0]:
                                need[k] = (v, d)
                    continue
                k, v = do["tok"]
                if v > need.get(k, (0, None))[0]:
                    need[k] = (v, d)
            waits = {}
            for k, (v, d) in need.items():
                if k.startswith("ch_"):
                    lst = chan_hist[k]
                    p = bisect.bisect_left(lst, (o["id"], 0)) - 1
                    v = max(v, lst[p][1])
                if kn.get(k, 0) >= v:
                    continue
                waits[k] = v
                kn[k] = v
                for kk, vv in snap[d].items():
                    if kn.get(kk, 0) < vv:
                        kn[kk] = vv
            o["waits"] = waits
            snap[o["id"]] = dict(kn)
            per_eng[e].append(o)
        return per_eng

    def emit(self):
        nc = self.nc
        per_eng = self.finalize()
        with contextlib.ExitStack() as st:
            sems = {k: st.enter_context(nc.semaphore(k)) for k in self.semkeys}
            block = st.enter_context(nc.Block())

            def run(ename):
                def body(eng):
                    for o in per_eng[ename]:
                        for k, v in o["waits"].items():
                            eng.wait_ge(sems[k], v)
                        ins = o["fn"](eng)
                        if o["chan"] is not None:
                            ins.then_inc(sems[o["tok"][0]], 16)
                        elif o["signal"]:
                            ins.then_inc(sems[o["tok"][0]], 1)
                    if ename == "sp":
                        for k in self.semkeys:
                            if k.startswith("ch_"):
                                eng.wait_ge(sems[k], self.cnt[k])
                return body

            block.sync(run("sp"))
            block.gpsimd(run("pool"))
            block.scalar(run("act"))
            block.vector(run("dve"))
            block.tensor(run("pe"))


def _t5_bucket(rel):
    nb = 16
    max_exact = 8
    ret = (rel > 0).astype(np.int32) * nb
    n = np.abs(rel)
    large = max_exact + (np.log(np.maximum(n, 1) / max_exact) / np.log(1024 / max_exact) * (nb - max_exact)).astype(np.int32)
    large = np.minimum(large, nb - 1)
    return (ret + np.where(n < max_exact, n, large)).astype(np.int32)


def _host_tables():
    j = np.arange(256)[:, None]
    i = np.arange(128)[None, :]
    buckets, masks = [], []
    for g, dil in enumerate((1, 4, 16)):
        delta = (j - 64 - i) if g < 2 else (j - i)
        m = (np.abs(delta) <= 64)
        if g == 2:
            m = m & (j < 128)
        buckets.append(_t5_bucket(delta * dil))
        masks.append(m.astype(np.float32))
    return buckets, masks


def _consts():
    s = np.arange(128)[:, None]
    c = np.arange(128)[None, :]
    v = -1.0 / 16.0
    c32 = np.stack([
        np.where(s <= c, v, 0.0), np.where(s >= c, v, 0.0),
        np.where(s > c, v, 0.0), np.where(s < c, v, 0.0)], 0).astype(np.float32)
    c32 = np.ascontiguousarray(c32.transpose(1, 0, 2))
    maskF = (s <= c).astype(np.float32)
    maskB = (s > c).astype(np.float32)
    ident = np.eye(128, dtype=np.float32)
    ones = np.ones((128, 128), np.float32)
    onesF = np.ones((128, 64), np.float32); onesF[0:64] = 0
    onesL = np.ones((128, 64), np.float32); onesL[64:128] = 0
    c16 = np.concatenate([maskF, maskB, ident, ones, onesF, onesL], 1)
    return c32, np.ascontiguousarray(c16)


def build_nc(debug=False):
    STOP = os.environ.get("MK_STOP", "")
    nc = bass.Bass("TRN2", target_bir_lowering=False)

    def din(name, shape):
        return nc.dram_tensor(name, list(shape), F32, kind="ExternalInput").ap()

    xT_d = din("xT", [D, S])
    x_d = din("x", [S, D])
    win_d = din("w_in", [D, NCOL])
    wlr_d = din("wlr", [33, 512])
    c32_d = din("c32", [128, 4, 128])
    c16_d = din("c16", [128, 640])
    bias_d = din("biasT", [128, 12, 2, 128])
    mask_d = din("maskT", [128, 3, 2, 128])
    wab_d = din("w_ab", [256, D])
    wgb_d = din("w_gb", [512, D])
    wout_d = din("w_out", [D, D])
    w1_d = din("w_ff1", [D, DFF])
    b1_d = din("b1T", [128, 32])
    w2_d = din("w_ff2", [DFF, D])
    gT_d = din("gT", [128, 4])
    vec_d = din("vecs", [128, 5, D])
    out_d = nc.dram_tensor("out", [S, D], F32, kind="ExternalOutput").ap()
    dbg_d = {}
    if debug:
        for nm, shp in (("d_yattn", [128, 2, S]), ("d_ygla", [128, 4, S]), ("d_merged", [128, 8, S])):
            dbg_d[nm] = nc.dram_tensor(nm, shp, F32, kind="ExternalOutput").ap()

    with contextlib.ExitStack() as st:
        def sb(name, shape, dt):
            return st.enter_context(nc.sbuf_tensor("sb_" + name, list(shape), dt))

        ARENA_KB = 172
        arena = sb("arena", [128, ARENA_KB * KB // 2], BF16)

        def av(off_kb, shape, dt):
            n = int(np.prod(shape))
            nb = n * (4 if dt == F32 else 2)
            o = int(off_kb * KB)
            assert o + nb <= ARENA_KB * KB, (off_kb, shape)
            ap = arena[:, o // 2:(o + nb) // 2]
            if dt == F32:
                ap = ap.bitcast(F32)
            if len(shape) == 2:
                ap = ap.rearrange("p (a b) -> p a b", a=shape[0])
            elif len(shape) == 3:
                ap = ap.rearrange("p (a b c) -> p a b c", a=shape[0], b=shape[1])
            elif len(shape) == 4:
                ap = ap.rearrange("p (a b c d) -> p a b c d", a=shape[0], b=shape[1], c=shape[2])
            return ap

        wst = [sb("wst%d" % i, [128, 8, 512], BF16) for i in range(NSLOT)]
        c32 = sb("c32", [128, 4, 128], F32)
        c16 = sb("c16", [128, 640], BF16)
        expb = sb("expb", [128, 12, 2, 128], BF16)
        wlr = sb("wlr", [33, 512], BF16)
        b1T = sb("b1T", [128, 32], F32)
        gT = sb("gT", [128, 4], F32)
        dec = sb("dec", [128, 2, 2, 16], F32)
        stats = sb("stats", [128, 4, 2, 6], F32)
        mv = sb("mv", [128, 4, 4], F32)
        epst = sb("epst", [128, 2], F32)
        psb = [st.enter_context(nc.psum_tensor("ps%d" % i, [128, 512], F32)) for i in range(8)]

        maskF = c16[:, 0:128]
        maskB = c16[:, 128:256]
        ident = c16[:, 256:384]
        ones = c16[:, 384:512]
        onesF = c16[:, 512:576]
        onesL = c16[:, 576:640]

        P = Prog(nc)
        bank_ctr = [0]

        def nb():
            b = bank_ctr[0] % 8
            bank_ctr[0] += 1
            return b

        def PS(b):
            return "ps%d" % b

        def mm(out, lhsT, rhs, start, stop, reads, writes, signal=None):
            P.op("pe", lambda e: e.matmul(out, lhsT, rhs, start=start, stop=stop), reads, writes,
                 signal=(stop if signal is None else signal))

        def act(out, in_, func, reads, writes, bias=None, scale=None):
            kw = {}
            if bias is not None:
                kw["bias"] = bias
            if scale is not None:
                kw["scale"] = scale
            P.op("act", lambda e: e.activation(out, in_, func, **kw), reads, writes)

        def tt(eng, out, in0, in1, op, reads, writes):
            P.op(eng, lambda e: e.tensor_tensor(out, in0, in1, op), reads, writes)

        def ts(eng, out, in0, s1, s2, op0, op1, reads, writes):
            if op1 is None:
                P.op(eng, lambda e: e.tensor_scalar(out, in0, s1, None, op0), reads, writes)
            else:
                P.op(eng, lambda e: e.tensor_scalar(out, in0, s1, s2, op0, op1), reads, writes)

        def stt(eng, out, in0, scalar, in1, op0, op1, reads, writes):
            P.op(eng, lambda e: e.scalar_tensor_tensor(out, in0, scalar, in1, op0, op1), reads, writes)

        def cp(eng, out, in_, reads, writes):
            if eng == "act":
                P.op("act", lambda e: e.activation(out, in_, AF.Copy), reads, writes)
            else:
                P.op(eng, lambda e: e.tensor_copy(out, in_), reads, writes)

        def memset(eng, ap, val, writes):
            P.op(eng, lambda e: e.memset(ap, val), (), writes)

        panel_seq = []
        ptr = {"use": 0, "issue": 0}

        def win_panel(c0, n):
            return win_d[:, c0:c0 + n].rearrange("(k p) c -> p k c", p=128)

        def issue_panels(hold=0):
            while ptr["issue"] < min(len(panel_seq), ptr["use"] + NSLOT - hold):
                s = ptr["issue"] % NSLOT
                for (off, n, src) in panel_seq[ptr["issue"]]:
                    P.dma("pool", wst[s][:, :, off:off + n], src, writes=["W%d" % s], chan="W%d" % s)
                ptr["issue"] += 1

        def next_panel(hold=0):
            issue_panels(hold)
            s = ptr["use"] % NSLOT
            ptr["use"] += 1
            return s

        panel_seq.append([(0, 32, win_panel(3840, 32))])
        panel_seq.append([(0, 512, win_panel(2304, 512))])
        panel_seq.append([(0, 512, win_panel(2816, 512))])
        panel_seq.append([(0, 512, win_panel(3328, 512))])
        for g in range(3):
            panel_seq.append([(0, 256, win_panel(g * 256, 256)), (256, 256, win_panel(768 + g * 256, 256))])
            panel_seq.append([(0, 256, win_panel(1536 + g * 256, 256))])
        for u in range(2):
            panel_seq.append([(0, 512, win_panel(3872 + u * 512, 512))])
            panel_seq.append([(0, 512, win_panel(4896 + u * 512, 512))])
        for tg in range(4):
            for u in range(8):
                panel_seq.append([(0, 512, w1_d[:, u * 512:(u + 1) * 512].rearrange("(k p) c -> p k c", p=128))])
            for half in range(2):
                for rep in range(2 if (tg == 3 and half == 1) else 1):
                    for u in range(4):
                        panel_seq.append([(0, 512, w2_d[u * 1024:(u + 1) * 1024, half * 512:(half + 1) * 512]
                                           .rearrange("(f p) c -> p f c", p=128))])

        P.dma("sp", c32[:, :, :], c32_d, writes=["c32"], chan="c32")
        P.dma("sp", b1T[:, :], b1_d, writes=["b1T"], chan="small")
        P.dma("sp", gT[:, :], gT_d, writes=["gT"], chan="gT")
        memset("dve", epst[:, 0:1], LN_EPS, ["epst"])
        memset("dve", epst[:, 1:2], NORM_EPS, ["epst"])

        xT = av(0, [8, 2560], BF16)
        memset("dve", xT[:, :, 0:256], 0.0, ["xTpadL"])
        memset("dve", xT[:, :, 2304:2560], 0.0, ["xTpadR"])
        issue_panels(hold=1)
        xT_src = xT_d.rearrange("(k p) t -> p k t", p=128)
        for k in range(8):
            P.dma("pool", xT[:, k, 256:2304], xT_src[:, k, :], writes=["xT%d" % k], chan="xT%d" % k)
        XT = ["xT%d" % k for k in range(8)] + ["xTpadL", "xTpadR"]

        def XTk(k):
            return ["xT%d" % k, "xTpadL", "xTpadR"]
        P.dma("pool", wlr[:, :], wlr_d, writes=["wlr"], chan="wlr")
        P.dma("pool", c16[:, :], c16_d, writes=["c16"], chan="c16")

        if STOP == "INIT":
            P.emit()
            return nc
        y_glaT = av(40, [4, S], BF16)
        gqT = av(64, [2, S], BF16)
        gkT = av(72, [2, S], BF16)
        gk_tm = av(80, [16, 256], BF16)
        qfT = av(88, [2, S], BF16)
        kfT = av(96, [2, S], BF16)
        qbT = av(104, [2, S], BF16)
        kbT = av(112, [2, S], BF16)
        kdec = av(120, [16, 2, 256], BF16)
        gv = av(136, [16, 512], BF16)
        goutT = av(152, [4, S], BF16)
        lrT = av(56, [S], BF16)

        def proj_fm(slot, col, ncols, evac):
            banks = [nb() for _ in range(4)]
            for k in range(8):
                for tg in range(4):
                    mm(psb[banks[tg]][0:ncols, :], wst[slot][:, k, col:col + ncols],
                       xT[:, k, 256 + tg * 512:256 + (tg + 1) * 512], k == 0, k == 7,
                       XTk(k) + ["W%d" % slot], [PS(banks[tg])])
            for tg in range(4):
                evac(tg, banks[tg])

        def proj_tm(slot, col, ncols, tok_ap_fn, evac):
            b = nb()
            for k in range(8):
                mm(psb[b][:, 0:ncols], tok_ap_fn(k), wst[slot][:, k, col:col + ncols], k == 0, k == 7,
                   XTk(k) + ["W%d" % slot], [PS(b)])
            evac(b)

        def proj_fm_part(slot, col, ncols, evac, tgs):
            banks = {tg: nb() for tg in tgs}
            for k in range(8):
                for tg in tgs:
                    mm(psb[banks[tg]][0:ncols, :], wst[slot][:, k, col:col + ncols],
                       xT[:, k, 256 + tg * 512:256 + (tg + 1) * 512], k == 0, k == 7,
                       XTk(k) + ["W%d" % slot], [PS(banks[tg])])
            for tg in tgs:
                evac(tg, banks[tg])

        s_lr = next_panel()
        memset("dve", lrT[32:33, :], 1.0, ["lrT1"])

        def ev_lr(tg, b):
            cp("act", lrT[0:32, tg * 512:(tg + 1) * 512], psb[b][0:32, :], [PS(b)], ["lrT"])
        proj_fm(s_lr, 0, 32, ev_lr)

        s_qk = next_panel()
        for c in range(4):
            dstT = gqT if c < 2 else gkT
            nm = "gqT" if c < 2 else "gkT"
            sc = 0.125 if c < 2 else 1.0

            def ev(tg, b, dstT=dstT, nm=nm, sc=sc, c=c):
                if (tg + c) % 2 == 0:
                    act(dstT[:, c % 2, tg * 512:(tg + 1) * 512], psb[b][:, :], AF.Copy, [PS(b)], [nm], scale=sc)
                else:
                    ts("dve", dstT[:, c % 2, tg * 512:(tg + 1) * 512], psb[b][:, :], sc, None, ALU.mult, None, [PS(b)], [nm])
            proj_fm(s_qk, c * 128, 128, ev)
        for t in range(16):
            def ev(b, t=t):
                cp("act" if t % 2 else "dve", gk_tm[:, t, :], psb[b][:, 0:256], [PS(b)], ["gk_tm"])
            proj_tm(s_qk, 256, 256, lambda k, t=t: xT[:, k, 256 + t * 128:256 + (t + 1) * 128], ev)

        gtmp = av(56 + 4, [512], F32)

        if STOP == "G1":
            P.emit()
            return nc
        sp_t2 = av(40, [2, 512], F32)
        e_pos2 = av(44, [2, 4, 128], F32)
        e_neg2 = av(48, [2, 4, 128], F32)
        e_k2 = av(52, [2, 2, 256], F32)
        def g2_bufs(t):
            g2 = t % 2
            return g2, sp_t2[:, g2, :], e_pos2[:, g2, :, :], e_neg2[:, g2, :, :], e_k2[:, g2, :, :]

        def g2_front(t):
            tok = slice(t * 128, (t + 1) * 128)
            g2, sp_t, e_pos, e_neg, e_k = g2_bufs(t)
            bz = nb()
            mm(psb[bz][:, :], lrT[0:33, tok], wlr[0:33, :], True, True, ["lrT", "lrT1", "wlr"], [PS(bz)])
            act(sp_t, psb[bz][:, :], AF.Exp, [PS(bz)], ["sp_t%d" % g2], scale=-1.0)
            act(sp_t, sp_t, AF.Ln, ["sp_t%d" % g2], ["sp_t%d" % g2], bias=1.0)

        def g2_back(t):
            tok = slice(t * 128, (t + 1) * 128)
            g2, sp_t, e_pos, e_neg, e_k = g2_bufs(t)
            bc_ = nb()
            for q in range(4):
                mm(psb[bc_][:, q * 128:(q + 1) * 128], sp_t[:, q * 128:(q + 1) * 128], c32[:, 0 if q < 2 else 1, :],
                   True, True, ["sp_t%d" % g2, "c32"], [PS(bc_)], signal=(q == 3))
            bk = nb()
            mm(psb[bk][:, 0:256], c32[:, 2, :], sp_t[:, 0:256], True, True, ["sp_t%d" % g2, "c32"], [PS(bk)], signal=False)
            mm(psb[bk][:, 256:512], c32[:, 3, :], sp_t[:, 256:512], True, True, ["sp_t%d" % g2, "c32"], [PS(bk)])
            pc4 = psb[bc_][:, :].rearrange("p (a b) -> p a b", a=4)
            act(e_pos, pc4, AF.Exp, [PS(bc_)], ["e_pos%d" % g2])
            act(e_neg, pc4, AF.Exp, [PS(bc_)], ["e_neg%d" % g2], scale=-1.0)
            act(e_k, psb[bk][:, :].rearrange("p (a b) -> p a b", a=2), AF.Exp, [PS(bk)], ["e_k%d" % g2])
            tt("dve", qfT[:, :, tok], gqT[:, :, tok], e_pos[:, 0:2, :], ALU.mult, ["gqT", "e_pos%d" % g2], ["qfT"])
            tt("dve", kfT[:, :, tok], gkT[:, :, tok], e_neg[:, 0:2, :], ALU.mult, ["gkT", "e_neg%d" % g2], ["kfT"])
            tt("dve", qbT[:, :, tok], gqT[:, :, tok], e_pos[:, 2:4, :], ALU.mult, ["gqT", "e_pos%d" % g2], ["qbT"])
            tt("dve", kbT[:, :, tok], gkT[:, :, tok], e_neg[:, 2:4, :], ALU.mult, ["gkT", "e_neg%d" % g2], ["kbT"])
            cp("dve", dec[:, 0, :, t], e_pos[:, 0:2, 127], ["e_pos%d" % g2], ["dec"])
            cp("dve", dec[:, 1, :, t], e_pos[:, 2:4, 0], ["e_pos%d" % g2], ["dec"])
            tt("dve", kdec[:, t, :, :], e_k, gk_tm[:, t:t + 1, :].to_broadcast([128, 2, 256]), ALU.mult,
               ["e_k%d" % g2, "gk_tm"], ["kdec"])

        g2_front(0)
        for t in range(16):
            if t + 1 < 16:
                g2_front(t + 1)
            g2_back(t)

        if STOP == "G2":
            P.emit()
            return nc
        S16 = av(64, [16, 2, 2, 128], BF16)
        S32 = av(80, [2, 2, 2, 128], F32)
        S16N = ["S16_%d_%d" % (t, d) for t in range(16) for d in range(2)]
        for nm in S16N:
            P.alias(nm, ["gqT", "gkT", "gk_tm"])
        memset("dve", S32[:, 0, 0, :, :], 0.0, ["S32_0_0"])
        memset("dve", S32[:, 0, 1, :, :], 0.0, ["S32_0_1"])
        am = av(84, [2, 8, 128], BF16)
        P.alias("am0", ["gk_tm"])
        P.alias("am1", ["gk_tm"])
        G2T = ["%s%d" % (n, i) for n in ("sp_t", "e_pos", "e_neg", "e_k") for i in range(2)]
        P.alias("y_glaT", G2T)

        def scan_step(i):
            tf = i
            tb = 15 - i
            b = nb()
            kvp = psb[b][:, :].rearrange("p (d j v) -> p d j v", d=2, j=2)
            for d, tsel in ((0, tf), (1, tb)):
                for j in range(2):
                    for hh in range(2):
                        h = 2 * j + hh
                        mm(kvp[hh * 64:(hh + 1) * 64, d, j, :], kdec[:, tsel, d, j * 128 + hh * 64:j * 128 + hh * 64 + 64],
                           gv[:, tsel, h * 128:(h + 1) * 128], True, True, ["kdec", "gv"], [PS(b)],
                           signal=(d == 1 and j == 1 and hh == 1))
            src, dst = i % 2, (i + 1) % 2
            for d, tsel, tout in ((0, tf, tf + 1), (1, tb, tb - 1)):
                for j in range(2):
                    stt("dve", S32[:, dst, d, j, :], S32[:, src, d, j, :], dec[:, d, j, tsel:tsel + 1], kvp[:, d, j, :],
                        ALU.mult, ALU.add, ["S32_%d_%d" % (src, d), "dec", PS(b)], ["S32_%d_%d" % (dst, d)])
                cp("act", S16[:, tout, d, :, :], S32[:, dst, d, :, :], ["S32_%d_%d" % (dst, d)], ["S16_%d_%d" % (tout, d)])

        sq3 = [av(56, [4, 128], BF16), av(57, [4, 128], BF16), av(58, [4, 128], BF16)]
        osb2 = [av(59, [4, 128], F32), av(61, [4, 128], F32)]
        for nm in ("sq0", "sq1", "sq2", "o_sb0", "o_sb1"):
            P.alias(nm, ["lrT", "lrT1", "gtmp"])
        g4bo = {}

        def g4_A(n, t):
            tok = slice(t * 128, (t + 1) * 128)
            ab = n % 2
            bA = [nb(), nb()]
            for hh in range(2):
                rows = slice(hh * 64, (hh + 1) * 64)
                for d, (kk, qq, knm, qnm) in enumerate(((kfT, qfT, "kfT", "qfT"), (kbT, qbT, "kbT", "qbT"))):
                    for j in range(2):
                        mm(psb[bA[hh]][:, (d * 2 + j) * 128:(d * 2 + j + 1) * 128], kk[rows, j, tok], qq[rows, j, tok],
                           True, True, [knm, qnm], [PS(bA[hh])], signal=(d == 1 and j == 1))
            for hh in range(2):
                tt("dve", am[:, ab, hh * 4:(hh + 1) * 4, :].rearrange("p (d j) c -> p d j c", d=2),
                   psb[bA[hh]][:, :].rearrange("p (d j c) -> p d j c", d=2, j=2),
                   c16[:, 0:256].rearrange("p (d c) -> p d c", d=2).unsqueeze(2).to_broadcast([128, 2, 2, 128]),
                   ALU.mult, [PS(bA[hh]), "c16"], ["am%d" % ab])

        def g4_o(n, t):
            tok = slice(t * 128, (t + 1) * 128)
            ab = n % 2
            bo = nb()
            g4bo[n] = bo
            for h in range(4):
                j, hh = h // 2, h % 2
                rows = slice(hh * 64, (hh + 1) * 64)
                outp = psb[bo][:, h * 128:(h + 1) * 128]
                parts = [(gv[:, t, h * 128:(h + 1) * 128], am[:, ab, hh * 4 + j, :], ["gv", "am%d" % ab]),
                         (gv[:, t, h * 128:(h + 1) * 128], am[:, ab, hh * 4 + 2 + j, :], ["gv", "am%d" % ab])]
                if t > 0:
                    parts.append((S16[rows, t, 0, j, :], qfT[rows, j, tok], ["S16_%d_0" % t, "qfT"]))
                if t < 15:
                    parts.append((S16[rows, t, 1, j, :], qbT[rows, j, tok], ["S16_%d_1" % t, "qbT"]))
                for pi, (l, r, rd) in enumerate(parts):
                    mm(outp, l, r, pi == 0, pi == len(parts) - 1, rd, [PS(bo)],
                       signal=(h == 3 and pi == len(parts) - 1))
            po4 = psb[bo][:, :].rearrange("p (h c) -> p h c", h=4)
            act(sq3[n % 3], po4, AF.Square, [PS(bo)], ["sq%d" % (n % 3)])
            cp("act", osb2[n % 2], po4, [PS(bo)], ["o_sb%d" % (n % 2)])

        def g4_fin(n, t):
            tok = slice(t * 128, (t + 1) * 128)
            sq, o_sb = sq3[n % 3], osb2[n % 2]
            bs = nb()
            mm(psb[bs][:, :], ones, sq.rearrange("p h c -> p (h c)"), True, True, ["sq%d" % (n % 3), "c16"], [PS(bs)])
            ps4 = psb[bs][:, :].rearrange("p (h c) -> p h c", h=4)
            act(ps4, ps4, AF.Ln, [PS(bs), "epst"], [PS(bs)], bias=epst[:, 1:2], scale=1.0 / 128.0)
            act(ps4, ps4, AF.Exp, [PS(bs)], [PS(bs)], scale=-0.5)
            tt("dve", o_sb, ps4, o_sb, ALU.mult, ["o_sb%d" % (n % 2), PS(bs)], ["o_sb%d" % (n % 2)])
            tt("dve", y_glaT[:, :, tok], o_sb, goutT[:, :, tok], ALU.mult, ["o_sb%d" % (n % 2), "goutT"], ["y_glaT"])

        s_gv = next_panel()
        s_go = next_panel(hold=1)
        go_slices = []
        for c in range(4):
            def ev_go(tg, b, c=c):
                act(gtmp, psb[b][:, :], AF.Silu, [PS(b)], ["gtmp"])
                ts("dve", goutT[:, c, tg * 512:(tg + 1) * 512], gtmp, gT[:, c:c + 1], None, ALU.mult, None,
                   ["gtmp", "gT"], ["goutT"])
            for tgs in ((0, 1), (2, 3)):
                go_slices.append(lambda c=c, ev_go=ev_go, tgs=tgs: proj_fm_part(s_go, c * 128, 128, ev_go, tgs))

        def gv_tile(t):
            def ev_gv(b):
                cp("dve" if t % 2 else "act", gv[:, t, :], psb[b][:, :], [PS(b)], ["gv"])
            proj_tm(s_gv, 0, 512, lambda k: xT[:, k, 256 + t * 128:256 + (t + 1) * 128], ev_gv)

        for i in range(8):
            gv_tile(i)
            gv_tile(15 - i)
            go_slices[i]()
            scan_step(i)
        for nm in ("sq0", "sq1", "sq2", "o_sb0", "o_sb1"):
            P.alias(nm, ["gtmp"])
        T = []
        for i in range(7, 15):
            T += [14 - i, i + 1]
        for n in range(16 + 2):
            if n % 2 == 0 and 8 + n // 2 <= 14:
                scan_step(8 + n // 2)
            if n < 16:
                g4_A(n, T[n])
            if 0 <= n - 1 < 16:
                g4_o(n - 1, T[n - 1])
            if 0 <= n - 2 < 16:
                g4_fin(n - 2, T[n - 2])

        if STOP == "G4":
            P.emit()
            return nc
        S16N_ = ["S16_%d_%d" % (t, d) for t in range(16) for d in range(2)]
        bias_st = av(64, [12, 2, 128], F32)
        mask_st = av(76, [3, 2, 128], F32)
        P.alias("bias_st", S16N_)
        P.alias("mask_st", S16N_)
        P.dma("sp", bias_st, bias_d, writes=["bias_st"], chan="bias")
        P.dma("sp", mask_st, mask_d, writes=["mask_st"], chan="mask")
        act(bias_st, bias_st, AF.Exp, ["bias_st"], ["bias_st"])
        for g in range(3):
            tt("dve", expb[:, 4 * g:4 * g + 4, :, :], bias_st[:, 4 * g:4 * g + 4, :, :],
               mask_st[:, g:g + 1, :, :].to_broadcast([128, 4, 2, 128]), ALU.mult,
               ["bias_st", "mask_st"], ["expb"])
        acc = av(64, [2, 2, S], F32)
        qT2 = [av(96, [2, S], BF16), av(140, [2, S], BF16)]
        kT2 = [av(104, [2, 2560], BF16), av(148, [2, 2560], BF16)]
        Vt2 = [av(114, [20, 256], BF16), av(158, [20, 256], BF16)]
        et_raw = av(124, [4, 8, 128], BF16)
        et = av(132, [4, 8, 128], BF16)
        ETN = ["et%d" % i for i in range(4)] + ["et_raw%d" % i for i in range(4)]
        QKV = ["qT0", "qT1", "kT0", "kT1", "kTpad0", "kTpad1", "Vt0", "Vt1"]
        P.alias("acc", S16N + ["S32_0_0", "S32_0_1", "S32_1_0", "S32_1_1", "am0", "am1", "qfT", "kfT", "bias_st", "mask_st"])
        P.alias("qT0", ["kfT", "qbT"])
        P.alias("kT0", ["qbT", "kbT", "kdec"])
        P.alias("kTpad0", ["qbT", "kbT", "kdec"])
        P.alias("Vt0", ["kbT", "kdec"])
        for nm in ("qT1", "kT1", "kTpad1", "Vt1"):
            P.alias(nm, ["gv", "goutT"])
        for nm in ETN:
            P.alias(nm, ["kdec", "gv"])
        y_attnT = av(56, [2, S], BF16)
        for q4 in range(4):
            P.alias("y_attnT%d" % q4, ["sq0", "sq1", "sq2", "o_sb0", "o_sb1", "gtmp", "lrT", "lrT1"])
        it = [0]

        def group_work(g, r):
            st_ = g % 2
            qT, kT, Vt = qT2[st_], kT2[st_], Vt2[st_]
            qn, kn, kpn, vn = "qT%d" % st_, "kT%d" % st_, "kTpad%d" % st_, "Vt%d" % st_
            L = S // r
            pad = 64 if g < 2 else 0
            Lp = L + 2 * pad
            nq = L // 128
            nk = 2 if g < 2 else 1
            ntile = {0: 17, 1: 5, 2: 1}[g]
            kTv = kT[:, :, 0:r * Lp].rearrange("p c (r m) -> p c r m", r=r)
            qTv = qT[:, :, :].rearrange("p c (r m) -> p c r m", r=r)
            slots = {}
            slices = []

            def sl_first():
                slots["qk"] = next_panel()
                if pad:
                    memset("dve", kTv[:, :, :, 0:pad], 0.0, [kpn])
                    memset("dve", kTv[:, :, :, pad + L:Lp], 0.0, [kpn])
            slices.append(sl_first)
            for c in range(2):
                def evq(tg, b, c=c):
                    src = psb[b][:, :].rearrange("p (m r) -> p r m", r=r)
                    dst = qTv[:, c, :, tg * 512 // r:(tg + 1) * 512 // r]
                    cp("act" if tg % 2 else "dve", dst, src, [PS(b)], [qn])

                def evk(tg, b, c=c):
                    src = psb[b][:, :].rearrange("p (m r) -> p r m", r=r)
                    dst = kTv[:, c, :, pad + tg * 512 // r:pad + (tg + 1) * 512 // r]
                    cp("dve" if tg % 2 else "act", dst, src, [PS(b)], [kn])
                for tgs in ((0, 1), (2, 3)):
                    slices.append(lambda c=c, evq=evq, tgs=tgs: proj_fm_part(slots["qk"], c * 128, 128, evq, tgs))
                    slices.append(lambda c=c, evk=evk, tgs=tgs: proj_fm_part(slots["qk"], 256 + c * 128, 128, evk, tgs))

            def sl_v0():
                slots["v"] = next_panel()
            slices.append(sl_v0)
            vts = [(cls, i) for cls in range(r) for i in range(ntile)]

            def v_tiles(lst):
                for (cls, i) in lst:
                    start = 256 + r * (128 * i - pad) + cls
                    vi = cls * ntile + i

                    def evv(b, vi=vi):
                        cp("act" if vi % 2 else "dve", Vt[:, vi, :], psb[b][:, 0:256], [PS(b)], [vn])
                    proj_tm(slots["v"], 0, 256,
                            lambda k, start=start: xT[:, k, start:start + 127 * r + 1:r], evv)
            for p0 in range(0, len(vts), 2):
                slices.append(lambda lst=vts[p0:p0 + 2]: v_tiles(lst))

            pairs = []

            U = 2 if g < 2 else 4

            def do_S(c, pair, eb):
                bS = [nb(), nb()]
                for hh in range(2):
                    rows = slice(hh * 64, (hh + 1) * 64)
                    spv = psb[bS[hh]][:, 0:U * nk * 128].rearrange("p (u k q) -> p u k q", u=U, k=nk)
                    for ui, (cls, qi) in enumerate(pair):
                        for kt in range(nk):
                            mm(spv[:, ui, kt, :], kTv[rows, c, cls, 128 * (qi + kt):128 * (qi + kt) + 128],
                               qTv[rows, c, cls, 128 * qi:128 * qi + 128], True, True, [kn, kpn, qn], [PS(bS[hh])],
                               signal=(ui == U - 1 and kt == nk - 1))
                for hh in range(2):
                    spv = psb[bS[hh]][:, 0:U * nk * 128].rearrange("p (u k q) -> p u k q", u=U, k=nk)
                    er = et_raw[:, eb, hh * 4:(hh + 1) * 4, :].rearrange("p (u k) q -> p u k q", u=U)
                    ee = et[:, eb, hh * 4:(hh + 1) * 4, :].rearrange("p (u k) q -> p u k q", u=U)
                    act(er, spv, AF.Exp, [PS(bS[hh])], ["et_raw%d" % eb], scale=0.125)
                    tt("dve", ee, er,
                       expb[:, 4 * g + 2 * c + hh, 0:nk, :].unsqueeze(1).to_broadcast([128, U, nk, 128]), ALU.mult,
                       ["et_raw%d" % eb, "expb"], ["et%d" % eb])

            def do_PV(c, pair, eb):
                accv = acc[:, c, :, :].rearrange("p n (m r) -> p n m r", r=r)
                bos = {}
                for ui, (cls, qi) in enumerate(pair):
                    if ui % 2 == 0:
                        bos[ui // 2] = nb()
                    bo = bos[ui // 2]
                    pv = psb[bo][:, (ui % 2) * 256:(ui % 2) * 256 + 256].rearrange("p (n q) -> p n q", n=2)
                    for hh in range(2):
                        rows = slice(hh * 64, (hh + 1) * 64)
                        hcol = (2 * c + hh) * 64
                        for n in range(2):
                            for kt in range(nk):
                                vi = cls * ntile + qi + kt
                                if n == 0:
                                    l = Vt[:, vi, hcol:hcol + 64]
                                elif g < 2 and qi + kt == 0:
                                    l = onesF
                                elif g < 2 and qi + kt == ntile - 1:
                                    l = onesL
                                else:
                                    l = ones[:, 0:64]
                                mm(pv[rows, n, :], l, et[:, eb, hh * 4 + ui * nk + kt, :], kt == 0, kt == nk - 1,
                                   [vn, "c16", "et%d" % eb], [PS(bo)],
                                   signal=(hh == 1 and n == 1 and kt == nk - 1))
                for ui, (cls, qi) in enumerate(pair):
                    bo = bos[ui // 2]
                    pv = psb[bo][:, (ui % 2) * 256:(ui % 2) * 256 + 256].rearrange("p (n q) -> p n q", n=2)
                    dst = accv[:, :, 128 * qi:128 * qi + 128, cls]
                    if g == 0:
                        cp("act", dst, pv, [PS(bo)], ["acc"])
                    else:
                        tt("dve", dst, pv, dst, ALU.add, [PS(bo), "acc"], ["acc"])

            units = [(cls, qi) for cls in range(r) for qi in range(nq)]
            for c in range(2):
                for up in range(0, len(units), U):
                    pr = units[up:up + U]
                    pairs.append((lambda eb, c=c, pr=pr: do_S(c, pr, eb), lambda eb, c=c, pr=pr: do_PV(c, pr, eb)))
            return slices, pairs

        work = [group_work(g, r) for g, r in enumerate((1, 4, 16))]
        for f in work[0][0]:
            f()
        wbr = av(144, [6, D], BF16)
        vec_ln1g = av(160, [D], F32)
        vec_ln1b = av(164, [D], F32)
        vec_ln2g = av(168, [D], F32)
        for g in range(3):
            pairs = work[g][1]
            nxt = work[g + 1][0] if g + 1 < 3 else []
            if g == 2:
                for nm in ("wbr", "vecs"):
                    P.alias(nm, ["qT1", "kT1", "kTpad1", "Vt1", "gv", "goutT"])
                P.dma("pool", wbr[:, 0:2, :], wab_d.rearrange("(k p) c -> p k c", p=128), writes=["wbr"], chan="wbr")
                P.dma("pool", wbr[:, 2:6, :], wgb_d.rearrange("(k p) c -> p k c", p=128), writes=["wbr"], chan="wbr")
                P.dma("sp", vec_ln1g, vec_d[:, 0, :], writes=["vecs"], chan="vecs")
                P.dma("sp", vec_ln1b, vec_d[:, 1, :], writes=["vecs"], chan="vecs")
                P.dma("sp", vec_ln2g, vec_d[:, 2, :], writes=["vecs"], chan="vecs")
            done = 0
            ebs = [(it[0] + i) % 4 for i in range(len(pairs))]
            it[0] += len(pairs)
            skew = 1 if nxt else 2
            for pi in range(min(skew, len(pairs))):
                pairs[pi][0](ebs[pi])
            for pi in range(len(pairs)):
                if pi + skew < len(pairs):
                    pairs[pi + skew][0](ebs[pi + skew])
                want = (pi + 1) * len(nxt) // len(pairs)
                while done < want:
                    nxt[done]()
                    done += 1
                pairs[pi][1](ebs[pi])

        if STOP == "A0":
            P.emit()
            return nc
        rden = av(96, [2, S], F32)
        for q4 in range(4):
            P.alias("rden%d" % q4, ["kfT", "qbT", "kbT", "qT0", "kT0", "kTpad0"])
        for q4 in range(4):
            qs = slice(q4 * 512, (q4 + 1) * 512)
            act(rden[:, :, qs], acc[:, :, 1, qs], AF.Ln, ["acc"], ["rden%d" % q4])
            act(rden[:, :, qs], rden[:, :, qs], AF.Exp, ["rden%d" % q4], ["rden%d" % q4], scale=-1.0)
            tt("dve", y_attnT[:, :, qs], acc[:, :, 0, qs], rden[:, :, qs], ALU.mult, ["acc", "rden%d" % q4], ["y_attnT%d" % q4])

        if debug:
            dtmp = av(112, [4, S], F32)
            P.alias("dtmp", ["gv", "kdec", "rden0", "rden1", "rden2", "rden3"] + ETN + QKV)
            cp("dve", dtmp[:, 0:2, :], y_attnT, ["y_attnT0", "y_attnT1", "y_attnT2", "y_attnT3"], ["dtmp"])
            P.dma("sp", dbg_d["d_yattn"], dtmp[:, 0:2, :], reads=["dtmp"], chan="dbg0")
            cp("dve", dtmp, y_glaT, ["y_glaT"], ["dtmp"])
            P.dma("sp", dbg_d["d_ygla"], dtmp, reads=["dtmp"], chan="dbg1")

        if STOP == "A":
            P.emit()
            return nc
        mergedT = av(64, [8, S], BF16)
        mtmp = av(128, [2, 2, 512], F32)
        sgt = av(136, [2, 2, 512], F32)
        for q4 in range(4):
            P.alias("mergedT%d" % q4, ["acc"])
        for nm in ("mtmp0", "mtmp1", "sgt0", "sgt1"):
            P.alias(nm, ["dtmp", "gv", "kdec"] + QKV + ETN)
        wout = av(96, [8, D], BF16)
        X1Bv = [av(112, [4, D], F32), av(144, [4, D], F32)]
        x1T = av(128, [2, 8, 512], BF16)
        vec_ln2b = av(58, [D], F32)
        hT = av(0, [32, 512], BF16)
        b2v = av(32, [D], F32)
        x1h = av(36, [2, D], BF16)
        rtmp4 = av(40, [2, 512], BF16)
        rtmp4b = av(62, [2, 512], BF16)
        rtmp = [rtmp4[:, 0, :], rtmp4[:, 1, :], rtmp4b[:, 0, :], rtmp4b[:, 1, :]]
        u2 = av(42, [4, D], F32)
        vecs = {0: (vec_ln1g, vec_ln1b), 2: (vec_ln2g, vec_ln2b)}
        P.alias("wout", QKV + ["rden0", "rden1", "rden2", "rden3"])
        X1B = ["x1b%d_%d" % (bf, i) for bf in range(2) for i in range(4)]
        for nm in X1B[0:4]:
            P.alias(nm, ["kdec", "dtmp"] + ETN + QKV)
        P.dma("pool", wout, wout_d.rearrange("(k p) c -> p k c", p=128), writes=["wout"], chan="wout")
        x_tiles = x_d.rearrange("(t p) d -> t p d", p=128)
        out_tiles = out_d.rearrange("(t p) d -> t p d", p=128)

        def layernorm(u_ap, unm, slot, gi, out_ap, onm):
            vg, vb = vecs[gi]
            for hf in range(2):
                P.op("dve", lambda e, hf=hf: e.bn_stats(stats[:, slot, hf, :], u_ap[:, hf * 512:(hf + 1) * 512]),
                     [unm], ["stats%d" % slot])
            P.op("dve", lambda e: e.bn_aggr(mv[:, slot, 0:2], stats[:, slot, :, :]), ["stats%d" % slot], ["mv%d" % slot])
            act(mv[:, slot, 2:3], mv[:, slot, 1:2], AF.Sqrt, ["mv%d" % slot, "epst"], ["mv%d" % slot], bias=epst[:, 0:1])
            P.op("dve", lambda e: e.reciprocal(mv[:, slot, 2:3], mv[:, slot, 2:3]), ["mv%d" % slot], ["mv%d" % slot])
            stt("dve", mv[:, slot, 3:4], mv[:, slot, 0:1], -1.0, mv[:, slot, 2:3], ALU.mult, ALU.mult,
                ["mv%d" % slot], ["mv%d" % slot])
            act(out_ap, u_ap, AF.Identity, [unm, "mv%d" % slot], [onm], bias=mv[:, slot, 3:4], scale=mv[:, slot, 2:3])
            tt("dve", out_ap, out_ap, vg, ALU.mult, [onm, "vecs", "vecs2"], [onm])
            tt("dve", out_ap, out_ap, vb, ALU.add, [onm, "vecs", "vecs2"], [onm])

        def F1a(tg, tiles, cast=True):
            bf = tg % 2
            for tt_ in tiles:
                t = tg * 4 + tt_
                xs = t % 2
                xnm = "x1b%d_%d" % (bf, tt_)
                xt = X1Bv[bf][:, tt_, :]
                P.dma("sp", xt, x_tiles[t], writes=[xnm], chan=xnm)
                bks = [nb(), nb()]
                for hf in range(2):
                    for j in range(8):
                        mm(psb[bks[hf]][:, :], mergedT[:, j, t * 128:(t + 1) * 128], wout[:, j, hf * 512:(hf + 1) * 512],
                           j == 0, j == 7, ["mergedT%d" % tg, "wout"], [PS(bks[hf])])
                for hf in range(2):
                    stt("dve", xt[:, hf * 512:(hf + 1) * 512], xt[:, hf * 512:(hf + 1) * 512], ALPHA,
                        psb[bks[hf]][:, :], ALU.mult, ALU.add, [xnm, PS(bks[hf])], [xnm])
                layernorm(xt, xnm, xs, 0, xt, xnm)
            if cast:
                F1a_cast(tg, tiles)

        def F1a_cast(tg, tiles):
            bf = tg % 2
            for tt_ in tiles:
                xs = (tg * 4 + tt_) % 2
                xnm = "x1b%d_%d" % (bf, tt_)
                xt = X1Bv[bf][:, tt_, :]
                cp("act", x1h[:, xs, :], xt, [xnm], ["x1h%d" % xs])
                stt("dve", xt, xt, ALPHA, b2v, ALU.mult, ALU.add, [xnm, "b2v"], [xnm])

        def F1b(tg, tiles):
            bf = tg % 2
            for tt_ in tiles:
                xs = (tg * 4 + tt_) % 2
                bt = nb()
                ptb = psb[bt][:, :].bitcast(BF16)
                for k in range(8):
                    P.op("pe", lambda e, k=k, xs=xs, ptb=ptb: e.transpose(ptb[:, k * 128:(k + 1) * 128],
                                                                         x1h[:, xs, k * 128:(k + 1) * 128], ident),
                         ["x1h%d" % xs, "c16"], [PS(bt)], signal=(k == 7))
                cp("act", x1T[:, bf, :, tt_ * 128:(tt_ + 1) * 128], ptb.rearrange("p (k c) -> p k c", k=8), [PS(bt)],
                   ["x1T%d" % bf])

        def F2(tg):
            bf = tg % 2
            for u in range(8):
                s1 = next_panel()
                for f4 in range(4):
                    fc = u * 4 + f4
                    b = nb()
                    for k in range(8):
                        mm(psb[b][:, :], wst[s1][:, k, f4 * 128:(f4 + 1) * 128], x1T[:, bf, k, :], k == 0, k == 7,
                           ["W%d" % s1, "x1T%d" % bf], [PS(b)])
                    rb = fc % 4
                    act(rtmp[rb], psb[b][:, :], AF.Relu, [PS(b), "b1T"], ["rtmp%d" % rb], bias=b1T[:, fc:fc + 1])
                    tt("dve", hT[:, fc, :], rtmp[rb], rtmp[rb], ALU.mult, ["rtmp%d" % rb], ["hT"])
                if tg > 0 and 2 <= u <= 5:
                    LN2(tg - 1, [u - 2])

        def F3h(tg, hf, before_last=None, tile_sets=((0, 1, 2, 3),), after_set=None):
            bf = tg % 2
            for si, tset in enumerate(tile_sets):
                banks = {tt_: nb() for tt_ in tset}
                for u in range(4):
                    if u == 3 and before_last is not None and si == len(tile_sets) - 1:
                        before_last()
                    s2 = next_panel()
                    for tt_ in tset:
                        for f8 in range(8):
                            fc = u * 8 + f8
                            mm(psb[banks[tt_]][:, :], hT[:, fc, tt_ * 128:(tt_ + 1) * 128], wst[s2][:, f8, :],
                               fc == 0, fc == 31, ["hT", "W%d" % s2], [PS(banks[tt_])])
                for tt_ in tset:
                    tt("dve", u2[:, tt_, hf * 512:(hf + 1) * 512], psb[banks[tt_]][:, :],
                       X1Bv[bf][:, tt_, hf * 512:(hf + 1) * 512], ALU.add,
                       [PS(banks[tt_]), "x1b%d_%d" % (bf, tt_)], ["u2_%d" % tt_])
                if after_set is not None:
                    after_set(tset)

        def LN2(tg, tiles):
            for tt_ in tiles:
                t = tg * 4 + tt_
                layernorm(u2[:, tt_, :], "u2_%d" % tt_, 2 + tt_ % 2, 2, u2[:, tt_, :], "u2_%d" % tt_)
                P.dma("sp", out_tiles[t], u2[:, tt_, :], reads=["u2_%d" % tt_], chan="ot%d" % tt_)


        mit = [0]
        for u in range(2):
            s_ga = next_panel()
            s_gg = next_panel(hold=1)
            order = [(jj, tg) for jj in range(4) for tg in range(4)] if u == 0 else \
                    [(jj, tg) for tg in range(4) for jj in range(4)]
            for (jj, tg) in order:
                if True:
                    j = u * 4 + jj
                    mb = mit[0] % 2
                    mit[0] += 1
                    tks = slice(tg * 512, (tg + 1) * 512)
                    b_ga, b_gg, b_za, b_zg = nb(), nb(), nb(), nb()
                    for (bq, sl) in ((b_ga, s_ga), (b_gg, s_gg)):
                        for k in range(8):
                            mm(psb[bq][:, :], wst[sl][:, k, jj * 128:(jj + 1) * 128],
                               xT[:, k, 256 + tg * 512:256 + (tg + 1) * 512], k == 0, k == 7,
                               XTk(k) + ["W%d" % sl], [PS(bq)])
                    for k in range(2):
                        mm(psb[b_za][:, :], wbr[:, k, j * 128:(j + 1) * 128], y_attnT[:, k, tks], k == 0, k == 1,
                           ["wbr", "y_attnT%d" % tg], [PS(b_za)])
                    for k in range(4):
                        mm(psb[b_zg][:, :], wbr[:, 2 + k, j * 128:(j + 1) * 128], y_glaT[:, k, tks], k == 0, k == 3,
                           ["wbr", "y_glaT"], [PS(b_zg)])
                    act(sgt[:, mb, 0, :], psb[b_ga][:, :], AF.Sigmoid, [PS(b_ga)], ["sgt%d" % mb])
                    act(sgt[:, mb, 1, :], psb[b_gg][:, :], AF.Sigmoid, [PS(b_gg)], ["sgt%d" % mb])
                    tt("dve", mtmp[:, mb, 0, :], psb[b_za][:, :], sgt[:, mb, 0, :], ALU.mult, [PS(b_za), "sgt%d" % mb], ["mtmp%d" % mb])
                    tt("dve", mtmp[:, mb, 1, :], psb[b_zg][:, :], sgt[:, mb, 1, :], ALU.mult, [PS(b_zg), "sgt%d" % mb], ["mtmp%d" % mb])
                    tt("dve", mergedT[:, j, tks], mtmp[:, mb, 0, :], mtmp[:, mb, 1, :], ALU.add, ["mtmp%d" % mb], ["mergedT%d" % tg])
                    if u == 1 and not debug and (tg, jj) in ((1, 0), (1, 3), (2, 2), (3, 1)):
                        F1a(0, ({(1, 0): 0, (1, 3): 1, (2, 2): 2, (3, 1): 3}[(tg, jj)],), cast=False)

        if debug:
            dtm = av(0, [8, S], F32)
            P.alias("dtm", XT + ["y_glaT"] + ["y_attnT0", "y_attnT1", "y_attnT2", "y_attnT3"])
            cp("dve", dtm, mergedT, ["mergedT0", "mergedT1", "mergedT2", "mergedT3"], ["dtm"])
            P.dma("sp", dbg_d["d_merged"], dtm, reads=["dtm"], chan="dbg2")

        if STOP == "M":
            P.emit()
            return nc
        for nm in X1B[4:8]:
            P.alias(nm, ["wbr", "gv", "goutT", "dtmp"] + QKV)
        for nm in ("x1T0", "x1T1"):
            P.alias(nm, ["mtmp0", "mtmp1", "sgt0", "sgt1", "gv", "kdec", "dtmp"] + ETN + QKV)
        for nm in ("hT", "b2v", "x1h0", "x1h1", "rtmp0", "rtmp1", "rtmp2", "rtmp3", "u2_0", "u2_1", "u2_2", "u2_3", "vecs2"):
            P.alias(nm, XT + ["y_glaT", "dtm"] + ["y_attnT0", "y_attnT1", "y_attnT2", "y_attnT3"])
        P.dma("sp", vec_ln2b, vec_d[:, 3, :], writes=["vecs2"], chan="vecs2")
        P.dma("sp", b2v, vec_d[:, 4, :], writes=["b2v"], chan="b2v")
        if debug:
            F1a(0, (0, 1, 2, 3), cast=False)
        for tl in ((0, 1), (2, 3)):
            F1a_cast(0, tl)
            F1b(0, tl)
        for tg in range(4):
            F2(tg)
            nxt = tg + 1 < 4
            if nxt:
                F1a(tg + 1, (0, 1))
            F3h(tg, 0)
            if nxt:
                F1b(tg + 1, (0, 1))
                F1a(tg + 1, (2, 3))
            if nxt:
                F3h(tg, 1, before_last=lambda tg=tg: F1b(tg + 1, (2, 3)))
            else:
                F3h(tg, 1, tile_sets=((0, 1), (2, 3)), after_set=lambda tset: LN2(3, list(tset)))

        P.emit()
    return nc


_NC_CACHE = {}


def _prep_shared(w_in, rel_bias, w_lr_fwd, b_lr_fwd, w_lr_bwd, b_lr_bwd, gla_norm_g, w_attn_branch,
                 w_gla_branch, w_out, ln1_g, ln1_b, w_ff1, b_ff1, w_ff2, b_ff2, ln2_g, ln2_b):
    f = lambda a: np.ascontiguousarray(np.asarray(a, dtype=np.float32))
    rel_bias = f(rel_bias)
    buckets, masks = _host_tables()
    bias = np.zeros((256, 12, 128), np.float32)
    for g in range(3):
        for hs in range(4):
            bias[:, 4 * g + hs, :] = rel_bias[buckets[g], 4 * g + hs]
    biasT = np.ascontiguousarray(bias.reshape(2, 128, 12, 128).transpose(1, 2, 0, 3))
    maskT = np.ascontiguousarray(np.stack(masks, 1).reshape(2, 128, 3, 128).transpose(1, 2, 0, 3))
    wlr = np.zeros((33, 512), np.float32)
    wlr[0:16, 0:256] = f(w_lr_fwd)[0]
    wlr[16:32, 256:512] = f(w_lr_bwd)[0]
    wlr[32, 0:256] = f(b_lr_fwd)[0]
    wlr[32, 256:512] = f(b_lr_bwd)[0]
    c32, c16 = _consts()
    vecs = np.stack([f(ln1_g)[0], f(ln1_b)[0], f(ln2_g)[0], f(ln2_b)[0], f(b_ff2)[0]], 0)
    vecs = np.ascontiguousarray(np.broadcast_to(vecs[None], (128, 5, D)))
    return {
        "w_in": f(w_in)[0], "wlr": wlr, "c32": c32, "c16": c16, "biasT": biasT, "maskT": maskT,
        "w_ab": f(w_attn_branch)[0], "w_gb": f(w_gla_branch)[0], "w_out": f(w_out)[0],
        "w_ff1": f(w_ff1)[0], "b1T": np.ascontiguousarray(f(b_ff1)[0].reshape(32, 128).T),
        "w_ff2": f(w_ff2)[0], "gT": np.ascontiguousarray(f(gla_norm_g)[0].reshape(4, 128).T),
        "vecs": vecs,
    }


def kernel(x, **params):
    debug = bool(os.environ.get("MK_DEBUG"))
    x = np.asarray(x, dtype=np.float32)
    B = x.shape[0]
    shared = _prep_shared(**params)
    key = debug
    if key not in _NC_CACHE:
        _NC_CACHE[key] = build_nc(debug)
    nc = _NC_CACHE[key]
    in_maps = []
    for b in range(B):
        m = dict(shared)
        m["x"] = np.ascontiguousarray(x[b])
        m["xT"] = np.ascontiguousarray(x[b].T)
        in_maps.append(m)
    res = run_bass_kernel_spmd(nc, in_maps, core_ids=list(range(B)))
    out = np.stack([np.asarray(r["out"], dtype=np.float32) for r in res.results], 0)
    if debug:
        kernel.dbg = [{k: np.asarray(v) for k, v in r.items() if k.startswith("d_")} for r in res.results]
    return out
```

```python
import contextlib
import os
import numpy as np
import concourse.bass as bass
import concourse.mybir as mybir
from concourse.bass_utils import run_bass_kernel_spmd

F32 = mybir.dt.float32
BF16 = mybir.dt.bfloat16
AF = mybir.ActivationFunctionType
ALU = mybir.AluOpType

S = 2048
D = 1024
DFF = 4096
NCOL = 5920
ALPHA = 2.0 ** 0.25
LN_EPS = 1e-5
NORM_EPS = 1e-6
NSLOT = 3
SAME_ENGINE_WAR_SYNC = True
KB = 1024

ENGS = ("pe", "act", "dve", "pool", "sp")


class Prog:
    def __init__(self, nc):
        self.nc = nc
        self.ops = []
        self.lastw = {}
        self.readers = {}

    def op(self, eng, fn, reads=(), writes=(), signal=True, chan=None):
        oid = len(self.ops)
        deps = set()
        for r in reads:
            w = self.lastw.get(r)
            if w is not None:
                deps.add((w, "raw"))
        for r in writes:
            w = self.lastw.get(r)
            if w is not None:
                deps.add((w, "waw"))
            for rd in self.readers.get(r, ()):
                deps.add((rd, "war"))
        for r in reads:
            self.readers.setdefault(r, []).append(oid)
        for r in writes:
            self.lastw[r] = oid
            self.readers[r] = []
        self.ops.append(dict(id=oid, eng=eng, fn=fn, deps=deps, signal=signal, chan=chan))
        return oid

    def alias(self, new, olds):
        rs = list(self.readers.get(new, []))
        w = self.lastw.get(new)
        if w is not None:
            rs.append(w)
        for o in olds:
            rs.extend(self.readers.get(o, []))
            w = self.lastw.get(o)
            if w is not None:
                rs.append(w)
        self.readers[new] = rs
        self.lastw.pop(new, None)

    def dma(self, eng, out, in_, reads=(), writes=(), chan=None):
        assert chan is not None
        return self.op(eng, lambda e: e.dma_start(out=out, in_=in_), reads, writes, True, chan)

    def finalize(self):
        ops = self.ops
        cnt = {}
        last_of = {}
        for o in ops:
            if o["chan"] is None:
                last_of[o["eng"]] = o["id"]
        for e, i in last_of.items():
            ops[i]["signal"] = True
        import bisect
        sig_ids = {}
        for o in ops:
            if o["chan"] is None and o["signal"]:
                sig_ids.setdefault(o["eng"], []).append(o["id"])
        for o in ops:
            for (d, kind) in o["deps"]:
                do = ops[d]
                if do["chan"] is not None or do["signal"]:
                    continue
                lst = sig_ids.setdefault(do["eng"], [])
                p = bisect.bisect_left(lst, d)
                if p >= len(lst) or lst[p] >= o["id"]:
                    do["signal"] = True
                    bisect.insort(lst, d)
        pending = {}
        for o in ops:
            if o["chan"] is not None:
                k = "ch_" + o["chan"]
                cnt[k] = cnt.get(k, 0) + 16
                o["tok"] = (k, cnt[k])
            else:
                k = "e_" + o["eng"]
                pending.setdefault(k, []).append(o)
                if o["signal"]:
                    cnt[k] = cnt.get(k, 0) + 1
                    for p in pending[k]:
                        p["tok"] = (k, cnt[k])
                    pending[k] = []
        self.semkeys = sorted(cnt.keys())
        self.cnt = cnt
        chan_hist = {}
        for o in ops:
            if o["chan"] is not None:
                chan_hist.setdefault(o["tok"][0], []).append((o["id"], o["tok"][1]))
        known = {e: {} for e in ENGS}
        snap = {}
        per_eng = {e: [] for e in ENGS}
        for o in ops:
            e = o["eng"]
            kn = known[e]
            need = {}
            for (d, kind) in o["deps"]:
                do = ops[d]
                same = (do["eng"] == e) and do["chan"] is None and o["chan"] is None
                if same and kind != "raw":
                    if e != "pe" and kn.get(do["tok"][0], 0) < do["tok"][1]:
                        self.n_same_war = getattr(self, "n_same_war", 0) + 1
                        if SAME_ENGINE_WAR_SYNC:
                            k, v = do["tok"]
                            if v > need.get(k, (0, None))[0]:
                                need[k] = (v, d)
                    continue
                k, v = do["tok"]
                if v > need.get(k, (0, None))[0]:
                    need[k] = (v, d)
            waits = {}
            for k, (v, d) in need.items():
                if k.startswith("ch_"):
                    lst = chan_hist[k]
                    p = bisect.bisect_left(lst, (o["id"], 0)) - 1
                    v = max(v, lst[p][1])
                if kn.get(k, 0) >= v:
                    continue
                waits[k] = v
                kn[k] = v
                for kk, vv in snap[d].items():
                    if kn.get(kk, 0) < vv:
                        kn[kk] = vv
            o["waits"] = waits
            snap[o["id"]] = dict(kn)
            per_eng[e].append(o)
        return per_eng

    def emit(self):
        nc = self.nc
        per_eng = self.finalize()
        with contextlib.ExitStack() as st:
            sems = {k: st.enter_context(nc.semaphore(k)) for k in self.semkeys}
            block = st.enter_context(nc.Block())

            def run(ename):
                def body(eng):
                    for o in per_eng[ename]:
                        for k, v in o["waits"].items():
                            eng.wait_ge(sems[k], v)
                        ins = o["fn"](eng)
                        if o["chan"] is not None:
                            ins.then_inc(sems[o["tok"][0]], 16)
                        elif o["signal"]:
                            ins.then_inc(sems[o["tok"][0]], 1)
                    if ename == "sp":
                        for k in self.semkeys:
                            if k.startswith("ch_"):
                                eng.wait_ge(sems[k], self.cnt[k])
                return body

            block.sync(run("sp"))
            block.gpsimd(run("pool"))
            block.scalar(run("act"))
            block.vector(run("dve"))
            block.tensor(run("pe"))


def _t5_bucket(rel):
    nb = 16
    max_exact = 8
    ret = (rel > 0).astype(np.int32) * nb
    n = np.abs(rel)
    large = max_exact + (np.log(np.maximum(n, 1) / max_exact) / np.log(1024 / max_exact) * (nb - max_exact)).astype(np.int32)
    large = np.minimum(large, nb - 1)
    return (ret + np.where(n < max_exact, n, large)).astype(np.int32)


def _host_tables():
    j = np.arange(256)[:, None]
    i = np.arange(128)[None, :]
    buckets, masks = [], []
    for g, dil in enumerate((1, 4, 16)):
        delta = (j - 64 - i) if g < 2 else (j - i)
        m = (np.abs(delta) <= 64)
        if g == 2:
            m = m & (j < 128)
        buckets.append(_t5_bucket(delta * dil))
        masks.append(m.astype(np.float32))
    return buckets, masks


def _consts():
    s = np.arange(128)[:, None]
    c = np.arange(128)[None, :]
    v = -1.0 / 16.0
    c32 = np.stack([
        np.where(s <= c, v, 0.0), np.where(s >= c, v, 0.0),
        np.where(s > c, v, 0.0), np.where(s < c, v, 0.0)], 0).astype(np.float32)
    c32 = np.ascontiguousarray(c32.transpose(1, 0, 2))
    maskF = (s <= c).astype(np.float32)
    maskB = (s > c).astype(np.float32)
    ident = np.eye(128, dtype=np.float32)
    ones = np.ones((128, 128), np.float32)
    onesF = np.ones((128, 64), np.float32); onesF[0:64] = 0
    onesL = np.ones((128, 64), np.float32); onesL[64:128] = 0
    c16 = np.concatenate([maskF, maskB, ident, ones, onesF, onesL], 1)
    return c32, np.ascontiguousarray(c16)


def build_nc(debug=False):
    STOP = os.environ.get("MK_STOP", "")
    nc = bass.Bass("TRN2", target_bir_lowering=False)

    def din(name, shape):
        return nc.dram_tensor(name, list(shape), F32, kind="ExternalInput").ap()

    xT_d = din("xT", [D, S])
    x_d = din("x", [S, D])
    win_d = din("w_in", [D, NCOL])
    wlr_d = din("wlr", [33, 512])
    c32_d = din("c32", [128, 4, 128])
    c16_d = din("c16", [128, 640])
    bias_d = din("biasT", [128, 12, 2, 128])
    mask_d = din("maskT", [128, 3, 2, 128])
    wab_d = din("w_ab", [256, D])
    wgb_d = din("w_gb", [512, D])
    wout_d = din("w_out", [D, D])
    w1_d = din("w_ff1", [D, DFF])
    b1_d = din("b1T", [128, 32])
    w2_d = din("w_ff2", [DFF, D])
    gT_d = din("gT", [128, 4])
    vec_d = din("vecs", [128, 5, D])
    out_d = nc.dram_tensor("out", [S, D], F32, kind="ExternalOutput").ap()
    dbg_d = {}
    if debug:
        for nm, shp in (("d_yattn", [128, 2, S]), ("d_ygla", [128, 4, S]), ("d_merged", [128, 8, S])):
            dbg_d[nm] = nc.dram_tensor(nm, shp, F32, kind="ExternalOutput").ap()

    with contextlib.ExitStack() as st:
        def sb(name, shape, dt):
            return st.enter_context(nc.sbuf_tensor("sb_" + name, list(shape), dt))

        ARENA_KB = 172
        arena = sb("arena", [128, ARENA_KB * KB // 2], BF16)

        def av(off_kb, shape, dt):
            n = int(np.prod(shape))
            nb = n * (4 if dt == F32 else 2)
            o = int(off_kb * KB)
            assert o + nb <= ARENA_KB * KB, (off_kb, shape)
            ap = arena[:, o // 2:(o + nb) // 2]
            if dt == F32:
                ap = ap.bitcast(F32)
            if len(shape) == 2:
                ap = ap.rearrange("p (a b) -> p a b", a=shape[0])
            elif len(shape) == 3:
                ap = ap.rearrange("p (a b c) -> p a b c", a=shape[0], b=shape[1])
            elif len(shape) == 4:
                ap = ap.rearrange("p (a b c d) -> p a b c d", a=shape[0], b=shape[1], c=shape[2])
            return ap

        wst = [sb("wst%d" % i, [128, 8, 512], BF16) for i in range(NSLOT)]
        c32 = sb("c32", [128, 4, 128], F32)
        c16 = sb("c16", [128, 640], BF16)
        expb = sb("expb", [128, 12, 2, 128], BF16)
        wlr = sb("wlr", [33, 512], BF16)
        b1T = sb("b1T", [128, 32], F32)
        gT = sb("gT", [128, 4], F32)
        dec = sb("dec", [128, 2, 2, 16], F32)
        stats = sb("stats", [128, 4, 2, 6], F32)
        mv = sb("mv", [128, 4, 4], F32)
        epst = sb("epst", [128, 2], F32)
        psb = [st.enter_context(nc.psum_tensor("ps%d" % i, [128, 512], F32)) for i in range(8)]

        maskF = c16[:, 0:128]
        maskB = c16[:, 128:256]
        ident = c16[:, 256:384]
        ones = c16[:, 384:512]
        onesF = c16[:, 512:576]
        onesL = c16[:, 576:640]

        P = Prog(nc)
        bank_ctr = [0]

        def nb():
            b = bank_ctr[0] % 8
            bank_ctr[0] += 1
            return b

        def PS(b):
            return "ps%d" % b

        def mm(out, lhsT, rhs, start, stop, reads, writes, signal=None):
            P.op("pe", lambda e: e.matmul(out, lhsT, rhs, start=start, stop=stop), reads, writes,
                 signal=(stop if signal is None else signal))

        def act(out, in_, func, reads, writes, bias=None, scale=None):
            kw = {}
            if bias is not None:
                kw["bias"] = bias
            if scale is not None:
                kw["scale"] = scale
            P.op("act", lambda e: e.activation(out, in_, func, **kw), reads, writes)

        def tt(eng, out, in0, in1, op, reads, writes):
            P.op(eng, lambda e: e.tensor_tensor(out, in0, in1, op), reads, writes)

        def ts(eng, out, in0, s1, s2, op0, op1, reads, writes):
            if op1 is None:
                P.op(eng, lambda e: e.tensor_scalar(out, in0, s1, None, op0), reads, writes)
            else:
                P.op(eng, lambda e: e.tensor_scalar(out, in0, s1, s2, op0, op1), reads, writes)

        def stt(eng, out, in0, scalar, in1, op0, op1, reads, writes):
            P.op(eng, lambda e: e.scalar_tensor_tensor(out, in0, scalar, in1, op0, op1), reads, writes)

        def cp(eng, out, in_, reads, writes):
            if eng == "act":
                P.op("act", lambda e: e.activation(out, in_, AF.Copy), reads, writes)
            else:
                P.op(eng, lambda e: e.tensor_copy(out, in_), reads, writes)

        def memset(eng, ap, val, writes):
            P.op(eng, lambda e: e.memset(ap, val), (), writes)

        panel_seq = []
        ptr = {"use": 0, "issue": 0}

        def win_panel(c0, n):
            return win_d[:, c0:c0 + n].rearrange("(k p) c -> p k c", p=128)

        def issue_panels(hold=0):
            while ptr["issue"] < min(len(panel_seq), ptr["use"] + NSLOT - hold):
                s = ptr["issue"] % NSLOT
                for (off, n, src) in panel_seq[ptr["issue"]]:
                    P.dma("pool", wst[s][:, :, off:off + n], src, writes=["W%d" % s], chan="W%d" % s)
                ptr["issue"] += 1

        def next_panel(hold=0):
            issue_panels(hold)
            s = ptr["use"] % NSLOT
            ptr["use"] += 1
            return s

        panel_seq.append([(0, 32, win_panel(3840, 32))])
        panel_seq.append([(0, 512, win_panel(2304, 512))])
        panel_seq.append([(0, 512, win_panel(2816, 512))])
        panel_seq.append([(0, 512, win_panel(3328, 512))])
        for g in range(3):
            panel_seq.append([(0, 256, win_panel(g * 256, 256)), (256, 256, win_panel(768 + g * 256, 256))])
            panel_seq.append([(0, 256, win_panel(1536 + g * 256, 256))])
        for u in range(2):
            panel_seq.append([(0, 512, win_panel(3872 + u * 512, 512))])
            panel_seq.append([(0, 512, win_panel(4896 + u * 512, 512))])
        for tg in range(4):
            for u in range(8):
                panel_seq.append([(0, 512, w1_d[:, u * 512:(u + 1) * 512].rearrange("(k p) c -> p k c", p=128))])
            for half in range(2):
                for rep in range(2 if (tg == 3 and half == 1) else 1):
                    for u in range(4):
                        panel_seq.append([(0, 512, w2_d[u * 1024:(u + 1) * 1024, half * 512:(half + 1) * 512]
                                           .rearrange("(f p) c -> p f c", p=128))])

        P.dma("sp", c32[:, :, :], c32_d, writes=["c32"], chan="c32")
        P.dma("sp", b1T[:, :], b1_d, writes=["b1T"], chan="small")
        P.dma("sp", gT[:, :], gT_d, writes=["gT"], chan="gT")
        memset("dve", epst[:, 0:1], LN_EPS, ["epst"])
        memset("dve", epst[:, 1:2], NORM_EPS, ["epst"])

        xT = av(0, [8, 2560], BF16)
        memset("dve", xT[:, :, 0:256], 0.0, ["xTpadL"])
        memset("dve", xT[:, :, 2304:2560], 0.0, ["xTpadR"])
        issue_panels(hold=2)
        xT_src = xT_d.rearrange("(k p) t -> p k t", p=128)
        for k in range(8):
            P.dma("pool", xT[:, k, 256:2304], xT_src[:, k, :], writes=["xT%d" % k], chan="xT%d" % k)
            if k == 1:
                issue_panels(hold=1)
        XT = ["xT%d" % k for k in range(8)] + ["xTpadL", "xTpadR"]

        def XTk(k):
            return ["xT%d" % k, "xTpadL", "xTpadR"]
        P.dma("pool", wlr[:, :], wlr_d, writes=["wlr"], chan="wlr")
        P.dma("pool", c16[:, :], c16_d, writes=["c16"], chan="c16")

        if STOP == "INIT":
            P.emit()
            return nc
        y_glaT = av(40, [4, S], BF16)
        gqT = av(64, [2, S], BF16)
        gkT = av(72, [2, S], BF16)
        gk_tm = av(80, [16, 256], BF16)
        qfT = av(88, [2, S], BF16)
        kfT = av(96, [2, S], BF16)
        qbT = av(104, [2, S], BF16)
        kbT = av(112, [2, S], BF16)
        kdec = av(120, [16, 2, 256], BF16)
        gv = av(136, [16, 512], BF16)
        goutT = av(152, [4, S], BF16)
        lrT = av(56, [S], BF16)

        def proj_fm(slot, col, ncols, evac):
            banks = [nb() for _ in range(4)]
            for k in range(8):
                for tg in range(4):
                    mm(psb[banks[tg]][0:ncols, :], wst[slot][:, k, col:col + ncols],
                       xT[:, k, 256 + tg * 512:256 + (tg + 1) * 512], k == 0, k == 7,
                       XTk(k) + ["W%d" % slot], [PS(banks[tg])])
            for tg in range(4):
                evac(tg, banks[tg])

        def proj_tm(slot, col, ncols, tok_ap_fn, evac):
            b = nb()
            for k in range(8):
                mm(psb[b][:, 0:ncols], tok_ap_fn(k), wst[slot][:, k, col:col + ncols], k == 0, k == 7,
                   XTk(k) + ["W%d" % slot], [PS(b)])
            evac(b)

        def proj_fm_part(slot, col, ncols, evac, tgs):
            banks = {tg: nb() for tg in tgs}
            for k in range(8):
                for tg in tgs:
                    mm(psb[banks[tg]][0:ncols, :], wst[slot][:, k, col:col + ncols],
                       xT[:, k, 256 + tg * 512:256 + (tg + 1) * 512], k == 0, k == 7,
                       XTk(k) + ["W%d" % slot], [PS(banks[tg])])
            for tg in tgs:
                evac(tg, banks[tg])

        s_lr = next_panel()
        memset("dve", lrT[32:33, :], 1.0, ["lrT1"])

        def ev_lr(tg, b):
            cp("act", lrT[0:32, tg * 512:(tg + 1) * 512], psb[b][0:32, :], [PS(b)], ["lrT"])

        s_qk = next_panel(hold=1)

        def make_ev_qk(c):
            dstT = gqT if c < 2 else gkT
            nm = "gqT" if c < 2 else "gkT"
            sc = 0.125 if c < 2 else 1.0

            def ev(tg, b):
                if (tg + c) % 2 == 0:
                    act(dstT[:, c % 2, tg * 512:(tg + 1) * 512], psb[b][:, :], AF.Copy, [PS(b)], [nm], scale=sc)
                else:
                    ts("dve", dstT[:, c % 2, tg * 512:(tg + 1) * 512], psb[b][:, :], sc, None, ALU.mult, None, [PS(b)], [nm])
            return ev
        bl = [nb() for _ in range(4)]
        bq = [nb() for _ in range(4)]
        for k in range(8):
            for tg in range(4):
                mm(psb[bl[tg]][0:32, :], wst[s_lr][:, k, 0:32], xT[:, k, 256 + tg * 512:256 + (tg + 1) * 512],
                   k == 0, k == 7, XTk(k) + ["W%d" % s_lr], [PS(bl[tg])])
            for tg in range(4):
                mm(psb[bq[tg]][:, :], wst[s_qk][:, k, 0:128], xT[:, k, 256 + tg * 512:256 + (tg + 1) * 512],
                   k == 0, k == 7, XTk(k) + ["W%d" % s_qk], [PS(bq[tg])])
        for tg in range(4):
            ev_lr(tg, bl[tg])
        issue_panels(hold=1)
        ev0 = make_ev_qk(0)
        for tg in range(4):
            ev0(tg, bq[tg])
        for c in range(1, 4):
            proj_fm(s_qk, c * 128, 128, make_ev_qk(c))
        for t in range(16):
            def ev(b, t=t):
                cp("act" if t % 2 else "dve", gk_tm[:, t, :], psb[b][:, 0:256], [PS(b)], ["gk_tm"])
            proj_tm(s_qk, 256, 256, lambda k, t=t: xT[:, k, 256 + t * 128:256 + (t + 1) * 128], ev)

        gtmp = av(56 + 4, [512], F32)

        if STOP == "G1":
            P.emit()
            return nc
        sp_t2 = av(40, [2, 512], F32)
        e_pos2 = av(44, [2, 4, 128], F32)
        e_neg2 = av(48, [2, 4, 128], F32)
        e_k2 = av(52, [2, 2, 256], F32)
        def g2_bufs(t):
            g2 = t % 2
            return g2, sp_t2[:, g2, :], e_pos2[:, g2, :, :], e_neg2[:, g2, :, :], e_k2[:, g2, :, :]

        def g2_front(t):
            tok = slice(t * 128, (t + 1) * 128)
            g2, sp_t, e_pos, e_neg, e_k = g2_bufs(t)
            bz = nb()
            mm(psb[bz][:, :], lrT[0:33, tok], wlr[0:33, :], True, True, ["lrT", "lrT1", "wlr"], [PS(bz)])
            act(sp_t, psb[bz][:, :], AF.Exp, [PS(bz)], ["sp_t%d" % g2], scale=-1.0)
            act(sp_t, sp_t, AF.Ln, ["sp_t%d" % g2], ["sp_t%d" % g2], bias=1.0)

        def g2_back(t):
            tok = slice(t * 128, (t + 1) * 128)
            g2, sp_t, e_pos, e_neg, e_k = g2_bufs(t)
            bc_ = nb()
            for q in range(4):
                mm(psb[bc_][:, q * 128:(q + 1) * 128], sp_t[:, q * 128:(q + 1) * 128], c32[:, 0 if q < 2 else 1, :],
                   True, True, ["sp_t%d" % g2, "c32"], [PS(bc_)], signal=(q == 3))
            bk = nb()
            mm(psb[bk][:, 0:256], c32[:, 2, :], sp_t[:, 0:256], True, True, ["sp_t%d" % g2, "c32"], [PS(bk)], signal=False)
            mm(psb[bk][:, 256:512], c32[:, 3, :], sp_t[:, 256:512], True, True, ["sp_t%d" % g2, "c32"], [PS(bk)])
            pc4 = psb[bc_][:, :].rearrange("p (a b) -> p a b", a=4)
            act(e_pos, pc4, AF.Exp, [PS(bc_)], ["e_pos%d" % g2])
            act(e_neg, pc4, AF.Exp, [PS(bc_)], ["e_neg%d" % g2], scale=-1.0)
            act(e_k, psb[bk][:, :].rearrange("p (a b) -> p a b", a=2), AF.Exp, [PS(bk)], ["e_k%d" % g2])
            tt("dve", qfT[:, :, tok], gqT[:, :, tok], e_pos[:, 0:2, :], ALU.mult, ["gqT", "e_pos%d" % g2], ["qfT"])
            tt("dve", kfT[:, :, tok], gkT[:, :, tok], e_neg[:, 0:2, :], ALU.mult, ["gkT", "e_neg%d" % g2], ["kfT"])
            tt("dve", qbT[:, :, tok], gqT[:, :, tok], e_pos[:, 2:4, :], ALU.mult, ["gqT", "e_pos%d" % g2], ["qbT"])
            tt("dve", kbT[:, :, tok], gkT[:, :, tok], e_neg[:, 2:4, :], ALU.mult, ["gkT", "e_neg%d" % g2], ["kbT"])
            cp("dve", dec[:, 0, :, t], e_pos[:, 0:2, 127], ["e_pos%d" % g2], ["dec"])
            cp("dve", dec[:, 1, :, t], e_pos[:, 2:4, 0], ["e_pos%d" % g2], ["dec"])
            tt("dve", kdec[:, t, :, :], e_k, gk_tm[:, t:t + 1, :].to_broadcast([128, 2, 256]), ALU.mult,
               ["e_k%d" % g2, "gk_tm"], ["kdec"])

        g2_front(0)
        for t in range(16):
            if t + 1 < 16:
                g2_front(t + 1)
            g2_back(t)

        if STOP == "G2":
            P.emit()
            return nc
        S16 = av(64, [16, 2, 2, 128], BF16)
        S32 = av(80, [2, 2, 2, 128], F32)
        S16N = ["S16_%d_%d" % (t, d) for t in range(16) for d in range(2)]
        for nm in S16N:
            P.alias(nm, ["gqT", "gkT", "gk_tm"])
        memset("dve", S32[:, 0, 0, :, :], 0.0, ["S32_0_0"])
        memset("dve", S32[:, 0, 1, :, :], 0.0, ["S32_0_1"])
        am = av(84, [2, 8, 128], BF16)
        P.alias("am0", ["gk_tm"])
        P.alias("am1", ["gk_tm"])
        G2T = ["%s%d" % (n, i) for n in ("sp_t", "e_pos", "e_neg", "e_k") for i in range(2)]
        P.alias("y_glaT", G2T)

        def scan_step(i):
            tf = i
            tb = 15 - i
            b = nb()
            kvp = psb[b][:, :].rearrange("p (d j v) -> p d j v", d=2, j=2)
            for d, tsel in ((0, tf), (1, tb)):
                for j in range(2):
                    for hh in range(2):
                        h = 2 * j + hh
                        mm(kvp[hh * 64:(hh + 1) * 64, d, j, :], kdec[:, tsel, d, j * 128 + hh * 64:j * 128 + hh * 64 + 64],
                           gv[:, tsel, h * 128:(h + 1) * 128], True, True, ["kdec", "gv"], [PS(b)],
                           signal=(d == 1 and j == 1 and hh == 1))
            src, dst = i % 2, (i + 1) % 2
            for d, tsel, tout in ((0, tf, tf + 1), (1, tb, tb - 1)):
                for j in range(2):
                    stt("dve", S32[:, dst, d, j, :], S32[:, src, d, j, :], dec[:, d, j, tsel:tsel + 1], kvp[:, d, j, :],
                        ALU.mult, ALU.add, ["S32_%d_%d" % (src, d), "dec", PS(b)], ["S32_%d_%d" % (dst, d)])
                cp("act", S16[:, tout, d, :, :], S32[:, dst, d, :, :], ["S32_%d_%d" % (dst, d)], ["S16_%d_%d" % (tout, d)])

        sq3 = [av(56, [4, 128], BF16), av(57, [4, 128], BF16), av(58, [4, 128], BF16)]
        osb2 = [av(59, [4, 128], F32), av(61, [4, 128], F32)]
        for nm in ("sq0", "sq1", "sq2", "o_sb0", "o_sb1"):
            P.alias(nm, ["lrT", "lrT1", "gtmp"])
        g4bo = {}

        def g4_A(n, t):
            tok = slice(t * 128, (t + 1) * 128)
            ab = n % 2
            bA = [nb(), nb()]
            for hh in range(2):
                rows = slice(hh * 64, (hh + 1) * 64)
                for d, (kk, qq, knm, qnm) in enumerate(((kfT, qfT, "kfT", "qfT"), (kbT, qbT, "kbT", "qbT"))):
                    for j in range(2):
                        mm(psb[bA[hh]][:, (d * 2 + j) * 128:(d * 2 + j + 1) * 128], kk[rows, j, tok], qq[rows, j, tok],
                           True, True, [knm, qnm], [PS(bA[hh])], signal=(d == 1 and j == 1))
            for hh in range(2):
                tt("dve", am[:, ab, hh * 4:(hh + 1) * 4, :].rearrange("p (d j) c -> p d j c", d=2),
                   psb[bA[hh]][:, :].rearrange("p (d j c) -> p d j c", d=2, j=2),
                   c16[:, 0:256].rearrange("p (d c) -> p d c", d=2).unsqueeze(2).to_broadcast([128, 2, 2, 128]),
                   ALU.mult, [PS(bA[hh]), "c16"], ["am%d" % ab])

        def g4_o(n, t):
            tok = slice(t * 128, (t + 1) * 128)
            ab = n % 2
            bo = nb()
            g4bo[n] = bo
            for h in range(4):
                j, hh = h // 2, h % 2
                rows = slice(hh * 64, (hh + 1) * 64)
                outp = psb[bo][:, h * 128:(h + 1) * 128]
                parts = [(gv[:, t, h * 128:(h + 1) * 128], am[:, ab, hh * 4 + j, :], ["gv", "am%d" % ab]),
                         (gv[:, t, h * 128:(h + 1) * 128], am[:, ab, hh * 4 + 2 + j, :], ["gv", "am%d" % ab])]
                if t > 0:
                    parts.append((S16[rows, t, 0, j, :], qfT[rows, j, tok], ["S16_%d_0" % t, "qfT"]))
                if t < 15:
                    parts.append((S16[rows, t, 1, j, :], qbT[rows, j, tok], ["S16_%d_1" % t, "qbT"]))
                for pi, (l, r, rd) in enumerate(parts):
                    mm(outp, l, r, pi == 0, pi == len(parts) - 1, rd, [PS(bo)],
                       signal=(h == 3 and pi == len(parts) - 1))
            po4 = psb[bo][:, :].rearrange("p (h c) -> p h c", h=4)
            act(sq3[n % 3], po4, AF.Square, [PS(bo)], ["sq%d" % (n % 3)])
            cp("act", osb2[n % 2], po4, [PS(bo)], ["o_sb%d" % (n % 2)])

        def g4_fin(n, t):
            tok = slice(t * 128, (t + 1) * 128)
            sq, o_sb = sq3[n % 3], osb2[n % 2]
            bs = nb()
            mm(psb[bs][:, :], ones, sq.rearrange("p h c -> p (h c)"), True, True, ["sq%d" % (n % 3), "c16"], [PS(bs)])
            ps4 = psb[bs][:, :].rearrange("p (h c) -> p h c", h=4)
            act(ps4, ps4, AF.Ln, [PS(bs), "epst"], [PS(bs)], bias=epst[:, 1:2], scale=1.0 / 128.0)
            act(ps4, ps4, AF.Exp, [PS(bs)], [PS(bs)], scale=-0.5)
            tt("dve", o_sb, ps4, o_sb, ALU.mult, ["o_sb%d" % (n % 2), PS(bs)], ["o_sb%d" % (n % 2)])
            tt("dve", y_glaT[:, :, tok], o_sb, goutT[:, :, tok], ALU.mult, ["o_sb%d" % (n % 2), "goutT"], ["y_glaT"])

        s_gv = next_panel()
        s_go = next_panel(hold=1)
        go_slices = []
        for c in range(4):
            def ev_go(tg, b, c=c):
                act(gtmp, psb[b][:, :], AF.Silu, [PS(b)], ["gtmp"])
                ts("dve", goutT[:, c, tg * 512:(tg + 1) * 512], gtmp, gT[:, c:c + 1], None, ALU.mult, None,
                   ["gtmp", "gT"], ["goutT"])
            for tgs in ((0, 1), (2, 3)):
                go_slices.append(lambda c=c, ev_go=ev_go, tgs=tgs: proj_fm_part(s_go, c * 128, 128, ev_go, tgs))

        def gv_tile(t):
            def ev_gv(b):
                cp("dve" if t % 2 else "act", gv[:, t, :], psb[b][:, :], [PS(b)], ["gv"])
            proj_tm(s_gv, 0, 512, lambda k: xT[:, k, 256 + t * 128:256 + (t + 1) * 128], ev_gv)

        for i in range(8):
            gv_tile(i)
            gv_tile(15 - i)
            go_slices[i]()
            scan_step(i)
        for nm in ("sq0", "sq1", "sq2", "o_sb0", "o_sb1"):
            P.alias(nm, ["gtmp"])
        T = []
        for i in range(7, 15):
            T += [14 - i, i + 1]
        for n in range(16 + 2):
            if n % 2 == 0 and 8 + n // 2 <= 14:
                scan_step(8 + n // 2)
            if n < 16:
                g4_A(n, T[n])
            if 0 <= n - 1 < 16:
                g4_o(n - 1, T[n - 1])
            if 0 <= n - 2 < 16:
                g4_fin(n - 2, T[n - 2])

        if STOP == "G4":
            P.emit()
            return nc
        S16N_ = ["S16_%d_%d" % (t, d) for t in range(16) for d in range(2)]
        bias_st = av(64, [12, 2, 128], F32)
        mask_st = av(76, [3, 2, 128], F32)
        P.alias("bias_st", S16N_)
        P.alias("mask_st", S16N_)
        P.dma("sp", bias_st, bias_d, writes=["bias_st"], chan="bias")
        P.dma("sp", mask_st, mask_d, writes=["mask_st"], chan="mask")
        act(bias_st, bias_st, AF.Exp, ["bias_st"], ["bias_st"])
        for g in range(3):
            tt("dve", expb[:, 4 * g:4 * g + 4, :, :], bias_st[:, 4 * g:4 * g + 4, :, :],
               mask_st[:, g:g + 1, :, :].to_broadcast([128, 4, 2, 128]), ALU.mult,
               ["bias_st", "mask_st"], ["expb"])
        acc = av(64, [2, 2, S], F32)
        qT2 = [av(96, [2, S], BF16), av(140, [2, S], BF16)]
        kT2 = [av(104, [2, 2560], BF16), av(148, [2, 2560], BF16)]
        Vt2 = [av(114, [20, 256], BF16), av(158, [20, 256], BF16)]
        et_raw = av(124, [4, 8, 128], BF16)
        et = av(132, [4, 8, 128], BF16)
        ETN = ["et%d" % i for i in range(4)] + ["et_raw%d" % i for i in range(4)]
        QKV = ["qT0", "qT1", "kT0", "kT1", "kTpad0", "kTpad1", "Vt0", "Vt1"]
        P.alias("acc", S16N + ["S32_0_0", "S32_0_1", "S32_1_0", "S32_1_1", "am0", "am1", "qfT", "kfT", "bias_st", "mask_st"])
        P.alias("qT0", ["kfT", "qbT"])
        P.alias("kT0", ["qbT", "kbT", "kdec"])
        P.alias("kTpad0", ["qbT", "kbT", "kdec"])
        P.alias("Vt0", ["kbT", "kdec"])
        for nm in ("qT1", "kT1", "kTpad1", "Vt1"):
            P.alias(nm, ["gv", "goutT"])
        for nm in ETN:
            P.alias(nm, ["kdec", "gv"])
        y_attnT = av(56, [2, S], BF16)
        for q4 in range(4):
            P.alias("y_attnT%d" % q4, ["sq0", "sq1", "sq2", "o_sb0", "o_sb1", "gtmp", "lrT", "lrT1"])
        it = [0]

        def group_work(g, r):
            st_ = g % 2
            qT, kT, Vt = qT2[st_], kT2[st_], Vt2[st_]
            qn, kn, kpn, vn = "qT%d" % st_, "kT%d" % st_, "kTpad%d" % st_, "Vt%d" % st_
            L = S // r
            pad = 64 if g < 2 else 0
            Lp = L + 2 * pad
            nq = L // 128
            nk = 2 if g < 2 else 1
            ntile = {0: 17, 1: 5, 2: 1}[g]
            kTv = kT[:, :, 0:r * Lp].rearrange("p c (r m) -> p c r m", r=r)
            qTv = qT[:, :, :].rearrange("p c (r m) -> p c r m", r=r)
            slots = {}
            slices = []

            def sl_first():
                slots["qk"] = next_panel()
                if pad:
                    memset("dve", kTv[:, :, :, 0:pad], 0.0, [kpn])
                    memset("dve", kTv[:, :, :, pad + L:Lp], 0.0, [kpn])
            slices.append(sl_first)
            for c in range(2):
                def evq(tg, b, c=c):
                    src = psb[b][:, :].rearrange("p (m r) -> p r m", r=r)
                    dst = qTv[:, c, :, tg * 512 // r:(tg + 1) * 512 // r]
                    cp("act" if tg % 2 else "dve", dst, src, [PS(b)], [qn])

                def evk(tg, b, c=c):
                    src = psb[b][:, :].rearrange("p (m r) -> p r m", r=r)
                    dst = kTv[:, c, :, pad + tg * 512 // r:pad + (tg + 1) * 512 // r]
                    cp("dve" if tg % 2 else "act", dst, src, [PS(b)], [kn])
                for tgs in ((0, 1), (2, 3)):
                    slices.append(lambda c=c, evq=evq, tgs=tgs: proj_fm_part(slots["qk"], c * 128, 128, evq, tgs))
                    slices.append(lambda c=c, evk=evk, tgs=tgs: proj_fm_part(slots["qk"], 256 + c * 128, 128, evk, tgs))

            def sl_v0():
                slots["v"] = next_panel()
            slices.append(sl_v0)
            vts = [(cls, i) for cls in range(r) for i in range(ntile)]

            def v_tiles(lst):
                for (cls, i) in lst:
                    start = 256 + r * (128 * i - pad) + cls
                    vi = cls * ntile + i

                    def evv(b, vi=vi):
                        cp("act" if vi % 2 else "dve", Vt[:, vi, :], psb[b][:, 0:256], [PS(b)], [vn])
                    proj_tm(slots["v"], 0, 256,
                            lambda k, start=start: xT[:, k, start:start + 127 * r + 1:r], evv)
            for p0 in range(0, len(vts), 2):
                slices.append(lambda lst=vts[p0:p0 + 2]: v_tiles(lst))

            pairs = []

            U = 2 if g < 2 else 4

            def do_S(c, pair, eb):
                bS = [nb(), nb()]
                for hh in range(2):
                    rows = slice(hh * 64, (hh + 1) * 64)
                    spv = psb[bS[hh]][:, 0:U * nk * 128].rearrange("p (u k q) -> p u k q", u=U, k=nk)
                    for ui, (cls, qi) in enumerate(pair):
                        for kt in range(nk):
                            mm(spv[:, ui, kt, :], kTv[rows, c, cls, 128 * (qi + kt):128 * (qi + kt) + 128],
                               qTv[rows, c, cls, 128 * qi:128 * qi + 128], True, True, [kn, kpn, qn], [PS(bS[hh])],
                               signal=(ui == U - 1 and kt == nk - 1))
                for hh in range(2):
                    spv = psb[bS[hh]][:, 0:U * nk * 128].rearrange("p (u k q) -> p u k q", u=U, k=nk)
                    er = et_raw[:, eb, hh * 4:(hh + 1) * 4, :].rearrange("p (u k) q -> p u k q", u=U)
                    ee = et[:, eb, hh * 4:(hh + 1) * 4, :].rearrange("p (u k) q -> p u k q", u=U)
                    act(er, spv, AF.Exp, [PS(bS[hh])], ["et_raw%d" % eb], scale=0.125)
                    tt("dve", ee, er,
                       expb[:, 4 * g + 2 * c + hh, 0:nk, :].unsqueeze(1).to_broadcast([128, U, nk, 128]), ALU.mult,
                       ["et_raw%d" % eb, "expb"], ["et%d" % eb])

            def do_PV(c, pair, eb):
                accv = acc[:, c, :, :].rearrange("p n (m r) -> p n m r", r=r)
                bos = {}
                for ui, (cls, qi) in enumerate(pair):
                    if ui % 2 == 0:
                        bos[ui // 2] = nb()
                    bo = bos[ui // 2]
                    pv = psb[bo][:, (ui % 2) * 256:(ui % 2) * 256 + 256].rearrange("p (n q) -> p n q", n=2)
                    for hh in range(2):
                        rows = slice(hh * 64, (hh + 1) * 64)
                        hcol = (2 * c + hh) * 64
                        for n in range(2):
                            for kt in range(nk):
                                vi = cls * ntile + qi + kt
                                if n == 0:
                                    l = Vt[:, vi, hcol:hcol + 64]
                                elif g < 2 and qi + kt == 0:
                                    l = onesF
                                elif g < 2 and qi + kt == ntile - 1:
                                    l = onesL
                                else:
                                    l = ones[:, 0:64]
                                mm(pv[rows, n, :], l, et[:, eb, hh * 4 + ui * nk + kt, :], kt == 0, kt == nk - 1,
                                   [vn, "c16", "et%d" % eb], [PS(bo)],
                                   signal=(hh == 1 and n == 1 and kt == nk - 1))
                for ui, (cls, qi) in enumerate(pair):
                    bo = bos[ui // 2]
                    pv = psb[bo][:, (ui % 2) * 256:(ui % 2) * 256 + 256].rearrange("p (n q) -> p n q", n=2)
                    dst = accv[:, :, 128 * qi:128 * qi + 128, cls]
                    if g == 0:
                        cp("act", dst, pv, [PS(bo)], ["acc"])
                    else:
                        tt("dve", dst, pv, dst, ALU.add, [PS(bo), "acc"], ["acc"])

            units = [(cls, qi) for cls in range(r) for qi in range(nq)]
            for c in range(2):
                for up in range(0, len(units), U):
                    pr = units[up:up + U]
                    pairs.append((lambda eb, c=c, pr=pr: do_S(c, pr, eb), lambda eb, c=c, pr=pr: do_PV(c, pr, eb)))
            return slices, pairs

        work = [group_work(g, r) for g, r in enumerate((1, 4, 16))]
        for f in work[0][0]:
            f()
        wbr = av(144, [6, D], BF16)
        vec_ln1g = av(160, [D], F32)
        vec_ln1b = av(164, [D], F32)
        vec_ln2g = av(168, [D], F32)
        for g in range(3):
            pairs = work[g][1]
            nxt = work[g + 1][0] if g + 1 < 3 else []
            if g == 2:
                for nm in ("wbr", "vecs"):
                    P.alias(nm, ["qT1", "kT1", "kTpad1", "Vt1", "gv", "goutT"])
                P.dma("pool", wbr[:, 0:2, :], wab_d.rearrange("(k p) c -> p k c", p=128), writes=["wbr"], chan="wbr")
                P.dma("pool", wbr[:, 2:6, :], wgb_d.rearrange("(k p) c -> p k c", p=128), writes=["wbr"], chan="wbr")
                P.dma("sp", vec_ln1g, vec_d[:, 0, :], writes=["vecs"], chan="vecs")
                P.dma("sp", vec_ln1b, vec_d[:, 1, :], writes=["vecs"], chan="vecs")
                P.dma("sp", vec_ln2g, vec_d[:, 2, :], writes=["vecs"], chan="vecs")
            done = 0
            ebs = [(it[0] + i) % 4 for i in range(len(pairs))]
            it[0] += len(pairs)
            skew = 1 if nxt else 2
            for pi in range(min(skew, len(pairs))):
                pairs[pi][0](ebs[pi])
            for pi in range(len(pairs)):
                if pi + skew < len(pairs):
                    pairs[pi + skew][0](ebs[pi + skew])
                want = (pi + 1) * len(nxt) // len(pairs)
                while done < want:
                    nxt[done]()
                    done += 1
                pairs[pi][1](ebs[pi])

        if STOP == "A0":
            P.emit()
            return nc
        rden = av(96, [2, S], F32)
        for q4 in range(4):
            P.alias("rden%d" % q4, ["kfT", "qbT", "kbT", "qT0", "kT0", "kTpad0"])
        for q4 in range(4):
            qs = slice(q4 * 512, (q4 + 1) * 512)
            act(rden[:, :, qs], acc[:, :, 1, qs], AF.Ln, ["acc"], ["rden%d" % q4])
            act(rden[:, :, qs], rden[:, :, qs], AF.Exp, ["rden%d" % q4], ["rden%d" % q4], scale=-1.0)
            tt("dve", y_attnT[:, :, qs], acc[:, :, 0, qs], rden[:, :, qs], ALU.mult, ["acc", "rden%d" % q4], ["y_attnT%d" % q4])

        if debug:
            dtmp = av(112, [4, S], F32)
            P.alias("dtmp", ["gv", "kdec", "rden0", "rden1", "rden2", "rden3"] + ETN + QKV)
            cp("dve", dtmp[:, 0:2, :], y_attnT, ["y_attnT0", "y_attnT1", "y_attnT2", "y_attnT3"], ["dtmp"])
            P.dma("sp", dbg_d["d_yattn"], dtmp[:, 0:2, :], reads=["dtmp"], chan="dbg0")
            cp("dve", dtmp, y_glaT, ["y_glaT"], ["dtmp"])
            P.dma("sp", dbg_d["d_ygla"], dtmp, reads=["dtmp"], chan="dbg1")

        if STOP == "A":
            P.emit()
            return nc
        mergedT = av(64, [8, S], BF16)
        mtmp = av(128, [2, 2, 512], F32)
        sgt = av(136, [2, 2, 512], F32)
        for q4 in range(4):
            P.alias("mergedT%d" % q4, ["acc"])
        for nm in ("mtmp0", "mtmp1", "sgt0", "sgt1"):
            P.alias(nm, ["dtmp", "gv", "kdec"] + QKV + ETN)
        wout = av(96, [8, D], BF16)
        X1Bv = [av(112, [4, D], F32), av(144, [4, D], F32)]
        x1T = av(128, [2, 8, 512], BF16)
        vec_ln2b = av(58, [D], F32)
        hT = av(0, [32, 512], BF16)
        b2v = av(32, [D], F32)
        x1h = av(36, [2, D], BF16)
        rtmp4 = av(40, [2, 512], BF16)
        rtmp4b = av(62, [2, 512], BF16)
        rtmp = [rtmp4[:, 0, :], rtmp4[:, 1, :], rtmp4b[:, 0, :], rtmp4b[:, 1, :]]
        u2 = av(42, [4, D], F32)
        vecs = {0: (vec_ln1g, vec_ln1b), 2: (vec_ln2g, vec_ln2b)}
        P.alias("wout", QKV + ["rden0", "rden1", "rden2", "rden3"])
        X1B = ["x1b%d_%d" % (bf, i) for bf in range(2) for i in range(4)]
        for nm in X1B[0:4]:
            P.alias(nm, ["kdec", "dtmp"] + ETN + QKV)
        P.dma("pool", wout, wout_d.rearrange("(k p) c -> p k c", p=128), writes=["wout"], chan="wout")
        x_tiles = x_d.rearrange("(t p) d -> t p d", p=128)
        out_tiles = out_d.rearrange("(t p) d -> t p d", p=128)

        def layernorm(u_ap, unm, slot, gi, out_ap, onm):
            vg, vb = vecs[gi]
            for hf in range(2):
                P.op("dve", lambda e, hf=hf: e.bn_stats(stats[:, slot, hf, :], u_ap[:, hf * 512:(hf + 1) * 512]),
                     [unm], ["stats%d" % slot])
            P.op("dve", lambda e: e.bn_aggr(mv[:, slot, 0:2], stats[:, slot, :, :]), ["stats%d" % slot], ["mv%d" % slot])
            act(mv[:, slot, 2:3], mv[:, slot, 1:2], AF.Sqrt, ["mv%d" % slot, "epst"], ["mv%d" % slot], bias=epst[:, 0:1])
            P.op("dve", lambda e: e.reciprocal(mv[:, slot, 2:3], mv[:, slot, 2:3]), ["mv%d" % slot], ["mv%d" % slot])
            stt("dve", mv[:, slot, 3:4], mv[:, slot, 0:1], -1.0, mv[:, slot, 2:3], ALU.mult, ALU.mult,
                ["mv%d" % slot], ["mv%d" % slot])
            act(out_ap, u_ap, AF.Identity, [unm, "mv%d" % slot], [onm], bias=mv[:, slot, 3:4], scale=mv[:, slot, 2:3])
            tt("dve", out_ap, out_ap, vg, ALU.mult, [onm, "vecs", "vecs2"], [onm])
            tt("dve", out_ap, out_ap, vb, ALU.add, [onm, "vecs", "vecs2"], [onm])

        def F1a(tg, tiles, cast=True):
            bf = tg % 2
            for tt_ in tiles:
                t = tg * 4 + tt_
                xs = t % 2
                xnm = "x1b%d_%d" % (bf, tt_)
                xt = X1Bv[bf][:, tt_, :]
                P.dma("sp", xt, x_tiles[t], writes=[xnm], chan=xnm)
                bks = [nb(), nb()]
                for hf in range(2):
                    for j in range(8):
                        mm(psb[bks[hf]][:, :], mergedT[:, j, t * 128:(t + 1) * 128], wout[:, j, hf * 512:(hf + 1) * 512],
                           j == 0, j == 7, ["mergedT%d" % tg, "wout"], [PS(bks[hf])])
                for hf in range(2):
                    stt("dve", xt[:, hf * 512:(hf + 1) * 512], xt[:, hf * 512:(hf + 1) * 512], ALPHA,
                        psb[bks[hf]][:, :], ALU.mult, ALU.add, [xnm, PS(bks[hf])], [xnm])
                layernorm(xt, xnm, xs, 0, xt, xnm)
            if cast:
                F1a_cast(tg, tiles)

        def F1a_cast(tg, tiles):
            bf = tg % 2
            for tt_ in tiles:
                xs = (tg * 4 + tt_) % 2
                xnm = "x1b%d_%d" % (bf, tt_)
                xt = X1Bv[bf][:, tt_, :]
                cp("act", x1h[:, xs, :], xt, [xnm], ["x1h%d" % xs])
                stt("dve", xt, xt, ALPHA, b2v, ALU.mult, ALU.add, [xnm, "b2v"], [xnm])

        def F1b(tg, tiles):
            bf = tg % 2
            for tt_ in tiles:
                xs = (tg * 4 + tt_) % 2
                bt = nb()
                ptb = psb[bt][:, :].bitcast(BF16)
                for k in range(8):
                    P.op("pe", lambda e, k=k, xs=xs, ptb=ptb: e.transpose(ptb[:, k * 128:(k + 1) * 128],
                                                                         x1h[:, xs, k * 128:(k + 1) * 128], ident),
                         ["x1h%d" % xs, "c16"], [PS(bt)], signal=(k == 7))
                cp("act", x1T[:, bf, :, tt_ * 128:(tt_ + 1) * 128], ptb.rearrange("p (k c) -> p k c", k=8), [PS(bt)],
                   ["x1T%d" % bf])

        def F2(tg):
            bf = tg % 2
            for u in range(8):
                s1 = next_panel()
                for f4 in range(4):
                    fc = u * 4 + f4
                    b = nb()
                    for k in range(8):
                        mm(psb[b][:, :], wst[s1][:, k, f4 * 128:(f4 + 1) * 128], x1T[:, bf, k, :], k == 0, k == 7,
                           ["W%d" % s1, "x1T%d" % bf], [PS(b)])
                    rb = fc % 4
                    act(rtmp[rb], psb[b][:, :], AF.Relu, [PS(b), "b1T"], ["rtmp%d" % rb], bias=b1T[:, fc:fc + 1])
                    tt("dve", hT[:, fc, :], rtmp[rb], rtmp[rb], ALU.mult, ["rtmp%d" % rb], ["hT"])
                if tg > 0 and 2 <= u <= 5:
                    LN2(tg - 1, [u - 2])

        def F3h(tg, hf, before_last=None, tile_sets=((0, 1, 2, 3),), after_set=None):
            bf = tg % 2
            for si, tset in enumerate(tile_sets):
                banks = {tt_: nb() for tt_ in tset}
                for u in range(4):
                    if u == 3 and before_last is not None and si == len(tile_sets) - 1:
                        before_last()
                    s2 = next_panel()
                    for tt_ in tset:
                        for f8 in range(8):
                            fc = u * 8 + f8
                            mm(psb[banks[tt_]][:, :], hT[:, fc, tt_ * 128:(tt_ + 1) * 128], wst[s2][:, f8, :],
                               fc == 0, fc == 31, ["hT", "W%d" % s2], [PS(banks[tt_])])
                for tt_ in tset:
                    tt("dve", u2[:, tt_, hf * 512:(hf + 1) * 512], psb[banks[tt_]][:, :],
                       X1Bv[bf][:, tt_, hf * 512:(hf + 1) * 512], ALU.add,
                       [PS(banks[tt_]), "x1b%d_%d" % (bf, tt_)], ["u2_%d" % tt_])
                if after_set is not None:
                    after_set(tset)

        def LN2(tg, tiles):
            for tt_ in tiles:
                t = tg * 4 + tt_
                layernorm(u2[:, tt_, :], "u2_%d" % tt_, 2 + tt_ % 2, 2, u2[:, tt_, :], "u2_%d" % tt_)
                P.dma("sp", out_tiles[t], u2[:, tt_, :], reads=["u2_%d" % tt_], chan="ot%d" % tt_)


        mit = [0]
        for u in range(2):
            s_ga = next_panel()
            s_gg = next_panel(hold=1)
            order = [(jj, tg) for jj in range(4) for tg in range(4)] if u == 0 else \
                    [(jj, tg) for tg in range(4) for jj in range(4)]
            for (jj, tg) in order:
                if True:
                    j = u * 4 + jj
                    mb = mit[0] % 2
                    mit[0] += 1
                    tks = slice(tg * 512, (tg + 1) * 512)
                    b_ga, b_gg, b_za, b_zg = nb(), nb(), nb(), nb()
                    for (bq, sl) in ((b_ga, s_ga), (b_gg, s_gg)):
                        for k in range(8):
                            mm(psb[bq][:, :], wst[sl][:, k, jj * 128:(jj + 1) * 128],
                               xT[:, k, 256 + tg * 512:256 + (tg + 1) * 512], k == 0, k == 7,
                               XTk(k) + ["W%d" % sl], [PS(bq)])
                    for k in range(2):
                        mm(psb[b_za][:, :], wbr[:, k, j * 128:(j + 1) * 128], y_attnT[:, k, tks], k == 0, k == 1,
                           ["wbr", "y_attnT%d" % tg], [PS(b_za)])
                    for k in range(4):
                        mm(psb[b_zg][:, :], wbr[:, 2 + k, j * 128:(j + 1) * 128], y_glaT[:, k, tks], k == 0, k == 3,
                           ["wbr", "y_glaT"], [PS(b_zg)])
                    act(sgt[:, mb, 0, :], psb[b_ga][:, :], AF.Sigmoid, [PS(b_ga)], ["sgt%d" % mb])
                    act(sgt[:, mb, 1, :], psb[b_gg][:, :], AF.Sigmoid, [PS(b_gg)], ["sgt%d" % mb])
                    tt("dve", mtmp[:, mb, 0, :], psb[b_za][:, :], sgt[:, mb, 0, :], ALU.mult, [PS(b_za), "sgt%d" % mb], ["mtmp%d" % mb])
                    tt("dve", mtmp[:, mb, 1, :], psb[b_zg][:, :], sgt[:, mb, 1, :], ALU.mult, [PS(b_zg), "sgt%d" % mb], ["mtmp%d" % mb])
                    tt("dve", mergedT[:, j, tks], mtmp[:, mb, 0, :], mtmp[:, mb, 1, :], ALU.add, ["mtmp%d" % mb], ["mergedT%d" % tg])
                    if u == 1 and not debug and (tg, jj) in ((1, 0), (1, 3), (2, 2), (3, 1)):
                        F1a(0, ({(1, 0): 0, (1, 3): 1, (2, 2): 2, (3, 1): 3}[(tg, jj)],), cast=False)

        if debug:
            dtm = av(0, [8, S], F32)
            P.alias("dtm", XT + ["y_glaT"] + ["y_attnT0", "y_attnT1", "y_attnT2", "y_attnT3"])
            cp("dve", dtm, mergedT, ["mergedT0", "mergedT1", "mergedT2", "mergedT3"], ["dtm"])
            P.dma("sp", dbg_d["d_merged"], dtm, reads=["dtm"], chan="dbg2")

        if STOP == "M":
            P.emit()
            return nc
        for nm in X1B[4:8]:
            P.alias(nm, ["wbr", "gv", "goutT", "dtmp"] + QKV)
        for nm in ("x1T0", "x1T1"):
            P.alias(nm, ["mtmp0", "mtmp1", "sgt0", "sgt1", "gv", "kdec", "dtmp"] + ETN + QKV)
        for nm in ("hT", "b2v", "x1h0", "x1h1", "rtmp0", "rtmp1", "rtmp2", "rtmp3", "u2_0", "u2_1", "u2_2", "u2_3", "vecs2"):
            P.alias(nm, XT + ["y_glaT", "dtm"] + ["y_attnT0", "y_attnT1", "y_attnT2", "y_attnT3"])
        P.dma("sp", vec_ln2b, vec_d[:, 3, :], writes=["vecs2"], chan="vecs2")
        P.dma("sp", b2v, vec_d[:, 4, :], writes=["b2v"], chan="b2v")
        if debug:
            F1a(0, (0, 1, 2, 3), cast=False)
        for tl in ((0, 1), (2, 3)):
            F1a_cast(0, tl)
            F1b(0, tl)
        for tg in range(4):
            F2(tg)
            nxt = tg + 1 < 4
            if nxt:
                F1a(tg + 1, (0, 1))
            F3h(tg, 0)
            if nxt:
                F1b(tg + 1, (0, 1))
                F1a(tg + 1, (2, 3))
            if nxt:
                F3h(tg, 1, before_last=lambda tg=tg: F1b(tg + 1, (2, 3)))
            else:
                F3h(tg, 1, tile_sets=((0, 1), (2, 3)), after_set=lambda tset: LN2(3, list(tset)))

        P.emit()
    return nc


_NC_CACHE = {}


def _prep_shared(w_in, rel_bias, w_lr_fwd, b_lr_fwd, w_lr_bwd, b_lr_bwd, gla_norm_g, w_attn_branch,
                 w_gla_branch, w_out, ln1_g, ln1_b, w_ff1, b_ff1, w_ff2, b_ff2, ln2_g, ln2_b):
    f = lambda a: np.ascontiguousarray(np.asarray(a, dtype=np.float32))
    rel_bias = f(rel_bias)
    buckets, masks = _host_tables()
    bias = np.zeros((256, 12, 128), np.float32)
    for g in range(3):
        for hs in range(4):
            bias[:, 4 * g + hs, :] = rel_bias[buckets[g], 4 * g + hs]
    biasT = np.ascontiguousarray(bias.reshape(2, 128, 12, 128).transpose(1, 2, 0, 3))
    maskT = np.ascontiguousarray(np.stack(masks, 1).reshape(2, 128, 3, 128).transpose(1, 2, 0, 3))
    wlr = np.zeros((33, 512), np.float32)
    wlr[0:16, 0:256] = f(w_lr_fwd)[0]
    wlr[16:32, 256:512] = f(w_lr_bwd)[0]
    wlr[32, 0:256] = f(b_lr_fwd)[0]
    wlr[32, 256:512] = f(b_lr_bwd)[0]
    c32, c16 = _consts()
    vecs = np.stack([f(ln1_g)[0], f(ln1_b)[0], f(ln2_g)[0], f(ln2_b)[0], f(b_ff2)[0]], 0)
    vecs = np.ascontiguousarray(np.broadcast_to(vecs[None], (128, 5, D)))
    return {
        "w_in": f(w_in)[0], "wlr": wlr, "c32": c32, "c16": c16, "biasT": biasT, "maskT": maskT,
        "w_ab": f(w_attn_branch)[0], "w_gb": f(w_gla_branch)[0], "w_out": f(w_out)[0],
        "w_ff1": f(w_ff1)[0], "b1T": np.ascontiguousarray(f(b_ff1)[0].reshape(32, 128).T),
        "w_ff2": f(w_ff2)[0], "gT": np.ascontiguousarray(f(gla_norm_g)[0].reshape(4, 128).T),
        "vecs": vecs,
    }


def kernel(x, **params):
    debug = bool(os.environ.get("MK_DEBUG"))
    x = np.asarray(x, dtype=np.float32)
    B = x.shape[0]
    shared = _prep_shared(**params)
    key = debug
    if key not in _NC_CACHE:
        _NC_CACHE[key] = build_nc(debug)
    nc = _NC_CACHE[key]
    in_maps = []
    for b in range(B):
        m = dict(shared)
        m["x"] = np.ascontiguousarray(x[b])
        m["xT"] = np.ascontiguousarray(x[b].T)
        in_maps.append(m)
    res = run_bass_kernel_spmd(nc, in_maps, core_ids=list(range(B)))
    out = np.stack([np.asarray(r["out"], dtype=np.float32) for r in res.results], 0)
    if debug:
        kernel.dbg = [{k: np.asarray(v) for k, v in r.items() if k.startswith("d_")} for r in res.results]
    return out
```
